# Optimizing a Trainium2 kernel written in Bass

```python
import math
import jax
import jax.numpy as jnp
from jax import lax
import numpy as np

D_MODEL = 4096
BATCH = 4
SEQ = 4096
DEPTH = 1

GRID_W = 64
CTX_LEN = 256
N_MOD = 6
D_MIX = D_MODEL
S5_WIDTH = D_MIX // 4
S5_GROUP = 16
S5_GROUPS = S5_WIDTH // S5_GROUP
S5_STATE = 64
MLA_NOPE = 128
MLA_ROPE = 64
MLA_V = 128
MLA_HEADS = (D_MIX - S5_WIDTH) // MLA_V
MLA_Q_RANK = D_MODEL // 4
MLA_KV_RANK = 512
MLA_SCALE = (MLA_NOPE + MLA_ROPE) ** -0.5
ROPE_FREQS = MLA_ROPE // 4
ROPE_BASE = 10000.0
IN_SPLITS = (S5_WIDTH, S5_WIDTH + MLA_Q_RANK, S5_WIDTH + MLA_Q_RANK + MLA_KV_RANK)
IN_COLS = IN_SPLITS[-1] + MLA_ROPE
D_FF = ((8 * D_MODEL // 3 + 255) // 256) * 256
CONV_WIDTH = 3
Q_BLOCK = 128
EPS = 1e-6
F32 = jnp.float32

kernel_name = 'hybrid_s5_mla_convffn_dit_layer'


def _rmsnorm(x, g):
    xf = x.astype(F32)
    y = xf * lax.rsqrt(jnp.mean(xf * xf, axis=-1, keepdims=True) + EPS)
    return (y * g.astype(F32)).astype(x.dtype)


def _modulate(h, shift, scale):
    return h * (1.0 + scale) + shift


def _axial_rope_angles(rows):
    row = jnp.repeat(jnp.arange(rows, dtype=F32), GRID_W)
    col = jnp.tile(jnp.arange(GRID_W, dtype=F32), rows)
    inv = ROPE_BASE ** (-jnp.arange(ROPE_FREQS, dtype=F32) / ROPE_FREQS)
    ang = jnp.concatenate([row[:, None] * inv, col[:, None] * inv], axis=-1)
    return jnp.cos(ang), jnp.sin(ang)


def _rope(x, cos, sin):
    xf = x.astype(F32).reshape(x.shape[:-1] + (MLA_ROPE // 2, 2))
    x1, x2 = xf[..., 0], xf[..., 1]
    out = jnp.stack([x1 * cos - x2 * sin, x1 * sin + x2 * cos], axis=-1)
    return out.reshape(x.shape).astype(x.dtype)


def _cmul(ar, ai, br, bi):
    return ar * br - ai * bi, ar * bi + ai * br


def _s5_discretise(lam_re, lam_im, log_step, b_re, b_im):
    lr, li = lam_re.astype(F32), lam_im.astype(F32)
    dt = jnp.exp(log_step.astype(F32))[:, None]
    mag = jnp.exp(lr * dt)
    ar, ai = mag * jnp.cos(li * dt), mag * jnp.sin(li * dt)
    den = lr * lr + li * li
    nr, ni = ar - 1.0, ai
    fr = (nr * lr + ni * li) / den
    fi = (ni * lr - nr * li) / den
    br, bi = b_re.astype(F32), b_im.astype(F32)
    bbar_r = fr[..., None] * br - fi[..., None] * bi
    bbar_i = fr[..., None] * bi + fi[..., None] * br
    return ar, ai, bbar_r, bbar_i


def _s5_drive(u, bbar_r, bbar_i):
    ug = u.astype(F32).reshape(u.shape[:2] + (S5_GROUPS, S5_GROUP))
    return jnp.einsum('blgc,gpc->blgp', ug, bbar_r), jnp.einsum('blgc,gpc->blgp', ug, bbar_i)


def _s5_scan(a_r, a_i, bu_r, bu_i, reverse, h0=None):
    if h0 is not None:
        first = -1 if reverse else 0
        hr, hi = _cmul(a_r, a_i, h0[0], h0[1])
        bu_r = bu_r.at[:, first].add(hr)
        bu_i = bu_i.at[:, first].add(hi)
    n = bu_r.shape[1]
    a_r = jnp.broadcast_to(a_r, (1, n) + a_r.shape)
    a_i = jnp.broadcast_to(a_i, (1, n) + a_i.shape)

    def combine(e1, e2):
        a1r, a1i, b1r, b1i = e1
        a2r, a2i, b2r, b2i = e2
        ar, ai = _cmul(a2r, a2i, a1r, a1i)
        br, bi = _cmul(a2r, a2i, b1r, b1i)
        return ar, ai, br + b2r, bi + b2i

    _, _, h_r, h_i = lax.associative_scan(combine, (a_r, a_i, bu_r, bu_i), reverse=reverse, axis=1)
    return h_r, h_i


def _s5_readout(h_r, h_i, c_re, c_im):
    y = jnp.einsum('blgp,gcp->blgc', h_r, c_re.astype(F32)) - jnp.einsum('blgp,gcp->blgc', h_i, c_im.astype(F32))
    return y.reshape(y.shape[:2] + (S5_WIDTH,))


def _s5_glu(y, w_glu):
    y = jax.nn.gelu(y)
    return y * jax.nn.sigmoid(y @ w_glu)


def _s5_mixer(u_x, u_c, lam_re, lam_im, log_step, b_re, b_im, c_re, c_im, d, w_glu, with_ctx_out):
    df = d.astype(F32)
    y_x = df * u_x.astype(F32)
    y_c = df * u_c.astype(F32) if with_ctx_out else None
    for dirn, reverse in enumerate((False, True)):
        a_r, a_i, bb_r, bb_i = _s5_discretise(lam_re[dirn], lam_im[dirn], log_step[dirn], b_re[dirn], b_im[dirn])
        cu_r, cu_i = _s5_drive(u_c, bb_r, bb_i)
        hc_r, hc_i = _s5_scan(a_r, a_i, cu_r, cu_i, reverse)
        end = 0 if reverse else -1
        xu_r, xu_i = _s5_drive(u_x, bb_r, bb_i)
        hx_r, hx_i = _s5_scan(a_r, a_i, xu_r, xu_i, reverse, (hc_r[:, end], hc_i[:, end]))
        y_x = y_x + _s5_readout(hx_r, hx_i, c_re[dirn], c_im[dirn])
        if with_ctx_out:
            y_c = y_c + _s5_readout(hc_r, hc_i, c_re[dirn], c_im[dirn])
    out_x = _s5_glu(y_x.astype(u_x.dtype), w_glu)
    out_c = _s5_glu(y_c.astype(u_c.dtype), w_glu) if with_ctx_out else None
    return out_x, out_c


def _mla_queries(q_c, g_q, w_uq):
    q = (_rmsnorm(q_c, g_q) @ w_uq).reshape(q_c.shape[:2] + (MLA_HEADS, MLA_NOPE + MLA_ROPE))
    return q[..., :MLA_NOPE], q[..., MLA_NOPE:]


def _mla_keys_values(kv_c, g_kv, w_ukv):
    kv = (_rmsnorm(kv_c, g_kv) @ w_ukv).reshape(kv_c.shape[:2] + (MLA_HEADS, MLA_NOPE + MLA_V))
    return kv[..., :MLA_NOPE], kv[..., MLA_NOPE:]


def _mla_attend(q_nope, q_rope, k_nope, k_rope, v):
    s = jnp.einsum('bqhd,bkhd->bhqk', q_nope, k_nope) + jnp.einsum('bqhr,bkr->bhqk', q_rope, k_rope)
    p = jax.nn.softmax(s.astype(F32) * MLA_SCALE, axis=-1).astype(v.dtype)
    return jnp.einsum('bhqk,bkhd->bqhd', p, v)


def _mla_block_attention(q_nope, q_rope, k_nope, k_rope, v):
    b, n = q_nope.shape[:2]
    nb = n // Q_BLOCK

    def to_blocks(t):
        return t.reshape((b, nb, Q_BLOCK) + t.shape[2:]).swapaxes(0, 1)

    out = lax.map(lambda qs: _mla_attend(qs[0], qs[1], k_nope, k_rope, v), (to_blocks(q_nope), to_blocks(q_rope)))
    return out.swapaxes(0, 1).reshape(b, n, MLA_HEADS * MLA_V)


def _dwconv3(g, w, b):
    gp = jnp.pad(g, ((0, 0), (1, 1), (0, 0)))
    return gp[:, :-2] * w[0] + gp[:, 1:-1] * w[1] + gp[:, 2:] * w[2] + b


def _conv_ffn(h, w_in, conv_w, conv_b, w_down):
    gate, up = jnp.split(h @ w_in, 2, axis=-1)
    return (jax.nn.silu(_dwconv3(gate, conv_w, conv_b)) * up) @ w_down


def setup_inputs(seed: int = 0) -> dict:
    key = jax.random.key(seed)
    ks = jax.random.split(key, 32)
    d = D_MODEL

    def nrm(k, shape, scale):
        return scale * jax.random.normal(k, shape, F32)

    def gain(k, shape):
        return 1.0 + nrm(k, shape, 0.02)

    lam_im_base = jnp.pi * jnp.arange(S5_STATE, dtype=F32)
    return {
        'x': nrm(ks[0], (BATCH, SEQ, d), 1.0),
        'c': nrm(ks[1], (BATCH, d), 1.0),
        'ctx': nrm(ks[2], (BATCH, CTX_LEN, d), 1.0),
        'c_ctx': nrm(ks[3], (d,), 1.0),
        'w_ada': nrm(ks[4], (DEPTH, d, N_MOD * d), 0.5 * d ** -0.5),
        'b_ada': nrm(ks[5], (DEPTH, N_MOD * d), 0.01),
        'g_pre_mix': gain(ks[6], (DEPTH, d)),
        'g_post_mix': gain(ks[7], (DEPTH, d)),
        'g_pre_ffn': gain(ks[8], (DEPTH, d)),
        'g_post_ffn': gain(ks[9], (DEPTH, d)),
        'w_in': nrm(ks[10], (DEPTH, d, IN_COLS), d ** -0.5),
        's5_lambda_re': -0.5 + nrm(ks[11], (DEPTH, 2, S5_GROUPS, S5_STATE), 0.01),
        's5_lambda_im': lam_im_base + nrm(ks[12], (DEPTH, 2, S5_GROUPS, S5_STATE), 0.01),
        's5_log_step': jax.random.uniform(ks[13], (DEPTH, 2, S5_GROUPS), F32, math.log(1e-3), math.log(1e-1)),
        's5_b_re': nrm(ks[14], (DEPTH, 2, S5_GROUPS, S5_STATE, S5_GROUP), (2 * S5_GROUP) ** -0.5),
        's5_b_im': nrm(ks[15], (DEPTH, 2, S5_GROUPS, S5_STATE, S5_GROUP), (2 * S5_GROUP) ** -0.5),
        's5_c_re': nrm(ks[16], (DEPTH, 2, S5_GROUPS, S5_GROUP, S5_STATE), 0.5),
        's5_c_im': nrm(ks[17], (DEPTH, 2, S5_GROUPS, S5_GROUP, S5_STATE), 0.5),
        's5_d': nrm(ks[18], (DEPTH, S5_WIDTH), 0.5),
        's5_w_glu': nrm(ks[19], (DEPTH, S5_WIDTH, S5_WIDTH), S5_WIDTH ** -0.5),
        'mla_g_q': gain(ks[20], (DEPTH, MLA_Q_RANK)),
        'mla_w_uq': nrm(ks[21], (DEPTH, MLA_Q_RANK, MLA_HEADS * (MLA_NOPE + MLA_ROPE)), MLA_Q_RANK ** -0.5),
        'mla_g_kv': gain(ks[22], (DEPTH, MLA_KV_RANK)),
        'mla_w_ukv': nrm(ks[23], (DEPTH, MLA_KV_RANK, MLA_HEADS * (MLA_NOPE + MLA_V)), MLA_KV_RANK ** -0.5),
        'w_out': nrm(ks[24], (DEPTH, D_MIX, d), D_MIX ** -0.5),
        'ffn_w_in': nrm(ks[25], (DEPTH, d, 2 * D_FF), d ** -0.5),
        'ffn_conv_w': nrm(ks[26], (DEPTH, CONV_WIDTH, D_FF), CONV_WIDTH ** -0.5),
        'ffn_conv_b': nrm(ks[27], (DEPTH, D_FF), 0.01),
        'ffn_w_down': nrm(ks[28], (DEPTH, D_FF, d), D_FF ** -0.5),
    }


def reference(x, c, ctx, c_ctx, w_ada, b_ada, g_pre_mix, g_post_mix, g_pre_ffn, g_post_ffn, w_in,
              s5_lambda_re, s5_lambda_im, s5_log_step, s5_b_re, s5_b_im, s5_c_re, s5_c_im, s5_d, s5_w_glu,
              mla_g_q, mla_w_uq, mla_g_kv, mla_w_ukv, w_out, ffn_w_in, ffn_conv_w, ffn_conv_b, ffn_w_down):
    rows = x.shape[1] // GRID_W
    cos, sin = _axial_rope_angles(rows)
    for layer in range(DEPTH):
        update_ctx = layer < DEPTH - 1
        mod_x = jax.nn.silu(c) @ w_ada[layer] + b_ada[layer]
        mod_c = jax.nn.silu(c_ctx) @ w_ada[layer] + b_ada[layer]
        sh_a, sc_a, ga_a, sh_f, sc_f, ga_f = jnp.split(mod_x[:, None, :], N_MOD, axis=-1)
        csh_a, csc_a, cga_a, csh_f, csc_f, cga_f = jnp.split(mod_c, N_MOD, axis=-1)

        px = _modulate(_rmsnorm(x, g_pre_mix[layer]), sh_a, sc_a) @ w_in[layer]
        pc = _modulate(_rmsnorm(ctx, g_pre_mix[layer]), csh_a, csc_a) @ w_in[layer]
        u_x, qc_x, kvc_x, kr_x = jnp.split(px, IN_SPLITS, axis=-1)
        u_c, qc_c, kvc_c, kr_c = jnp.split(pc, IN_SPLITS, axis=-1)

        s5_x, s5_c = _s5_mixer(u_x, u_c, s5_lambda_re[layer], s5_lambda_im[layer], s5_log_step[layer],
                               s5_b_re[layer], s5_b_im[layer], s5_c_re[layer], s5_c_im[layer],
                               s5_d[layer], s5_w_glu[layer], update_ctx)

        kn_x, v_x = _mla_keys_values(kvc_x, mla_g_kv[layer], mla_w_ukv[layer])
        kn_c, v_c = _mla_keys_values(kvc_c, mla_g_kv[layer], mla_w_ukv[layer])
        kr_x = _rope(kr_x, cos, sin)
        qn_x, qr_x = _mla_queries(qc_x, mla_g_q[layer], mla_w_uq[layer])
        qr_x = _rope(qr_x, cos[:, None, :], sin[:, None, :])
        att_x = _mla_block_attention(qn_x, qr_x,
                                     jnp.concatenate([kn_c, kn_x], axis=1),
                                     jnp.concatenate([kr_c, kr_x], axis=1),
                                     jnp.concatenate([v_c, v_x], axis=1))
        mix_x = jnp.concatenate([s5_x, att_x], axis=-1) @ w_out[layer]
        x_mid = x + ga_a * _rmsnorm(mix_x, g_post_mix[layer])

        if update_ctx:
            qn_c, qr_c = _mla_queries(qc_c, mla_g_q[layer], mla_w_uq[layer])
            att_c = _mla_attend(qn_c, qr_c, kn_c, kr_c, v_c)
            att_c = att_c.reshape(att_c.shape[:2] + (MLA_HEADS * MLA_V,))
            mix_c = jnp.concatenate([s5_c, att_c], axis=-1) @ w_out[layer]
            ctx = ctx + cga_a * _rmsnorm(mix_c, g_post_mix[layer])
            hc = _modulate(_rmsnorm(ctx, g_pre_ffn[layer]), csh_f, csc_f)
            ctx = ctx + cga_f * _rmsnorm(_conv_ffn(hc, ffn_w_in[layer], ffn_conv_w[layer], ffn_conv_b[layer], ffn_w_down[layer]), g_post_ffn[layer])

        hx = _modulate(_rmsnorm(x_mid, g_pre_ffn[layer]), sh_f, sc_f)
        f_x = _conv_ffn(hx, ffn_w_in[layer], ffn_conv_w[layer], ffn_conv_b[layer], ffn_w_down[layer])
        x = x_mid + ga_f * _rmsnorm(f_x, g_post_ffn[layer])
    return x
```

```python
import contextlib
import numpy as np
import concourse.bass as bass
import concourse.mybir as mybir
from concourse.bass_utils import run_bass_kernel_spmd

F32 = mybir.dt.float32
BF16 = mybir.dt.bfloat16
AF = mybir.ActivationFunctionType
ALU = mybir.AluOpType

D = 4096
KT = 32
SEQ = 4096
CTX = 256
NKEY = SEQ + CTX
NOWN = 2050
NT = 410
OWN_TILES = [(i * NT, NT) for i in range(5)]
REST_TILES = [(2050, 510), (2560, 512), (3072, 512), (3584, 512)]
UCOLS = CTX + SEQ + CTX
NH = 24
DFF = 11008
FT = 86
EPS = 1e-6
MLA_SCALE = 192.0 ** -0.5
NBA = 32 + 257
NBB = 544
NYB = 257
STOP_AFTER = None
DEBUG = False
P1_LIMIT = None
P1_STAGE = 0
SKIP = set()
P5_LIMIT = None


class Tok:
    __slots__ = ("sem", "val", "key")

    def __init__(self, sem, val, key):
        self.sem, self.val, self.key = sem, val, key


class Buf:
    __slots__ = ("w", "r", "dsem", "dcnt", "dkey", "last_dma", "name", "dram", "bg")

    def __init__(self, name="", dram=False):
        self.w, self.r = {}, {}
        self.dsem = None
        self.dcnt = 0
        self.last_dma = None
        self.name = name
        self.dram = dram
        self.bg = False

    @staticmethod
    def _add(d, tok):
        o = d.get(tok.key)
        if o is None or o.val < tok.val:
            d[tok.key] = tok


class Eng:
    def __init__(self, kb, h, name, is_pe=False, compute=True):
        self.kb, self.h, self.name, self.is_pe, self.compute = kb, h, name, is_pe, compute
        self.sem = kb.new_sem("e_" + name)
        self.key = "e_" + name
        self.cnt = 0
        self.waited = {}
        self.pending = False

    def wait(self, tok):
        if tok.key == self.key:
            if self.is_pe:
                return
        if self.waited.get(tok.key, 0) >= tok.val:
            return
        self.h.wait_ge(tok.sem, tok.val)
        self.waited[tok.key] = tok.val

    def wait_all(self, d):
        for t in list(d.values()):
            self.wait(t)

    def mark(self, inst, signal):
        if signal:
            self.cnt += 1
            inst.then_inc(self.sem, 1)
            self.pending = False
            return Tok(self.sem, self.cnt, self.key)
        self.pending = True
        return Tok(self.sem, self.cnt + 1, self.key)


class KB:
    def __init__(self, nc, es):
        self.nc, self.es = nc, es
        self.nsem = 0
        self.pe = Eng(self, nc.tensor, "pe", is_pe=True)
        self.dve = Eng(self, nc.vector, "dve")
        self.act = Eng(self, nc.scalar, "act")
        self.pool = Eng(self, nc.gpsimd, "pool")
        self.sp = Eng(self, nc.sync, "sp", compute=False)
        self.engs = [self.pe, self.dve, self.act, self.pool, self.sp]
        self.dbufs = []
        self.retired = []

    def new_sem(self, name):
        self.nsem += 1
        return self.es.enter_context(self.nc.semaphore(f"{name}_{self.nsem}"))

    def op(self, eng, build, reads=(), writes=(), parts=(), signal=True):
        for b in reads:
            eng.wait_all(b.w)
        for b in writes:
            eng.wait_all(b.r)
            eng.wait_all(b.w)
        for b in parts:
            eng.wait_all(b.r)
        inst = build()
        tok = eng.mark(inst, signal)
        for b in reads:
            if not b.dram:
                Buf._add(b.r, tok)
        for b in writes:
            b.w = {tok.key: tok}
            b.r = {}
        for b in parts:
            Buf._add(b.w, tok)
        return tok

    def dma(self, q, out_ap, in_ap, out_buf, in_buf, part=False, **kw):
        sb = out_buf if not out_buf.dram else (in_buf if not in_buf.dram else out_buf)
        if sb.dsem is not None and sb.dcnt >= 30000:
            if not sb.bg:
                self.retired.append(Tok(sb.dsem, sb.dcnt, sb.dkey))
            sb.dsem = self.new_sem("d")
            sb.dkey = f"d{self.nsem}"
            sb.dcnt = 0
        if sb.dsem is None:
            sb.dsem = self.new_sem("d")
            sb.dkey = f"d{self.nsem}"
            self.dbufs.append(sb)
        if sb.last_dma is not None and not sb.dram:
            q.wait(sb.last_dma)
        q.wait_all(in_buf.w)
        if not out_buf.dram:
            q.wait_all(out_buf.r)
            if not part:
                q.wait_all(out_buf.w)
        inst = q.h.dma_start(out=out_ap, in_=in_ap, **kw)
        sb.dcnt += 16
        inst.then_inc(sb.dsem, 16)
        tok = Tok(sb.dsem, sb.dcnt, sb.dkey)
        sb.last_dma = tok
        if not in_buf.dram:
            Buf._add(in_buf.r, tok)
        if out_buf.dram or part:
            Buf._add(out_buf.w, tok)
        else:
            out_buf.w = {tok.key: tok}
            out_buf.r = {}
        return tok

    def barrier(self):
        assert not self.pe.pending
        toks = [Tok(e.sem, e.cnt, e.key) for e in self.engs if e.compute and e.cnt > 0]
        toks += [Tok(b.dsem, b.dcnt, b.dkey) for b in self.dbufs if b.dcnt > 0 and not b.bg]
        toks += self.retired
        for e in self.engs:
            for t in toks:
                e.wait(t)


def _build(debug_out=None):
    nc = bass.Bass("TRN2", target_bir_lowering=False)
    es = contextlib.ExitStack()
    kb = KB(nc, es)
    pe, dve, act, pool, sp = kb.pe, kb.dve, kb.act, kb.pool, kb.sp
    V, S, T, G = nc.vector, nc.scalar, nc.tensor, nc.gpsimd

    def din(name, shape, dt=F32):
        return nc.dram_tensor(name, list(shape), dt, kind="ExternalInput").ap()

    dbg = debug_out or ()

    def dscr(name, shape, dt):
        kind = "ExternalOutput" if name in dbg else "Internal"
        return nc.dram_tensor(name, list(shape), dt, kind=kind).ap()

    IN_SHAPES = dict(xl=[SEQ, D], ctxl=[CTX, D], ccol=[128, KT, 2], wada=[192, 128, KT * 128], bada=[128, 192],
                     gvec=[128, 4, KT], win=[21, 128, KT * 128], gq=[128, 8], gkv=[128, 4],
                     wuq=[NH, 128, 8 * 256], wuk=[128, 4 * 3072], wuv=[128, 4 * 3072], wout=[KT, 128, KT * 128],
                     wglu=[8, 128, 8 * 128], wffg=[FT, 128, KT * 128], wffu=[FT, 128, KT * 128],
                     wdn=[KT, 128, FT * 128], convw=[128, FT, 4], ropeq=[64, 2, NOWN], ropek=[64, 2, NKEY],
                     ident=[128, 128], s5lam=[128, 2, 64], s5ls=[128, 64], s5b=[128, 64, 2, 16],
                     s5c=[128, 64, 2, 16], s5d=[32, 32])
    declared = {}

    class _In:
        def __getattr__(self, name):
            if name not in declared:
                declared[name] = din(name, IN_SHAPES[name])
            return declared[name]

    I = _In()
    nc._declared_inputs = declared
    out = nc.dram_tensor("out", [2048, D], F32, kind="ExternalOutput").ap()

    uT = dscr("uT", [1024, UCOLS], BF16)
    qcnTs = dscr("qcnTs", [1024, NOWN], BF16)
    KTs = dscr("KTs", [NH, 128, NKEY], BF16)
    Vs = dscr("Vs", [34, 128, 3072], BF16)
    yactT = dscr("yactT", [1024, 2056], BF16)
    s5outT = dscr("s5outT", [1024, NOWN], BF16)
    attT = dscr("attT", [3072, NOWN], BF16)
    xmidT = dscr("xmidT", [D, NOWN], F32)
    hxT = dscr("hxT", [D, NOWN], BF16)
    fTs = dscr("fTs", [5, D, NT], F32)
    B_uT, B_qcn, B_KT, B_V, B_yact, B_s5o, B_att, B_xmid, B_hx, B_f = [Buf(n, dram=True) for n in
                                                                       "uT qcn KT V yact s5o att xmid hx f".split()]
    wffg_b = dscr("wffg_b", [FT, 128, KT * 128], BF16)
    wffu_b = dscr("wffu_b", [FT, 128, KT * 128], BF16)
    wdn_b = dscr("wdn_b", [KT, 128, FT * 128], BF16)
    wout_b = dscr("wout_b", [KT, 128, KT * 128], BF16)
    B_wcast = Buf("wcast", dram=True)
    B_wcast.bg = True
    B_in = Buf("inputs", dram=True)
    B_out = Buf("out", dram=True)

    sbn = [0]

    def sb(st, name, shape, dt):
        sbn[0] += 1
        return st.enter_context(nc.sbuf_tensor(f"s{sbn[0]}_{name}", list(shape), dt))

    ps = [es.enter_context(nc.psum_tensor(f"ps{i}", [128, 512], F32)) for i in range(8)]
    PB = [Buf(f"ps{i}") for i in range(8)]

    ident = sb(es, "ident", [128, 128], F32)
    identb = sb(es, "identb", [128, 128], BF16)
    onesb = sb(es, "onesb", [128, 128], BF16)
    onesf = sb(es, "onesf", [128, 128], F32)
    modc = sb(es, "modc", [128, 192, 2], F32)
    gv = sb(es, "gv", [128, 4, KT], F32)
    vecs = sb(es, "vecs", [128, 8, KT], F32)
    gqs = sb(es, "gqs", [128, 8], F32)
    gkvs = sb(es, "gkvs", [128, 4], F32)
    sc2 = sb(es, "sc2", [128, KT, 2], BF16)
    badas = sb(es, "badas", [128, 192], F32)
    B_c = Buf("consts")
    B_mod = Buf("mod")
    B_vecs = Buf("vecs")
    B_krT = Buf("krT")
    B_kvcn = Buf("kvcn")
    ccs = sb(es, "ccs", [128, KT, 2], F32)
    kvst = contextlib.ExitStack()
    _EXTRA.clear()
    _EXTRA.append(kvst)
    krT = sb(kvst, "krT", [64, NKEY], BF16)
    kvcnT = sb(kvst, "kvcnT", [128, 4, NKEY], BF16)

    def mm(o, l, r, start, stop, reads, writes=(), parts=(), signal=False):
        return kb.op(pe, lambda: T.matmul(o, l, r, start=start, stop=stop), reads=reads, writes=writes,
                     parts=parts, signal=signal)

    kb.dma(sp, ident[:], I.ident, B_c, B_in)
    kb.dma(sp, gv[:], I.gvec, B_c, B_in, part=True)
    kb.dma(sp, gqs[:], I.gq, B_c, B_in, part=True)
    kb.dma(sp, gkvs[:], I.gkv, B_c, B_in, part=True)
    kb.dma(sp, badas[:], I.bada, B_c, B_in, part=True)
    kb.dma(sp, ccs[:], I.ccol, B_c, B_in, part=True)
    kb.op(dve, lambda: V.tensor_copy(identb[:], ident[:]), reads=[B_c], parts=[B_c])
    kb.op(dve, lambda: V.memset(onesb[:], 1.0), parts=[B_c])
    kb.op(dve, lambda: V.memset(onesf[:], 1.0), parts=[B_c])
    kb.op(act, lambda: S.activation(sc2[:], ccs[:], AF.Silu), reads=[B_c], parts=[B_c])

    wst = contextlib.ExitStack()
    NWS = 3
    wslot = [sb(wst, f"wslot{i}", [128, KT * 128], BF16) for i in range(NWS)]
    WB = [Buf(f"wslot{i}") for i in range(NWS)]
    wctr = [0]

    def load_w(src_ap):
        i = wctr[0] % NWS
        wctr[0] += 1
        kb.dma(pool, wslot[i][:], src_ap, WB[i], B_in)
        return wslot[i], WB[i]

    ada_next = [0]

    def adaln(n):
        for _ in range(n):
            m = ada_next[0]
            if m >= 192:
                return
            ada_next[0] += 1
            w, wb = load_w(I.wada[m])
            pb = 7
            for k in range(KT):
                mm(ps[pb][:, 0:2], w[:, k * 128:(k + 1) * 128], sc2[:, k, :], k == 0, k == KT - 1,
                   reads=[wb, B_c], writes=[PB[pb]] if k == 0 else (), parts=() if k == 0 else [PB[pb]],
                   signal=(k == KT - 1))
            kb.op(dve, lambda: V.tensor_scalar(modc[:, m, :], ps[pb][:, 0:2], badas[:, m:m + 1], None, ALU.add),
                  reads=[PB[pb], B_c], parts=[B_mod])

    adaln(64)
    def vec_scale(dst, gi, mlo, col):
        kb.op(dve, lambda: V.scalar_tensor_tensor(vecs[:, dst, :], modc[:, mlo:mlo + KT, col], 1.0, gv[:, gi, :],
                                                  ALU.add, ALU.mult), reads=[B_mod, B_c], parts=[B_vecs])

    def vec_copy(dst, mlo, col):
        kb.op(dve, lambda: V.tensor_copy(vecs[:, dst, :], modc[:, mlo:mlo + KT, col]), reads=[B_mod], parts=[B_vecs])

    def vec_mul(dst, gi, mlo, col):
        kb.op(dve, lambda: V.tensor_tensor(vecs[:, dst, :], modc[:, mlo:mlo + KT, col], gv[:, gi, :], ALU.mult),
              reads=[B_mod, B_c], parts=[B_vecs])

    vec_scale(0, 0, 32, 0)
    vec_copy(1, 0, 0)
    vec_scale(2, 0, 32, 1)
    vec_copy(3, 0, 1)

    if STOP_AFTER == "p0":
        d3 = nc.dram_tensor("dbg_vecs", [128, 8, KT], F32, kind="ExternalOutput").ap()
        kb.dma(sp, d3, vecs[:], B_out, B_vecs)
        wst.close()
        return finish(nc, es, kb, out, B_out)
    st = contextlib.ExitStack()
    xch = [sb(st, f"xch{i}", [128, D], F32) for i in range(2)]
    XB = [Buf(f"xch{i}") for i in range(2)]
    hmod = sb(st, "hmod", [128, KT, 512], BF16)
    B_h = Buf("hmod")
    ssq = sb(st, "ssq", [128, 2], F32)
    rs = sb(st, "rs", [128, 2], F32)
    B_ss = [Buf("ss0"), Buf("ss1")]
    junk = sb(st, "junk", [128, D], BF16)
    B_junk = Buf("junk")
    ust = [sb(st, f"ust{i}", [128, 512], BF16) for i in range(3)]
    UB = [Buf(f"ust{i}") for i in range(3)]
    qcT = sb(st, "qcT", [128, 8, 512], F32)
    B_qc = Buf("qcT")
    sqb = [sb(st, f"sqb{i}", [128, 512], BF16) for i in range(2)]
    SQB = [Buf("sqb0"), Buf("sqb1")]
    rstd = sb(st, "rstd", [128, 512], F32)
    B_rstd = Buf("rstd")
    rtab = sb(st, "rtab", [64, 2, 512], F32)
    B_rtab = Buf("rtab")
    rtmp = sb(st, "rtmp", [64, 512], F32)
    B_rtmp = Buf("rtmp")
    uctr = [0]
    xctr = [0]
    sqctr = [0]

    def p1_tile(kind, t0, w):
        src = I.ctxl if kind == "ctx" else I.xl
        vs, vb = (2, 3) if kind == "ctx" else (0, 1)
        c0 = 0
        while c0 < w:
            cw = min(128, w - c0)
            xi = xctr[0] % 2
            xctr[0] += 1
            xc, xb = xch[xi], XB[xi]
            kb.dma(sp, xc[0:cw, :], src[t0 + c0:t0 + c0 + cw, :], xb, B_in)
            kb.op(act, lambda: S.activation(junk[0:cw, :], xc[0:cw, :], AF.Square, accum_out=ssq[0:cw, xi:xi + 1]),
                  reads=[xb], writes=[B_junk, B_ss[xi]])
            kb.op(act, lambda: S.activation(rs[0:cw, xi:xi + 1], ssq[0:cw, xi:xi + 1], AF.Sqrt, bias=EPS,
                                            scale=1.0 / D), reads=[B_ss[xi]], parts=[B_ss[xi]])
            kb.op(dve, lambda: V.reciprocal(rs[0:cw, xi:xi + 1], rs[0:cw, xi:xi + 1]), reads=[B_ss[xi]],
                  parts=[B_ss[xi]])
            kb.op(dve, lambda: V.tensor_scalar(xc[0:cw, :], xc[0:cw, :], rs[0:cw, xi:xi + 1], None, ALU.mult),
                  reads=[B_ss[xi]], parts=[xb])
            for k4 in range(8):
                pb = k4 % 2
                for j in range(4):
                    k = k4 * 4 + j
                    kb.op(pe, lambda: T.transpose(ps[pb][:, j * 128:j * 128 + cw], xc[0:cw, k * 128:(k + 1) * 128],
                                                  ident[0:cw, 0:cw]),
                          reads=[xb, B_c], writes=[PB[pb]] if j == 0 else (), parts=() if j == 0 else [PB[pb]],
                          signal=(j == 3))
                for j in range(4):
                    k = k4 * 4 + j
                    eng = dve
                    if eng is dve:
                        kb.op(dve, lambda: V.tensor_scalar(hmod[:, k, c0:c0 + cw], ps[pb][:, j * 128:j * 128 + cw],
                                                           vecs[:, vs, k:k + 1], vecs[:, vb, k:k + 1], ALU.mult,
                                                           ALU.add),
                              reads=[PB[pb], B_vecs], parts=[B_h])
                    else:
                        kb.op(act, lambda: S.activation(hmod[:, k, c0:c0 + cw], ps[pb][:, j * 128:j * 128 + cw],
                                                        AF.Identity, bias=vecs[:, vb, k:k + 1],
                                                        scale=vecs[:, vs, k:k + 1]),
                              reads=[PB[pb], B_vecs], parts=[B_h])
            c0 += cw
        mlist = list(range(21)) if kind == "own" else (list(range(8)) + list(range(16, 21)))
        if P1_STAGE == 1:
            return
        if P1_STAGE == 2:
            mlist = [0, 1]
        if P1_STAGE in (3, 4, 5):
            mlist = [0, 1, 16, 17, 18, 19]
        if kind == "ctx":
            keyc = SEQ
        else:
            keyc = t0
        for m in mlist:
            wt, wb = load_w(I.win[m])
            if m < 20:
                pb = 2 + (m % 2)
                for k in range(KT):
                    mm(ps[pb][:, 0:w], wt[:, k * 128:(k + 1) * 128], hmod[:, k, 0:w], k == 0, k == KT - 1,
                       reads=[wb, B_h], writes=[PB[pb]] if k == 0 else (), parts=() if k == 0 else [PB[pb]],
                       signal=(k == KT - 1))
                if m < 8:
                    ui = uctr[0] % 3
                    uctr[0] += 1
                    kb.op(act, lambda: S.copy(ust[ui][:, 0:w], ps[pb][:, 0:w]), reads=[PB[pb]], writes=[UB[ui]])
                    if kind == "ctx":
                        kb.dma(sp, uT[m * 128:(m + 1) * 128, 0:CTX], ust[ui][:, 0:w], B_uT, UB[ui])
                        kb.dma(sp, uT[m * 128:(m + 1) * 128, CTX + SEQ:UCOLS], ust[ui][:, 0:w], B_uT, UB[ui])
                    else:
                        kb.dma(sp, uT[m * 128:(m + 1) * 128, CTX + t0:CTX + t0 + w], ust[ui][:, 0:w], B_uT, UB[ui])
                else:
                    j = m - 8 if m < 16 else m - 16
                    nj = 8 if m < 16 else 4
                    kb.op(dve, lambda: V.tensor_copy(qcT[:, j, 0:w], ps[pb][:, 0:w]), reads=[PB[pb]], parts=[B_qc])
                    si = sqctr[0] % 2
                    sqctr[0] += 1
                    kb.op(act, lambda: S.activation(sqb[si][:, 0:w], qcT[:, j, 0:w], AF.Square), reads=[B_qc],
                          writes=[SQB[si]])
                    if P1_STAGE != 5:
                        mm(ps[4][:, 0:w], onesb[:], sqb[si][:, 0:w], j == 0, j == nj - 1, reads=[SQB[si], B_c],
                           writes=[PB[4]] if j == 0 else (), parts=() if j == 0 else [PB[4]], signal=True)
                    if j == nj - 1 and P1_STAGE not in (4, 5):
                        nfeat = 1024.0 if m < 16 else 512.0
                        kb.op(act, lambda: S.activation(rstd[:, 0:w], ps[4][:, 0:w], AF.Sqrt, bias=EPS,
                                                        scale=1.0 / nfeat), reads=[PB[4]], writes=[B_rstd])
                        kb.op(dve, lambda: V.reciprocal(rstd[:, 0:w], rstd[:, 0:w]), reads=[B_rstd], parts=[B_rstd])
                        for jj in range(nj):
                            if m < 16:
                                ui = uctr[0] % 3
                                uctr[0] += 1
                                kb.op(dve, lambda: V.scalar_tensor_tensor(ust[ui][:, 0:w], qcT[:, jj, 0:w],
                                                                          gqs[:, jj:jj + 1], rstd[:, 0:w], ALU.mult,
                                                                          ALU.mult),
                                      reads=[B_qc, B_rstd, B_c], writes=[UB[ui]])
                                kb.dma(sp, qcnTs[jj * 128:(jj + 1) * 128, t0:t0 + w], ust[ui][:, 0:w], B_qcn, UB[ui])
                            else:
                                kb.op(dve, lambda: V.scalar_tensor_tensor(kvcnT[:, jj, keyc:keyc + w],
                                                                          qcT[:, jj, 0:w], gkvs[:, jj:jj + 1],
                                                                          rstd[:, 0:w], ALU.mult, ALU.mult),
                                      reads=[B_qc, B_rstd, B_c], parts=[B_kvcn])
            else:
                kb.dma(sp, rtab[:, :, 0:w], I.ropek[:, :, keyc:keyc + w], B_rtab, B_in)
                for half in range(2):
                    pb = 5 + half
                    for k in range(KT):
                        mm(ps[pb][0:64, 0:w], wt[:, k * 128 + half * 64:k * 128 + half * 64 + 64], hmod[:, k, 0:w],
                           k == 0, k == KT - 1, reads=[wb, B_h], writes=[PB[pb]] if k == 0 else (),
                           parts=() if k == 0 else [PB[pb]], signal=(k == KT - 1))
                kb.op(dve, lambda: V.tensor_tensor(rtmp[:, 0:w], ps[5][0:64, 0:w], rtab[:, 0, 0:w], ALU.mult),
                      reads=[PB[5], B_rtab], writes=[B_rtmp])
                kb.op(dve, lambda: V.tensor_tensor(rtab[:, 1, 0:w], ps[6][0:64, 0:w], rtab[:, 1, 0:w], ALU.mult),
                      reads=[PB[6], B_rtab], parts=[B_rtab])
                kb.op(dve, lambda: V.tensor_tensor(krT[:, keyc:keyc + w], rtmp[:, 0:w], rtab[:, 1, 0:w], ALU.add),
                      reads=[B_rtmp, B_rtab], parts=[B_krT])

    tiles = [("ctx", 0, CTX)] + [("own", a, b) for a, b in OWN_TILES] + [("rest", a, b) for a, b in REST_TILES]
    if P1_LIMIT is not None:
        tiles = tiles[:P1_LIMIT]
    if "p1" in SKIP:
        tiles = []
    for (kind, t0, w) in tiles:
        p1_tile(kind, t0, w)
        adaln(13)
    adaln(200)
    vec_mul(4, 1, 64, 0)
    vec_scale(5, 2, 128, 0)
    vec_copy(6, 96, 0)
    vec_mul(7, 3, 160, 0)
    kb.barrier()
    st.close()
    wst.close()
    if STOP_AFTER == "p1":
        d1 = nc.dram_tensor("dbg_kvcn", [128, 4, NKEY], BF16, kind="ExternalOutput").ap()
        d2 = nc.dram_tensor("dbg_krT", [64, NKEY], BF16, kind="ExternalOutput").ap()
        d3 = nc.dram_tensor("dbg_vecs", [128, 8, KT], F32, kind="ExternalOutput").ap()
        if P1_LIMIT is None and P1_STAGE == 0:
            kb.dma(sp, d1, kvcnT[:], B_out, B_kvcn)
            kb.dma(sp, d2, krT[:], B_out, B_krT)
        kb.dma(sp, d3, vecs[:], B_out, B_vecs)
        return finish(nc, es, kb, out, B_out)

    st = contextlib.ExitStack()
    wk = sb(st, "wk", [128, 4 * 3072], BF16)
    wv = sb(st, "wv", [128, 4 * 3072], BF16)
    B_wk, B_wv = Buf("wk"), Buf("wv")
    kb.dma(pool, wk[:], I.wuk, B_wk, B_in)
    kb.dma(pool, wv[:], I.wuv, B_wv, B_in)
    kst = [sb(st, f"kst{i}", [128, 512], BF16) for i in range(4)]
    KSB = [Buf(f"kst{i}") for i in range(4)]
    ctr = 0
    for h in range(0 if "p2" in SKIP else NH):
        for c0 in range(0, NKEY, 512):
            w = min(512, NKEY - c0)
            pb = ctr % 2
            si = ctr % 4
            ctr += 1
            for rk in range(4):
                mm(ps[pb][:, 0:w], wk[:, rk * 3072 + h * 128:rk * 3072 + (h + 1) * 128], kvcnT[:, rk, c0:c0 + w],
                   rk == 0, rk == 3, reads=[B_wk, B_kvcn], writes=[PB[pb]] if rk == 0 else (),
                   parts=() if rk == 0 else [PB[pb]], signal=(rk == 3))
            if ctr % 2 == 0:
                kb.op(act, lambda: S.copy(kst[si][:, 0:w], ps[pb][:, 0:w]), reads=[PB[pb]], writes=[KSB[si]])
            else:
                kb.op(dve, lambda: V.tensor_copy(kst[si][:, 0:w], ps[pb][:, 0:w]), reads=[PB[pb]], writes=[KSB[si]])
            kb.dma(sp, KTs[h, :, c0:c0 + w], kst[si][:, 0:w], B_KT, KSB[si])
    for kt in range(0 if "p2" in SKIP else 34):
        for hg in range(6):
            pb = ctr % 2
            si = ctr % 4
            ctr += 1
            for rk in range(4):
                mm(ps[pb][:, 0:512], kvcnT[:, rk, kt * 128:(kt + 1) * 128],
                   wv[:, rk * 3072 + hg * 512:rk * 3072 + (hg + 1) * 512], rk == 0, rk == 3,
                   reads=[B_wv, B_kvcn], writes=[PB[pb]] if rk == 0 else (), parts=() if rk == 0 else [PB[pb]],
                   signal=(rk == 3))
            if ctr % 2 == 0:
                kb.op(act, lambda: S.copy(kst[si][:, :], ps[pb][:, :]), reads=[PB[pb]], writes=[KSB[si]])
            else:
                kb.op(dve, lambda: V.tensor_copy(kst[si][:, :], ps[pb][:, :]), reads=[PB[pb]], writes=[KSB[si]])
            kb.dma(sp, Vs[kt, :, hg * 512:(hg + 1) * 512], kst[si][:, :], B_V, KSB[si])
    kb.barrier()
    st.close()
    if STOP_AFTER == "p2":
        return finish(nc, es, kb, out, B_out)

    st = contextlib.ExitStack()
    TWO_PI = 6.283185307179586
    lam = sb(st, "lam", [128, 2, 64], F32)
    lsd = sb(st, "lsd", [128, 64], F32)
    Ball = sb(st, "Ball", [128, 64, 2, 32], F32)
    Call = sb(st, "Call", [128, 64, 2, 32], F32)
    dcol = sb(st, "dcol", [32, 32], F32)
    B_s5 = Buf("s5setup")
    B_BC = Buf("BC")
    kb.dma(sp, lam[:], I.s5lam, B_s5, B_in)
    kb.dma(sp, lsd[:], I.s5ls, B_s5, B_in, part=True)
    kb.dma(sp, dcol[:], I.s5d, B_s5, B_in, part=True)
    kb.op(dve, lambda: V.memset(Ball[:], 0.0), writes=[B_BC])
    kb.op(dve, lambda: V.memset(Call[:], 0.0), parts=[B_BC])
    kb.dma(sp, Ball[0:64, :, :, 0:16], I.s5b[0:64], B_BC, B_in)
    kb.dma(sp, Ball[64:128, :, :, 16:32], I.s5b[64:128], B_BC, B_in, part=True)
    kb.dma(sp, Call[0:64, :, :, 0:16], I.s5c[0:64], B_BC, B_in, part=True)
    kb.dma(sp, Call[64:128, :, :, 16:32], I.s5c[64:128], B_BC, B_in, part=True)
    nsc = [0]

    def stile(dt=F32, shape=(128, 64)):
        nsc[0] += 1
        return sb(st, f"s5t{nsc[0]}", list(shape), dt)

    def dv(f):
        kb.op(dve, f, reads=[B_s5], parts=[B_s5])

    def ac(f):
        kb.op(act, f, reads=[B_s5], parts=[B_s5])

    lr, li = lam[:, 0, :], lam[:, 1, :]
    dtt, mag, ang, nf, s2, s4, ch, sinr, cosr, t1, t2 = [stile() for _ in range(11)]
    ni = stile(mybir.dt.int32)
    ac(lambda: S.activation(dtt[:], lsd[:], AF.Exp))
    dv(lambda: V.tensor_tensor(t1[:], lr, dtt[:], ALU.mult))
    ac(lambda: S.activation(mag[:], t1[:], AF.Exp))
    dv(lambda: V.tensor_tensor(ang[:], li, dtt[:], ALU.mult))
    dv(lambda: V.tensor_scalar(t1[:], ang[:], 1.0 / TWO_PI, None, ALU.mult))
    dv(lambda: V.tensor_copy(ni[:], t1[:]))
    dv(lambda: V.tensor_copy(nf[:], ni[:]))
    dv(lambda: V.scalar_tensor_tensor(t2[:], nf[:], -TWO_PI, ang[:], ALU.mult, ALU.add))
    ac(lambda: S.activation(s2[:], t2[:], AF.Sin, scale=0.5))
    ac(lambda: S.activation(s4[:], t2[:], AF.Sin, scale=0.25))
    dv(lambda: V.tensor_tensor(t1[:], s4[:], s4[:], ALU.mult))
    dv(lambda: V.tensor_scalar(ch[:], t1[:], -2.0, 1.0, ALU.mult, ALU.add))
    dv(lambda: V.tensor_tensor(t1[:], s2[:], ch[:], ALU.mult))
    dv(lambda: V.tensor_scalar(sinr[:], t1[:], 2.0, None, ALU.mult))
    dv(lambda: V.tensor_tensor(t1[:], s2[:], s2[:], ALU.mult))
    dv(lambda: V.tensor_scalar(cosr[:], t1[:], -2.0, 1.0, ALU.mult, ALU.add))
    apw = sb(st, "apw", [128, 9, 2, 64], F32)
    napw = sb(st, "napw", [128, 9, 2, 64], F32)
    lev = sb(st, "lev", [128, 10, 2, 64], F32)
    nlev = sb(st, "nlev", [128, 10, 64], F32)
    ff = sb(st, "ff", [128, 2, 64], F32)
    nfi = stile()
    dv(lambda: V.memset(apw[:, 0, 0, :], 1.0))
    dv(lambda: V.memset(apw[:, 0, 1, :], 0.0))
    dv(lambda: V.tensor_tensor(apw[:, 1, 0, :], mag[:], cosr[:], ALU.mult))
    dv(lambda: V.tensor_tensor(apw[:, 1, 1, :], mag[:], sinr[:], ALU.mult))
    ar, ai = apw[:, 1, 0, :], apw[:, 1, 1, :]
    nr, den = stile(), stile()
    dv(lambda: V.tensor_scalar(nr[:], ar, -1.0, None, ALU.add))
    dv(lambda: V.tensor_tensor(t1[:], lr, lr, ALU.mult))
    dv(lambda: V.tensor_tensor(t2[:], li, li, ALU.mult))
    dv(lambda: V.tensor_tensor(den[:], t1[:], t2[:], ALU.add))
    dv(lambda: V.reciprocal(den[:], den[:]))
    dv(lambda: V.tensor_tensor(t1[:], nr[:], lr, ALU.mult))
    dv(lambda: V.tensor_tensor(t2[:], ai, li, ALU.mult))
    dv(lambda: V.tensor_tensor(t1[:], t1[:], t2[:], ALU.add))
    dv(lambda: V.tensor_tensor(ff[:, 0, :], t1[:], den[:], ALU.mult))
    dv(lambda: V.tensor_tensor(t1[:], ai, lr, ALU.mult))
    dv(lambda: V.tensor_tensor(t2[:], nr[:], li, ALU.mult))
    dv(lambda: V.tensor_tensor(t1[:], t1[:], t2[:], ALU.subtract))
    dv(lambda: V.tensor_tensor(ff[:, 1, :], t1[:], den[:], ALU.mult))
    dv(lambda: V.tensor_scalar(nfi[:], ff[:, 1, :], -1.0, None, ALU.mult))

    def cmul(o_r, o_i, a_r, a_i, b_r, b_i):
        dv(lambda: V.tensor_tensor(t1[:], a_r, b_r, ALU.mult))
        dv(lambda: V.tensor_tensor(t2[:], a_i, b_i, ALU.mult))
        dv(lambda: V.tensor_tensor(den[:], a_r, b_i, ALU.mult))
        dv(lambda: V.tensor_tensor(nr[:], a_i, b_r, ALU.mult))
        dv(lambda: V.tensor_tensor(o_r, t1[:], t2[:], ALU.subtract))
        dv(lambda: V.tensor_tensor(o_i, den[:], nr[:], ALU.add))

    for k in range(2, 9):
        cmul(apw[:, k, 0, :], apw[:, k, 1, :], apw[:, k - 1, 0, :], apw[:, k - 1, 1, :], ar, ai)
    dv(lambda: V.tensor_scalar(napw[:], apw[:], -1.0, None, ALU.mult))
    dv(lambda: V.tensor_copy(lev[:, 0, :, :], apw[:, 8, :, :]))
    for l in range(1, 10):
        cmul(lev[:, l, 0, :], lev[:, l, 1, :], lev[:, l - 1, 0, :], lev[:, l - 1, 1, :], lev[:, l - 1, 0, :],
             lev[:, l - 1, 1, :])
    dv(lambda: V.tensor_scalar(nlev[:], lev[:, :, 1, :], -1.0, None, ALU.mult))

    uTp = [sb(st, f"uTp{i}", [32, UCOLS], BF16) for i in range(2)]
    B_uTp = [Buf("uTp0"), Buf("uTp1")]
    bbar = sb(st, "bbar", [128, 2, 32], F32)
    B_bbar = Buf("bbar")
    Eb = sb(st, "Eb", [128, 8, 2, 32], BF16)
    B_Eb = Buf("Eb")
    Fb = [sb(st, f"Fb{i}", [128, 8, 2, 32], BF16) for i in range(2)]
    B_Fb = [Buf("Fb0"), Buf("Fb1")]
    Cb = sb(st, "Cb", [128, 2, 32], BF16)
    B_Cb = Buf("Cb")
    Bw = [sb(st, f"Bw{i}", [32, 8, 2, 128], BF16) for i in range(2)]
    B_Bw = [Buf("Bw0"), Buf("Bw1")]
    Kt = [sb(st, f"Kt{i}", [32, 8, 32], BF16) for i in range(2)]
    B_Kt = [Buf("Kt0"), Buf("Kt1")]
    K0 = sb(st, "K0", [32, 32], BF16)
    K0f = sb(st, "K0f", [32, 32], F32)
    B_K0 = Buf("K0")
    tmpE = [sb(st, f"tmpE{i}", [128, 32], F32) for i in range(4)]
    B_tmpE = [Buf(f"tmpE{i}") for i in range(4)]
    XA = [sb(st, f"XA{i}", [128, 2, NBA], F32) for i in range(2)]
    XBt = [sb(st, f"XB{i}", [128, 2, NBB], F32) for i in range(2)]
    B_XA = [Buf("XA0"), Buf("XA1")]
    B_XB = [Buf("XB0"), Buf("XB1")]
    SA = sb(st, "SA", [128, 2, NBA], BF16)
    SB_ = sb(st, "SB", [128, 2, NBB], BF16)
    B_SA, B_SB = Buf("SA"), Buf("SB")
    ys = [sb(st, f"ys{i}", [32, 512], F32) for i in range(3)]
    B_ys = [Buf(f"ys{i}") for i in range(3)]
    yo = [sb(st, f"yo{i}", [32, 512], BF16) for i in range(2)]
    B_yo = [Buf("yo0"), Buf("yo1")]
    yfs = [sb(st, f"yf{i}", [32, 512], F32) for i in range(2)]
    B_yfs = [Buf("yf0"), Buf("yf1")]
    tec = [0]
    psb6 = ps[6][:].bitcast(BF16)

    def two_term(out_ap, a_ap, sa, b_ap, sbb, reads, out_buf, part=True):
        i = tec[0] % 4
        tec[0] += 1
        kb.op(dve, lambda: V.tensor_scalar(tmpE[i][:], b_ap, sbb, None, ALU.mult), reads=reads + [B_s5],
              writes=[B_tmpE[i]])
        kb.op(dve, lambda: V.scalar_tensor_tensor(out_ap, a_ap, sa, tmpE[i][:], ALU.mult, ALU.add),
              reads=reads + [B_s5, B_tmpE[i]], parts=[out_buf])

    def hs_scan(X, BX, nblk, dp, forward):
        cur, s, l = 0, 1, 0
        while s < nblk:
            Pr, Pi, nPi = lev[:, l, 0, dp:dp + 1], lev[:, l, 1, dp:dp + 1], nlev[:, l, dp:dp + 1]
            o, n = X[cur], X[1 - cur]
            bo, bn = BX[cur], BX[1 - cur]
            if forward:
                d0, d1, s0, s1, k0, k1 = s, nblk, 0, nblk - s, 0, s
            else:
                d0, d1, s0, s1, k0, k1 = 0, nblk - s, s, nblk, nblk - s, nblk
            kb.op(dve, lambda: V.scalar_tensor_tensor(n[:, 0, d0:d1], o[:, 0, s0:s1], Pr, o[:, 0, d0:d1], ALU.mult,
                                                      ALU.add), reads=[bo, B_s5], writes=[bn])
            kb.op(dve, lambda: V.scalar_tensor_tensor(n[:, 0, d0:d1], o[:, 1, s0:s1], nPi, n[:, 0, d0:d1], ALU.mult,
                                                      ALU.add), reads=[bo, B_s5, bn], parts=[bn])
            kb.op(dve, lambda: V.scalar_tensor_tensor(n[:, 1, d0:d1], o[:, 0, s0:s1], Pi, o[:, 1, d0:d1], ALU.mult,
                                                      ALU.add), reads=[bo, B_s5], parts=[bn])
            kb.op(dve, lambda: V.scalar_tensor_tensor(n[:, 1, d0:d1], o[:, 1, s0:s1], Pr, n[:, 1, d0:d1], ALU.mult,
                                                      ALU.add), reads=[bo, B_s5, bn], parts=[bn])
            kb.op(act, lambda: S.copy(n[:, :, k0:k1], o[:, :, k0:k1]), reads=[bo], parts=[bn])
            cur, s, l = 1 - cur, s * 2, l + 1
        return cur

    YCH = [(0, 64), (64, 64), (128, 64), (192, 64), (256, 1)]
    ychk = [0]
    for pair in range(0 if "p3" in SKIP else 32):
        ui = pair % 2
        kb.dma(sp, uTp[ui][:], uT[pair * 32:(pair + 1) * 32, :], B_uTp[ui], B_uT)
        for dr in range(2):
            dp = dr * 32 + pair
            Br, Bi = Ball[:, dp, 0, :], Ball[:, dp, 1, :]
            Cr, Ci = Call[:, dp, 0, :], Call[:, dp, 1, :]
            fr, fi, nfi_ = ff[:, 0, dp:dp + 1], ff[:, 1, dp:dp + 1], nfi[:, dp:dp + 1]
            kb.op(dve, lambda: V.tensor_scalar(bbar[:, 0, :], Br, fr, None, ALU.mult), reads=[B_BC, B_s5],
                  writes=[B_bbar])
            kb.op(dve, lambda: V.scalar_tensor_tensor(bbar[:, 0, :], Bi, nfi_, bbar[:, 0, :], ALU.mult, ALU.add),
                  reads=[B_BC, B_s5, B_bbar], parts=[B_bbar])
            kb.op(dve, lambda: V.tensor_scalar(bbar[:, 1, :], Br, fi, None, ALU.mult), reads=[B_BC, B_s5],
                  parts=[B_bbar])
            kb.op(dve, lambda: V.scalar_tensor_tensor(bbar[:, 1, :], Bi, fr, bbar[:, 1, :], ALU.mult, ALU.add),
                  reads=[B_BC, B_s5, B_bbar], parts=[B_bbar])
            for k in range(8):
                akr, aki, naki = apw[:, k, 0, dp:dp + 1], apw[:, k, 1, dp:dp + 1], napw[:, k, 1, dp:dp + 1]
                two_term(Eb[:, k, 0, :], bbar[:, 0, :], akr, bbar[:, 1, :], naki, [B_bbar], B_Eb)
                two_term(Eb[:, k, 1, :], bbar[:, 0, :], aki, bbar[:, 1, :], akr, [B_bbar], B_Eb)
            for k in range(1, 9):
                akr, aki = apw[:, k, 0, dp:dp + 1], apw[:, k, 1, dp:dp + 1]
                nakr, naki = napw[:, k, 0, dp:dp + 1], napw[:, k, 1, dp:dp + 1]
                two_term(Fb[dr][:, k - 1, 0, :], Cr, akr, Ci, naki, [B_BC], B_Fb[dr])
                two_term(Fb[dr][:, k - 1, 1, :], Cr, naki, Ci, nakr, [B_BC], B_Fb[dr])
            kb.op(dve, lambda: V.tensor_copy(Cb[:, 0, :], Cr), reads=[B_BC], parts=[B_Cb])
            kb.op(dve, lambda: V.tensor_scalar(Cb[:, 1, :], Ci, -1.0, None, ALU.mult), reads=[B_BC], parts=[B_Cb])
            for ri in range(2):
                for sg_ in range(8):
                    k = (7 - sg_) if dr == 0 else sg_
                    kb.op(pe, lambda: T.transpose(psb6[0:32, sg_ * 128:(sg_ + 1) * 128], Eb[:, k, ri, :], identb[:, :]),
                          reads=[B_Eb, B_c], writes=[PB[6]] if sg_ == 0 else (), parts=() if sg_ == 0 else [PB[6]],
                          signal=(sg_ == 7))
                kb.op(act, lambda: S.copy(Bw[dr][:, :, ri, :], psb6[0:32, 0:1024].rearrange("p (s c) -> p s c", c=128)),
                      reads=[PB[6]], parts=[B_Bw[dr]])
            for tau in range(8):
                mm(ps[7][0:32, tau * 32:(tau + 1) * 32], Eb[:, tau, 0, :], Cb[:, 0, :], True, False,
                   reads=[B_Eb, B_Cb], writes=[PB[7]] if tau == 0 else (), parts=() if tau == 0 else [PB[7]])
                mm(ps[7][0:32, tau * 32:(tau + 1) * 32], Eb[:, tau, 1, :], Cb[:, 1, :], False, True,
                   reads=[B_Eb, B_Cb], parts=[PB[7]], signal=(tau == 7))
            kb.op(dve, lambda: V.tensor_copy(Kt[dr][:], ps[7][0:32, 0:256].rearrange("p (t c) -> p t c", c=32)),
                  reads=[PB[7]], writes=[B_Kt[dr]])
            if dr == 0:
                kb.op(dve, lambda: V.tensor_copy(K0f[:], ps[7][0:32, 0:32]), reads=[PB[7]], writes=[B_K0])
            else:
                kb.op(dve, lambda: V.tensor_tensor(K0f[:], K0f[:], ps[7][0:32, 0:32], ALU.add), reads=[PB[7], B_K0],
                      parts=[B_K0])
                kb.op(dve, lambda: V.scalar_tensor_tensor(K0f[:], ident[0:32, 0:32], dcol[:, pair:pair + 1], K0f[:],
                                                          ALU.mult, ALU.add), reads=[B_K0, B_c, B_s5], parts=[B_K0])
                kb.op(dve, lambda: V.tensor_copy(K0[:], K0f[:]), reads=[B_K0], parts=[B_K0])
        for ri in range(2):
            for sg_ in range(8):
                mm(ps[ri][:, 0:NBA], Bw[0][:, sg_, ri, :], uTp[ui][:, sg_:8 * NBA:8], sg_ == 0, sg_ == 7,
                   reads=[B_Bw[0], B_uTp[ui]], writes=[PB[ri]] if sg_ == 0 else (), parts=() if sg_ == 0 else [PB[ri]],
                   signal=(sg_ == 7))
            kb.op(act if ri == 0 else dve,
                  (lambda: S.copy(XA[0][:, ri, :], ps[ri][:, 0:NBA])) if ri == 0 else
                  (lambda: V.tensor_copy(XA[0][:, ri, :], ps[ri][:, 0:NBA])),
                  reads=[PB[ri]], writes=[B_XA[0]] if ri == 0 else (), parts=() if ri == 0 else [B_XA[0]])
        for ri in range(2):
            for c in range(2):
                pbk = 2 + 2 * ri + c
                base = CTX + 8 * 272 * c
                for sg_ in range(8):
                    mm(ps[pbk][:, 0:272], Bw[1][:, sg_, ri, :], uTp[ui][:, base + sg_:base + 8 * 272:8], sg_ == 0,
                       sg_ == 7, reads=[B_Bw[1], B_uTp[ui]], writes=[PB[pbk]] if sg_ == 0 else (),
                       parts=() if sg_ == 0 else [PB[pbk]], signal=(sg_ == 7))
                first = (ri == 0 and c == 0)
                kb.op(dve, lambda: V.tensor_copy(XBt[0][:, ri, 272 * c:272 * (c + 1)], ps[pbk][:, 0:272]),
                      reads=[PB[pbk]], writes=[B_XB[0]] if first else (), parts=() if first else [B_XB[0]])
        ca = hs_scan(XA, B_XA, NBA, pair, True)
        cb = hs_scan(XBt, B_XB, NBB, 32 + pair, False)
        kb.op(dve, lambda: V.tensor_copy(SA[:], XA[ca][:]), reads=[B_XA[ca]], writes=[B_SA])
        kb.op(dve, lambda: V.tensor_copy(SB_[:], XBt[cb][:]), reads=[B_XB[cb]], writes=[B_SB])
        for (jb0, nb) in YCH:
            yb = 6 + (ychk[0] % 2)
            ychk[0] += 1
            for sg_ in range(8):
                o_ap = ps[yb][0:32, sg_:8 * nb:8]
                kA, kB = sg_, 7 - sg_
                mm(o_ap, Fb[0][:, kA, 0, :], SA[:, 0, 31 + jb0:31 + jb0 + nb], True, False,
                   reads=[B_Fb[0], B_SA], writes=[PB[yb]] if sg_ == 0 else (), parts=() if sg_ == 0 else [PB[yb]])
                mm(o_ap, Fb[0][:, kA, 1, :], SA[:, 1, 31 + jb0:31 + jb0 + nb], False, False,
                   reads=[B_Fb[0], B_SA], parts=[PB[yb]])
                mm(o_ap, Fb[1][:, kB, 0, :], SB_[:, 0, jb0 + 1:jb0 + 1 + nb], False, False,
                   reads=[B_Fb[1], B_SB], parts=[PB[yb]])
                mm(o_ap, Fb[1][:, kB, 1, :], SB_[:, 1, jb0 + 1:jb0 + 1 + nb], False, False,
                   reads=[B_Fb[1], B_SB], parts=[PB[yb]])
                for sp_ in range(8):
                    if sp_ < sg_:
                        l_ap = Kt[0][:, sg_ - sp_, :]
                    elif sp_ > sg_:
                        l_ap = Kt[1][:, sp_ - sg_, :]
                    else:
                        l_ap = K0[:, :]
                    c0 = CTX + 8 * jb0 + sp_
                    mm(o_ap, l_ap, uTp[ui][:, c0:CTX + 8 * (jb0 + nb):8], False, sp_ == 7,
                       reads=[B_Kt[0], B_Kt[1], B_K0, B_uTp[ui]], parts=[PB[yb]], signal=(sp_ == 7 and sg_ == 7))
            n8 = 8 * nb
            yi = ychk[0] % 3
            yf = yfs[ychk[0] % 2]
            B_yf = B_yfs[ychk[0] % 2]
            kb.op(dve, lambda: V.tensor_copy(yf[:, 0:n8], ps[yb][0:32, 0:n8]), reads=[PB[yb]], writes=[B_yf])
            kb.op(act, lambda: S.activation(ys[yi][:, 0:n8], yf[:, 0:n8], AF.Square), reads=[B_yf],
                  writes=[B_ys[yi]])
            kb.op(dve, lambda: V.tensor_scalar(ys[yi][:, 0:n8], ys[yi][:, 0:n8], 0.044715, 1.0, ALU.mult, ALU.add),
                  reads=[B_ys[yi]], parts=[B_ys[yi]])
            kb.op(dve, lambda: V.tensor_tensor(ys[yi][:, 0:n8], ys[yi][:, 0:n8], yf[:, 0:n8], ALU.mult),
                  reads=[B_ys[yi], B_yf], parts=[B_ys[yi]])
            kb.op(act, lambda: S.activation(ys[yi][:, 0:n8], ys[yi][:, 0:n8], AF.Sigmoid, scale=1.5957691216),
                  reads=[B_ys[yi]], parts=[B_ys[yi]])
            oi = ychk[0] % 2
            kb.op(dve, lambda: V.tensor_tensor(yo[oi][:, 0:n8], ys[yi][:, 0:n8], yf[:, 0:n8], ALU.mult),
                  reads=[B_ys[yi], B_yf], writes=[B_yo[oi]])
            kb.dma(sp, yactT[pair * 32:(pair + 1) * 32, 8 * jb0:8 * jb0 + n8], yo[oi][:, 0:n8], B_yact, B_yo[oi])
    kb.barrier()
    st.close()
    if STOP_AFTER == "p3a":
        return finish(nc, es, kb, out, B_out)
    st = contextlib.ExitStack()
    wg = sb(st, "wg", [128, 8, 8 * 128], BF16)
    B_wg = Buf("wg")
    for m in range(8):
        kb.dma(pool, wg[:, m, :], I.wglu[m], B_wg, B_in, part=(m > 0))
    ya = [sb(st, f"ya{i}", [128, 8, NT], BF16) for i in range(2)]
    B_ya = [Buf("ya0"), Buf("ya1")]
    sgt = [sb(st, f"sgt{i}", [128, NT], F32) for i in range(2)]
    B_sgt = [Buf("sgt0"), Buf("sgt1")]
    go = [sb(st, f"go{i}", [128, NT], BF16) for i in range(2)]
    B_go = [Buf("go0"), Buf("go1")]
    gctr = 0
    for ti, (t0, w) in enumerate([] if "p3" in SKIP else OWN_TILES):
        yi = ti % 2
        for k in range(8):
            kb.dma(sp, ya[yi][:, k, 0:w], yactT[k * 128:(k + 1) * 128, t0:t0 + w], B_ya[yi], B_yact, part=(k > 0))
        for m in range(8):
            pb = m % 2
            for k in range(8):
                mm(ps[pb][:, 0:w], wg[:, m, k * 128:(k + 1) * 128], ya[yi][:, k, 0:w], k == 0, k == 7,
                   reads=[B_wg, B_ya[yi]], writes=[PB[pb]] if k == 0 else (), parts=() if k == 0 else [PB[pb]],
                   signal=(k == 7))
            gi = gctr % 2
            gctr += 1
            kb.op(act, lambda: S.activation(sgt[gi][:, 0:w], ps[pb][:, 0:w], AF.Sigmoid), reads=[PB[pb]],
                  writes=[B_sgt[gi]])
            kb.op(dve, lambda: V.tensor_tensor(go[gi][:, 0:w], sgt[gi][:, 0:w], ya[yi][:, m, 0:w], ALU.mult),
                  reads=[B_sgt[gi], B_ya[yi]], writes=[B_go[gi]])
            kb.dma(sp, s5outT[m * 128:(m + 1) * 128, t0:t0 + w], go[gi][:, 0:w], B_s5o, B_go[gi])
    kb.barrier()
    st.close()
    if STOP_AFTER == "p3":
        return finish(nc, es, kb, out, B_out)

    st = contextlib.ExitStack()
    qcn = sb(st, "qcn", [128, 8, NOWN], BF16)
    B_qcnS = Buf("qcnS")
    for j in range(0 if "p4" in SKIP else 8):
        kb.dma(sp, qcn[:, j, :], qcnTs[j * 128:(j + 1) * 128, :], B_qcnS, B_qcn, part=(j > 0))
    rq = sb(st, "rq", [64, 2, NOWN], F32)
    B_rq = Buf("rq")
    kb.dma(sp, rq[:], I.ropeq, B_rq, B_in)
    kth = [sb(st, f"kth{i}", [128, NKEY], BF16) for i in range(2)]
    vh = [sb(st, f"vh{i}", [128, 34, 128], BF16) for i in range(2)]
    wq = [sb(st, f"wq{i}", [128, 8 * 256], BF16) for i in range(2)]
    B_kth = [Buf("kth0"), Buf("kth1")]
    B_vh = [Buf("vh0"), Buf("vh1")]
    B_wq = [Buf("wq0"), Buf("wq1")]
    qn = [sb(st, f"qn{i}", [128, NT], BF16) for i in range(2)]
    qr = [sb(st, f"qr{i}", [64, NT], BF16) for i in range(2)]
    B_qn = [Buf("qn0"), Buf("qn1")]
    B_qr = [Buf("qr0"), Buf("qr1")]
    qtmp = sb(st, "qtmp", [64, NT], F32)
    qtmp2 = sb(st, "qtmp2", [64, NT], F32)
    B_qtmp, B_qtmp2 = Buf("qtmp"), Buf("qtmp2")
    NPS = 4
    pT = [sb(st, f"pT{i}", [128, NT], BF16) for i in range(NPS)]
    B_pT = [Buf(f"pT{i}") for i in range(NPS)]
    acc = sb(st, "acc", [128, NT], F32)
    B_acc = Buf("acc")
    rinv = sb(st, "rinv", [128, NT], F32)
    B_rinv = Buf("rinv")
    ast = [sb(st, f"ast{i}", [128, NT], BF16) for i in range(2)]
    B_ast = [Buf("ast0"), Buf("ast1")]

    def load_head(h):
        i = h % 2
        kb.dma(sp, kth[i][:], KTs[h], B_kth[i], B_KT)
        kb.dma(sp, vh[i][:], Vs[:, :, h * 128:(h + 1) * 128].rearrange("k p d -> p k d"), B_vh[i], B_V)
        kb.dma(pool, wq[i][:], I.wuq[h], B_wq[i], B_in)

    work = [(h, t0, w) for h in range(0 if "p4" in SKIP else NH) for (t0, w) in OWN_TILES]

    def emit_qproj(idx):
        h, t0, w = work[idx]
        hi, qi = h % 2, idx % 2
        for k in range(8):
            mm(ps[0][:, 0:w], wq[hi][:, k * 256:k * 256 + 128], qcn[:, k, t0:t0 + w], k == 0, k == 7,
               reads=[B_wq[hi], B_qcnS], writes=[PB[0]] if k == 0 else (), parts=() if k == 0 else [PB[0]],
               signal=(k == 7))
        for half in range(2):
            pb = 1 + half
            for k in range(8):
                mm(ps[pb][0:64, 0:w], wq[hi][:, k * 256 + 128 + half * 64:k * 256 + 192 + half * 64],
                   qcn[:, k, t0:t0 + w], k == 0, k == 7, reads=[B_wq[hi], B_qcnS],
                   writes=[PB[pb]] if k == 0 else (), parts=() if k == 0 else [PB[pb]], signal=(k == 7))
        kb.op(act, lambda: S.copy(qn[qi][:, 0:w], ps[0][:, 0:w]), reads=[PB[0]], writes=[B_qn[qi]])
        kb.op(dve, lambda: V.tensor_tensor(qtmp[:, 0:w], ps[1][0:64, 0:w], rq[:, 0, t0:t0 + w], ALU.mult),
              reads=[PB[1], B_rq], writes=[B_qtmp])
        kb.op(dve, lambda: V.tensor_tensor(qtmp2[:, 0:w], ps[2][0:64, 0:w], rq[:, 1, t0:t0 + w], ALU.mult),
              reads=[PB[2], B_rq], writes=[B_qtmp2])
        kb.op(dve, lambda: V.tensor_tensor(qr[qi][:, 0:w], qtmp[:, 0:w], qtmp2[:, 0:w], ALU.add),
              reads=[B_qtmp, B_qtmp2], writes=[B_qr[qi]])

    if work:
        load_head(0)
    for m in range(KT):
        kb.dma(pool, wout_b[m].rearrange("p (a b) -> (p a) b", b=2048),
               I.wout[m].rearrange("p (a b) -> (p a) b", b=2048), B_wcast, B_in)
    for ft in range(FT):
        kb.dma(pool, wffg_b[ft].rearrange("p (a b) -> (p a) b", b=2048),
               I.wffg[ft].rearrange("p (a b) -> (p a) b", b=2048), B_wcast, B_in)
        kb.dma(pool, wffu_b[ft].rearrange("p (a b) -> (p a) b", b=2048),
               I.wffu[ft].rearrange("p (a b) -> (p a) b", b=2048), B_wcast, B_in)
    for m in range(KT):
        kb.dma(pool, wdn_b[m].rearrange("p (a b) -> (p a) b", b=1376),
               I.wdn[m].rearrange("p (a b) -> (p a) b", b=1376), B_wcast, B_in)
    if work:
        emit_qproj(0)
    pctr = 0
    ob = 3
    for idx, (h, t0, w) in enumerate(work):
        hi, qi = h % 2, idx % 2
        if t0 == 0 and h + 1 < NH:
            load_head(h + 1)

        def score(kt):
            sbk = 4 + (kt % 3)
            mm(ps[sbk][:, 0:w], kth[hi][:, kt * 128:(kt + 1) * 128], qn[qi][:, 0:w], True, False,
               reads=[B_kth[hi], B_qn[qi]], writes=[PB[sbk]])
            mm(ps[sbk][:, 0:w], krT[:, kt * 128:(kt + 1) * 128], qr[qi][:, 0:w], False, True,
               reads=[B_krT, B_qr[qi]], parts=[PB[sbk]], signal=True)

        score(0)
        score(1)
        for kt in range(34):
            sbk = 4 + (kt % 3)
            pi = pctr % NPS
            pctr += 1
            kb.op(act, lambda: S.activation(pT[pi][:, 0:w], ps[sbk][:, 0:w], AF.Exp, scale=MLA_SCALE),
                  reads=[PB[sbk]], writes=[B_pT[pi]])
            if kt == 0:
                kb.op(dve, lambda: V.tensor_copy(acc[:, 0:w], pT[pi][:, 0:w]), reads=[B_pT[pi]], writes=[B_acc])
            else:
                kb.op(dve, lambda: V.tensor_tensor(acc[:, 0:w], acc[:, 0:w], pT[pi][:, 0:w], ALU.add),
                      reads=[B_pT[pi], B_acc], parts=[B_acc])
            if kt + 2 < 34:
                score(kt + 2)
            if kt == 12 and idx + 1 < len(work):
                emit_qproj(idx + 1)
            mm(ps[ob][:, 0:w], vh[hi][:, kt, :], pT[pi][:, 0:w], kt == 0, kt == 33,
               reads=[B_vh[hi], B_pT[pi]], writes=[PB[ob]] if kt == 0 else (), parts=() if kt == 0 else [PB[ob]],
               signal=(kt == 33))
        mm(ps[7][:, 0:w], onesf[:], acc[:, 0:w], True, True, reads=[B_c, B_acc], writes=[PB[7]], signal=True)
        kb.op(dve, lambda: V.reciprocal(rinv[:, 0:w], ps[7][:, 0:w]), reads=[PB[7]], writes=[B_rinv])
        ai = idx % 2
        kb.op(dve, lambda: V.tensor_tensor(ast[ai][:, 0:w], ps[ob][:, 0:w], rinv[:, 0:w], ALU.mult),
              reads=[PB[ob], B_rinv], writes=[B_ast[ai]])
        kb.dma(sp, attT[h * 128:(h + 1) * 128, t0:t0 + w], ast[ai][:, 0:w], B_att, B_ast[ai])
    kb.barrier()
    st.close()
    kvst.close()
    if STOP_AFTER == "p4":
        return finish(nc, es, kb, out, B_out)

    st = contextlib.ExitStack()
    mixin = sb(st, "mixin", [128, KT, NT], BF16)
    B_mixin = Buf("mixin")
    mixT = sb(st, "mixT", [128, KT, NT], F32)
    B_mixT = Buf("mixT")
    xT = sb(st, "xT", [128, KT, NT], F32)
    B_xT = Buf("xT")
    xch = [sb(st, f"xch{i}", [128, D], F32) for i in range(2)]
    XB = [Buf(f"xch{i}") for i in range(2)]
    NWS = 3
    wslot = [sb(st, f"wslot{i}", [128, KT * 128], BF16) for i in range(NWS)]
    WB = [Buf(f"wslot{i}") for i in range(NWS)]
    sqb = [sb(st, f"sqb{i}", [128, NT], BF16) for i in range(2)]
    SQB = [Buf("sqb0"), Buf("sqb1")]
    rstd = sb(st, "rstd", [128, NT], F32)
    B_rstd = Buf("rstd")
    tmpf = [sb(st, f"tmpf{i}", [128, NT], F32) for i in range(2)]
    B_tmpf = [Buf("tmpf0"), Buf("tmpf1")]
    hst = [sb(st, f"hst{i}", [128, NT], BF16) for i in range(2)]
    B_hst = [Buf("hst0"), Buf("hst1")]
    wctr[0] = 0

    def load_w5(src_ap):
        i = wctr[0] % NWS
        wctr[0] += 1
        kb.dma(pool, wslot[i][:], src_ap, WB[i], B_wcast)
        return wslot[i], WB[i]

    def rstd_from(psb, w, nfeat):
        kb.op(act, lambda: S.activation(rstd[:, 0:w], ps[psb][:, 0:w], AF.Sqrt, bias=EPS, scale=1.0 / nfeat),
              reads=[PB[psb]], writes=[B_rstd])
        kb.op(dve, lambda: V.reciprocal(rstd[:, 0:w], rstd[:, 0:w]), reads=[B_rstd], parts=[B_rstd])

    xctr[0] = 0
    sq5 = [0]
    for (t0, w) in ([] if "p5a" in SKIP else OWN_TILES[:P5_LIMIT]):
        for k in range(KT):
            src = s5outT[k * 128:(k + 1) * 128, t0:t0 + w] if k < 8 else attT[(k - 8) * 128:(k - 7) * 128, t0:t0 + w]
            kb.dma(sp, mixin[:, k, 0:w], src, B_mixin, B_s5o if k < 8 else B_att, part=(k > 0))
        c0 = 0
        while c0 < w:
            cw = min(128, w - c0)
            xi = xctr[0] % 2
            xctr[0] += 1
            kb.dma(sp, xch[xi][0:cw, :], I.xl[t0 + c0:t0 + c0 + cw, :], XB[xi], B_in)
            for k4 in range(8):
                pb = k4 % 2
                for j in range(4):
                    k = k4 * 4 + j
                    kb.op(pe, lambda: T.transpose(ps[pb][:, j * 128:j * 128 + cw],
                                                  xch[xi][0:cw, k * 128:(k + 1) * 128], ident[0:cw, 0:cw]),
                          reads=[XB[xi], B_c], writes=[PB[pb]] if j == 0 else (), parts=() if j == 0 else [PB[pb]],
                          signal=(j == 3))
                for j in range(4):
                    k = k4 * 4 + j
                    if pb == 0:
                        kb.op(dve, lambda: V.tensor_copy(xT[:, k, c0:c0 + cw], ps[pb][:, j * 128:j * 128 + cw]),
                              reads=[PB[pb]], parts=[B_xT])
                    else:
                        kb.op(act, lambda: S.copy(xT[:, k, c0:c0 + cw], ps[pb][:, j * 128:j * 128 + cw]),
                              reads=[PB[pb]], parts=[B_xT])
            c0 += cw
        for m in range(KT):
            wt, wb = load_w5(wout_b[m])
            pb = 2 + (m % 2)
            for k in range(KT):
                mm(ps[pb][:, 0:w], wt[:, k * 128:(k + 1) * 128], mixin[:, k, 0:w], k == 0, k == KT - 1,
                   reads=[wb, B_mixin], writes=[PB[pb]] if k == 0 else (), parts=() if k == 0 else [PB[pb]],
                   signal=(k == KT - 1))
            kb.op(dve, lambda: V.tensor_copy(mixT[:, m, 0:w], ps[pb][:, 0:w]), reads=[PB[pb]], parts=[B_mixT])
            si = sq5[0] % 2
            sq5[0] += 1
            kb.op(act, lambda: S.activation(sqb[si][:, 0:w], mixT[:, m, 0:w], AF.Square), reads=[B_mixT],
                  writes=[SQB[si]])
            mm(ps[4][:, 0:w], onesb[:], sqb[si][:, 0:w], m == 0, m == KT - 1, reads=[SQB[si], B_c],
               writes=[PB[4]] if m == 0 else (), parts=() if m == 0 else [PB[4]], signal=True)
        rstd_from(4, w, float(D))
        for m in range(KT):
            ti = m % 2
            kb.op(dve, lambda: V.tensor_tensor(tmpf[ti][:, 0:w], mixT[:, m, 0:w], rstd[:, 0:w], ALU.mult),
                  reads=[B_mixT, B_rstd], writes=[B_tmpf[ti]])
            kb.op(dve, lambda: V.scalar_tensor_tensor(xT[:, m, 0:w], tmpf[ti][:, 0:w], vecs[:, 4, m:m + 1],
                                                      xT[:, m, 0:w], ALU.mult, ALU.add),
                  reads=[B_tmpf[ti], B_vecs, B_xT], parts=[B_xT])
            kb.dma(sp, xmidT[m * 128:(m + 1) * 128, t0:t0 + w], xT[:, m, 0:w], B_xmid, B_xT)
            si = sq5[0] % 2
            sq5[0] += 1
            kb.op(act, lambda: S.activation(sqb[si][:, 0:w], xT[:, m, 0:w], AF.Square), reads=[B_xT],
                  writes=[SQB[si]])
            mm(ps[5][:, 0:w], onesb[:], sqb[si][:, 0:w], m == 0, m == KT - 1, reads=[SQB[si], B_c],
               writes=[PB[5]] if m == 0 else (), parts=() if m == 0 else [PB[5]], signal=True)
        rstd_from(5, w, float(D))
        for m in range(KT):
            ti = m % 2
            kb.op(dve, lambda: V.tensor_tensor(tmpf[ti][:, 0:w], xT[:, m, 0:w], rstd[:, 0:w], ALU.mult),
                  reads=[B_xT, B_rstd], writes=[B_tmpf[ti]])
            kb.op(dve, lambda: V.tensor_scalar(hst[ti][:, 0:w], tmpf[ti][:, 0:w], vecs[:, 5, m:m + 1],
                                               vecs[:, 6, m:m + 1], ALU.mult, ALU.add),
                  reads=[B_tmpf[ti], B_vecs], writes=[B_hst[ti]])
            kb.dma(sp, hxT[m * 128:(m + 1) * 128, t0:t0 + w], hst[ti][:, 0:w], B_hx, B_hst[ti])
    kb.barrier()
    st.close()
    if STOP_AFTER == "p5a":
        return finish(nc, es, kb, out, B_out)

    st = contextlib.ExitStack()
    FT_TILES = [(0, 410), (410, 410), (820, 410), (1230, 410), (1640, 408)]
    hx = sb(st, "hx", [128, KT, NT + 2], BF16)
    B_hxs = Buf("hxs")
    actT = sb(st, "actT", [128, FT, NT], BF16)
    B_act = Buf("actT")
    NWS = 8
    wpool = sb(st, "wpool", [128, 33024], BF16)
    wslot = [wpool[:, i * 4096:(i + 1) * 4096] for i in range(NWS)]
    WB = [Buf(f"wslot{i}") for i in range(NWS)]
    dslot = [wpool[:, j * 11008:(j + 1) * 11008] for j in range(3)]
    DB = [Buf(f"dslot{j}") for j in range(3)]

    def fence(q, bufs):
        for b_ in bufs:
            q.wait_all(b_.r)
            q.wait_all(b_.w)
    cws = sb(st, "cws", [128, FT, 4], F32)
    B_cw = Buf("cw")
    kb.dma(sp, cws[:], I.convw, B_cw, B_in)
    cv = [sb(st, f"cv{i}", [128, NT], F32) for i in range(2)]
    B_cv = [Buf("cv0"), Buf("cv1")]
    sg = [sb(st, f"sg{i}", [128, NT], F32) for i in range(2)]
    B_sg = [Buf("sg0"), Buf("sg1")]
    sqb = [sb(st, f"sqb{i}", [128, NT], BF16) for i in range(2)]
    SQB = [Buf("sqb0"), Buf("sqb1")]
    rstds = [sb(st, f"rstd{i}", [128, NT], F32) for i in range(2)]
    B_rstds = [Buf("rstd0"), Buf("rstd1")]
    fst = [sb(st, f"fst{i}", [128, NT], F32) for i in range(2)]
    B_fst = [Buf("fst0"), Buf("fst1")]
    xm4 = sb(st, "xm4", [128, 4, NT], F32)
    B_xm4 = Buf("xm4")
    f4 = sb(st, "f4", [128, 4, NT], F32)
    B_f4 = Buf("f4")
    ost = [sb(st, f"ost{i}", [128, 512], F32) for i in range(2)]
    B_ost = [Buf("ost0"), Buf("ost1")]
    wctr[0] = 0
    dctr = 0
    sqc = 0
    octr = 0
    pending_out = []
    def out_step(ti5, t0, w, mg, rsel):
        nonlocal octr
        rstd = rstds[rsel]
        B_rstd = B_rstds[rsel]
        if True:
            kb.dma(sp, f4[:, :, 0:w], fTs[ti5, mg * 512:(mg + 1) * 512, 0:w].rearrange("(a p) t -> p a t", p=128),
                   B_f4, B_f)
            kb.dma(sp, xm4[:, :, 0:w], xmidT[mg * 512:(mg + 1) * 512, t0:t0 + w].rearrange("(a p) t -> p a t", p=128),
                   B_xm4, B_xmid)
            for a in range(4):
                m = mg * 4 + a
                kb.op(dve, lambda: V.tensor_tensor(f4[:, a, 0:w], f4[:, a, 0:w], rstd[:, 0:w], ALU.mult),
                      reads=[B_f4, B_rstd], parts=[B_f4])
                kb.op(dve, lambda: V.scalar_tensor_tensor(f4[:, a, 0:w], f4[:, a, 0:w], vecs[:, 7, m:m + 1],
                                                          xm4[:, a, 0:w], ALU.mult, ALU.add),
                      reads=[B_f4, B_xm4, B_vecs], parts=[B_f4])
            c0 = 0
            while c0 < w:
                cw = min(128, w - c0)
                for a in range(4):
                    kb.op(pe, lambda: T.transpose(ps[7][0:cw, a * 128:(a + 1) * 128], f4[:, a, c0:c0 + cw],
                                                  ident[:, :]),
                          reads=[B_f4, B_c], writes=[PB[7]] if a == 0 else (), parts=() if a == 0 else [PB[7]],
                          signal=(a == 3))
                oi = octr % 2
                octr += 1
                kb.op(act, lambda: S.copy(ost[oi][0:cw, :], ps[7][0:cw, :]), reads=[PB[7]], writes=[B_ost[oi]])
                kb.dma(sp, out[t0 + c0:t0 + c0 + cw, mg * 512:(mg + 1) * 512], ost[oi][0:cw, :], B_out, B_ost[oi])
                c0 += cw

    for ti5, (t0, w) in enumerate(FT_TILES):
        rsel = ti5 % 2
        rstd = rstds[rsel]
        B_rstd = B_rstds[rsel]
        if t0 == 0:
            kb.op(dve, lambda: V.memset(hx[:, :, 0:1], 0.0), writes=[B_hxs])
            for k in range(KT):
                kb.dma(sp, hx[:, k, 1:w + 2], hxT[k * 128:(k + 1) * 128, 0:w + 1], B_hxs, B_hx, part=True)
        else:
            for k in range(KT):
                kb.dma(sp, hx[:, k, 0:w + 2], hxT[k * 128:(k + 1) * 128, t0 - 1:t0 + w + 1], B_hxs, B_hx,
                       part=(k > 0))
        for ft in range(FT):
            if ft % 10 == 5 and pending_out:
                out_step(*pending_out.pop(0))
            if ft == 0:
                fence(pool, DB)
            i = wctr[0] % NWS
            wctr[0] += 2
            kb.dma(pool, wslot[i], wffg_b[ft], WB[i], B_wcast)
            kb.dma(pool, wslot[i + 1], wffu_b[ft], WB[i + 1], B_wcast)
            gb, ub = (0, 1) if ft % 2 == 0 else (2, 3)
            for k in range(KT):
                mm(ps[gb][:, 0:w + 2], wslot[i][:, k * 128:(k + 1) * 128], hx[:, k, 0:w + 2], k == 0, k == KT - 1,
                   reads=[WB[i], B_hxs], writes=[PB[gb]] if k == 0 else (), parts=() if k == 0 else [PB[gb]],
                   signal=(k == KT - 1))
            for k in range(KT):
                mm(ps[ub][:, 0:w], wslot[i + 1][:, k * 128:(k + 1) * 128], hx[:, k, 1:w + 1], k == 0, k == KT - 1,
                   reads=[WB[i + 1], B_hxs], writes=[PB[ub]] if k == 0 else (), parts=() if k == 0 else [PB[ub]],
                   signal=(k == KT - 1))
            ci = ft % 2
            kb.op(dve, lambda: V.tensor_scalar(cv[ci][:, 0:w], ps[gb][:, 1:w + 1], cws[:, ft, 1:2], cws[:, ft, 3:4],
                                               ALU.mult, ALU.add), reads=[PB[gb], B_cw], writes=[B_cv[ci]])
            kb.op(dve, lambda: V.scalar_tensor_tensor(cv[ci][:, 0:w], ps[gb][:, 0:w], cws[:, ft, 0:1],
                                                      cv[ci][:, 0:w], ALU.mult, ALU.add),
                  reads=[PB[gb], B_cw, B_cv[ci]], parts=[B_cv[ci]])
            kb.op(dve, lambda: V.scalar_tensor_tensor(cv[ci][:, 0:w], ps[gb][:, 2:w + 2], cws[:, ft, 2:3],
                                                      cv[ci][:, 0:w], ALU.mult, ALU.add),
                  reads=[PB[gb], B_cw, B_cv[ci]], parts=[B_cv[ci]])
            kb.op(act, lambda: S.activation(sg[ci][:, 0:w], cv[ci][:, 0:w], AF.Silu), reads=[B_cv[ci]],
                  writes=[B_sg[ci]])
            kb.op(dve, lambda: V.tensor_tensor(actT[:, ft, 0:w], sg[ci][:, 0:w], ps[ub][:, 0:w], ALU.mult),
                  reads=[B_sg[ci], PB[ub]], parts=[B_act])
        for m in range(KT):
            di = dctr % 3
            dctr += 1
            if m == 0:
                fence(pool, WB)
            kb.dma(pool, dslot[di], wdn_b[m], DB[di], B_wcast)
            pb = 4 + (m % 2)
            for k in range(FT):
                mm(ps[pb][:, 0:w], dslot[di][:, k * 128:(k + 1) * 128], actT[:, k, 0:w], k == 0, k == FT - 1,
                   reads=[DB[di], B_act], writes=[PB[pb]] if k == 0 else (), parts=() if k == 0 else [PB[pb]],
                   signal=(k == FT - 1))
            fi = m % 2
            kb.op(dve, lambda: V.tensor_copy(fst[fi][:, 0:w], ps[pb][:, 0:w]), reads=[PB[pb]], writes=[B_fst[fi]])
            kb.dma(sp, fTs[ti5, m * 128:(m + 1) * 128, 0:w], fst[fi][:, 0:w], B_f, B_fst[fi])
            si = sqc % 2
            sqc += 1
            kb.op(act, lambda: S.activation(sqb[si][:, 0:w], fst[fi][:, 0:w], AF.Square), reads=[B_fst[fi]],
                  writes=[SQB[si]])
            mm(ps[6][:, 0:w], onesb[:], sqb[si][:, 0:w], m == 0, m == KT - 1, reads=[SQB[si], B_c],
               writes=[PB[6]] if m == 0 else (), parts=() if m == 0 else [PB[6]], signal=True)
        kb.op(act, lambda: S.activation(rstd[:, 0:w], ps[6][:, 0:w], AF.Sqrt, bias=EPS, scale=1.0 / D),
              reads=[PB[6]], writes=[B_rstd])
        kb.op(dve, lambda: V.reciprocal(rstd[:, 0:w], rstd[:, 0:w]), reads=[B_rstd], parts=[B_rstd])
        pending_out.extend([(ti5, t0, w, mg, rsel) for mg in range(8)])
    while pending_out:
        out_step(*pending_out.pop(0))
    kb.barrier()
    st.close()
    return finish(nc, es, kb, out, B_out)


_EXTRA = []


def finish(nc, es, kb, out, B_out):
    kb.barrier()
    for s_ in _EXTRA:
        s_.close()
    es.close()
    return nc


def _cols(v):
    return np.ascontiguousarray(v.reshape(-1, 128).T)


def _wtiles(w):
    K, M = w.shape
    return np.ascontiguousarray(w.reshape(K // 128, 128, M // 128, 128).transpose(2, 1, 0, 3)).reshape(
        M // 128, 128, (K // 128) * 128)


def _rope_tables(tpos, is_x):
    inv = (10000.0 ** (-np.arange(16, dtype=np.float32) / 16)).astype(np.float32)
    t = tpos.astype(np.float32)
    row = np.floor(t / 64).astype(np.float32)
    col = (t - row * 64).astype(np.float32)
    ang = np.concatenate([row[:, None] * inv, col[:, None] * inv], axis=-1).astype(np.float32)
    cos, sin = np.cos(ang).astype(np.float32), np.sin(ang).astype(np.float32)
    cos = np.where(is_x[:, None], cos, 1.0).astype(np.float32)
    sin = np.where(is_x[:, None], sin, 0.0).astype(np.float32)
    cc = np.concatenate([cos.T, cos.T], axis=0)
    ss = np.concatenate([-sin.T, sin.T], axis=0)
    return np.ascontiguousarray(np.stack([cc, ss], axis=1)).astype(np.float32)


def _prep_shared(inp):
    sh = {}
    sh["wada"] = _wtiles(inp["w_ada"][0])
    sh["bada"] = _cols(inp["b_ada"][0])
    sh["gvec"] = np.ascontiguousarray(np.stack([_cols(inp[k][0]) for k in
                                                ("g_pre_mix", "g_post_mix", "g_pre_ffn", "g_post_ffn")], axis=1))
    w_in = inp["w_in"][0]
    kr = w_in[:, 2560:2624]
    ev, od = kr[:, 0::2], kr[:, 1::2]
    sh["win"] = _wtiles(np.concatenate([w_in[:, :2560], ev, od, od, ev], axis=1))
    sh["gq"] = _cols(inp["mla_g_q"][0])
    sh["gkv"] = _cols(inp["mla_g_kv"][0])
    wq = inp["mla_w_uq"][0].reshape(1024, NH, 192)
    nope, rp = wq[:, :, :128], wq[:, :, 128:]
    ev, od = rp[:, :, 0::2], rp[:, :, 1::2]
    wqp = np.concatenate([nope, ev, od, od, ev], axis=2)
    sh["wuq"] = np.ascontiguousarray(wqp.reshape(8, 128, NH, 256).transpose(2, 1, 0, 3)).reshape(NH, 128, 8 * 256)
    wkv = inp["mla_w_ukv"][0].reshape(512, NH, 256)
    wk = wkv[:, :, :128].reshape(4, 128, NH * 128)
    wv = wkv[:, :, 128:].reshape(4, 128, NH * 128)
    sh["wuk"] = np.ascontiguousarray(wk.transpose(1, 0, 2)).reshape(128, 4 * 3072)
    sh["wuv"] = np.ascontiguousarray(wv.transpose(1, 0, 2)).reshape(128, 4 * 3072)
    sh["wout"] = _wtiles(inp["w_out"][0])
    sh["wglu"] = _wtiles(inp["s5_w_glu"][0])
    fw = inp["ffn_w_in"][0]
    sh["wffg"] = _wtiles(fw[:, :DFF])
    sh["wffu"] = _wtiles(fw[:, DFF:])
    sh["wdn"] = _wtiles(inp["ffn_w_down"][0])
    sh["ident"] = np.eye(128, dtype=np.float32)
    return sh


def _prep_core(inp, sh, b, half):
    m = dict(sh)
    x = inp["x"][b]
    ctx = inp["ctx"][b]
    if half == 1:
        x = x[::-1]
        ctx = ctx[::-1]
    m["xl"] = np.ascontiguousarray(x)
    m["ctxl"] = np.ascontiguousarray(ctx)
    m["ccol"] = np.ascontiguousarray(np.stack([_cols(inp["c"][b]), _cols(inp["c_ctx"])], axis=-1))
    cw = inp["ffn_conv_w"][0]
    if half == 1:
        cw = cw[::-1]
    m["convw"] = np.ascontiguousarray(np.stack([_cols(cw[0]), _cols(cw[1]), _cols(cw[2]),
                                                _cols(inp["ffn_conv_b"][0])], axis=-1))
    loc = np.arange(SEQ)
    tpos = loc if half == 0 else (SEQ - 1 - loc)
    kp = np.concatenate([tpos, np.zeros(CTX, dtype=tpos.dtype)])
    isx = np.concatenate([np.ones(SEQ, bool), np.zeros(CTX, bool)])
    m["ropek"] = _rope_tables(kp, isx)
    m["ropeq"] = _rope_tables(tpos[:NOWN], np.ones(NOWN, bool))
    dsel = [0, 1] if half == 0 else [1, 0]

    def gp(a):
        a = a[dsel].reshape(2, 32, 2, 64)
        return a.transpose(2, 3, 0, 1).reshape(128, 64)

    m["s5lam"] = np.ascontiguousarray(np.stack([gp(inp["s5_lambda_re"][0]), gp(inp["s5_lambda_im"][0])], axis=1))
    ls = np.broadcast_to(inp["s5_log_step"][0][:, :, None], (2, 64, 64))
    m["s5ls"] = np.ascontiguousarray(gp(ls))

    def gb(ar, ai):
        a = np.stack([ar, ai], axis=-2)[dsel]
        a = a.reshape(2, 32, 2, 64, 2, 16)
        return np.ascontiguousarray(a.transpose(2, 3, 0, 1, 4, 5)).reshape(128, 64, 2, 16)

    m["s5b"] = gb(inp["s5_b_re"][0], inp["s5_b_im"][0])
    m["s5c"] = gb(inp["s5_c_re"][0].transpose(0, 1, 3, 2), inp["s5_c_im"][0].transpose(0, 1, 3, 2))
    m["s5d"] = np.ascontiguousarray(inp["s5_d"][0].reshape(32, 2, 16).transpose(1, 2, 0)).reshape(32, 32)
    return {k: np.ascontiguousarray(v, dtype=np.float32) for k, v in m.items()}


def kernel(**inputs):
    inp = {k: np.asarray(v) for k, v in inputs.items()}
    sh = _prep_shared(inp)
    in_maps = [_prep_core(inp, sh, c // 2, c % 2) for c in range(8)]
    nc = _build()
    in_maps = [{k: v for k, v in m.items() if k in nc._declared_inputs} for m in in_maps]
    res = run_bass_kernel_spmd(nc, in_maps, core_ids=list(range(8)))
    outp = np.empty((4, SEQ, D), dtype=np.float32)
    for c in range(8):
        o = res.results[c]["out"]
        b, half = c // 2, c % 2
        if half == 0:
            outp[b, :2048] = o
        else:
            outp[b, 2048:] = o[::-1]
    return outp
```

```python
import contextlib
import numpy as np
import concourse.bass as bass
import concourse.mybir as mybir
from concourse.bass_utils import run_bass_kernel_spmd

F32 = mybir.dt.float32
BF16 = mybir.dt.bfloat16
AF = mybir.ActivationFunctionType
ALU = mybir.AluOpType

D = 4096
KT = 32
SEQ = 4096
CTX = 256
NKEY = SEQ + CTX
NOWN = 2050
NT = 410
OWN_TILES = [(i * NT, NT) for i in range(5)]
REST_TILES = [(2050, 510), (2560, 512), (3072, 512), (3584, 512)]
UCOLS = CTX + SEQ + CTX
NH = 24
DFF = 11008
FT = 86
EPS = 1e-6
MLA_SCALE = 192.0 ** -0.5
NBA = 32 + 257
NBB = 544
NYB = 257
STOP_AFTER = None
DEBUG = False
P1_LIMIT = None
P1_STAGE = 0
SKIP = set()
P5_LIMIT = None


class Tok:
    __slots__ = ("sem", "val", "key")

    def __init__(self, sem, val, key):
        self.sem, self.val, self.key = sem, val, key


class Buf:
    __slots__ = ("w", "r", "dsem", "dcnt", "dkey", "last_dma", "name", "dram", "bg")

    def __init__(self, name="", dram=False):
        self.w, self.r = {}, {}
        self.dsem = None
        self.dcnt = 0
        self.last_dma = None
        self.name = name
        self.dram = dram
        self.bg = False

    @staticmethod
    def _add(d, tok):
        o = d.get(tok.key)
        if o is None or o.val < tok.val:
            d[tok.key] = tok


class Eng:
    def __init__(self, kb, h, name, is_pe=False, compute=True):
        self.kb, self.h, self.name, self.is_pe, self.compute = kb, h, name, is_pe, compute
        self.sem = kb.new_sem("e_" + name)
        self.key = "e_" + name
        self.cnt = 0
        self.waited = {}
        self.pending = False

    def wait(self, tok):
        if tok.key == self.key:
            if self.is_pe:
                return
        if self.waited.get(tok.key, 0) >= tok.val:
            return
        self.h.wait_ge(tok.sem, tok.val)
        self.waited[tok.key] = tok.val

    def wait_all(self, d):
        for t in list(d.values()):
            self.wait(t)

    def mark(self, inst, signal):
        if signal:
            self.cnt += 1
            inst.then_inc(self.sem, 1)
            self.pending = False
            return Tok(self.sem, self.cnt, self.key)
        self.pending = True
        return Tok(self.sem, self.cnt + 1, self.key)


class KB:
    def __init__(self, nc, es):
        self.nc, self.es = nc, es
        self.nsem = 0
        self.pe = Eng(self, nc.tensor, "pe", is_pe=True)
        self.dve = Eng(self, nc.vector, "dve")
        self.act = Eng(self, nc.scalar, "act")
        self.pool = Eng(self, nc.gpsimd, "pool")
        self.sp = Eng(self, nc.sync, "sp", compute=False)
        self.engs = [self.pe, self.dve, self.act, self.pool, self.sp]
        self.dbufs = []
        self.retired = []

    def new_sem(self, name):
        self.nsem += 1
        return self.es.enter_context(self.nc.semaphore(f"{name}_{self.nsem}"))

    def op(self, eng, build, reads=(), writes=(), parts=(), signal=True):
        for b in reads:
            eng.wait_all(b.w)
        for b in writes:
            eng.wait_all(b.r)
            eng.wait_all(b.w)
        for b in parts:
            eng.wait_all(b.r)
        inst = build()
        tok = eng.mark(inst, signal)
        for b in reads:
            if not b.dram:
                Buf._add(b.r, tok)
        for b in writes:
            b.w = {tok.key: tok}
            b.r = {}
        for b in parts:
            Buf._add(b.w, tok)
        return tok

    def dma(self, q, out_ap, in_ap, out_buf, in_buf, part=False, **kw):
        sb = out_buf if not out_buf.dram else (in_buf if not in_buf.dram else out_buf)
        if sb.dsem is not None and sb.dcnt >= 30000:
            if not sb.bg:
                self.retired.append(Tok(sb.dsem, sb.dcnt, sb.dkey))
            sb.dsem = self.new_sem("d")
            sb.dkey = f"d{self.nsem}"
            sb.dcnt = 0
        if sb.dsem is None:
            sb.dsem = self.new_sem("d")
            sb.dkey = f"d{self.nsem}"
            self.dbufs.append(sb)
        if sb.last_dma is not None and not sb.dram:
            q.wait(sb.last_dma)
        q.wait_all(in_buf.w)
        if not out_buf.dram:
            q.wait_all(out_buf.r)
            if not part:
                q.wait_all(out_buf.w)
        inst = q.h.dma_start(out=out_ap, in_=in_ap, **kw)
        sb.dcnt += 16
        inst.then_inc(sb.dsem, 16)
        tok = Tok(sb.dsem, sb.dcnt, sb.dkey)
        sb.last_dma = tok
        if not in_buf.dram:
            Buf._add(in_buf.r, tok)
        if out_buf.dram or part:
            Buf._add(out_buf.w, tok)
        else:
            out_buf.w = {tok.key: tok}
            out_buf.r = {}
        return tok

    def barrier(self):
        assert not self.pe.pending
        toks = [Tok(e.sem, e.cnt, e.key) for e in self.engs if e.compute and e.cnt > 0]
        toks += [Tok(b.dsem, b.dcnt, b.dkey) for b in self.dbufs if b.dcnt > 0 and not b.bg]
        toks += self.retired
        for e in self.engs:
            for t in toks:
                e.wait(t)


def _build(debug_out=None):
    nc = bass.Bass("TRN2", target_bir_lowering=False)
    es = contextlib.ExitStack()
    kb = KB(nc, es)
    pe, dve, act, pool, sp = kb.pe, kb.dve, kb.act, kb.pool, kb.sp
    V, S, T, G = nc.vector, nc.scalar, nc.tensor, nc.gpsimd

    def din(name, shape, dt=F32):
        return nc.dram_tensor(name, list(shape), dt, kind="ExternalInput").ap()

    dbg = debug_out or ()

    def dscr(name, shape, dt):
        kind = "ExternalOutput" if name in dbg else "Internal"
        return nc.dram_tensor(name, list(shape), dt, kind=kind).ap()

    IN_SHAPES = dict(xl=[SEQ, D], ctxl=[CTX, D], ccol=[128, KT, 2], wada=[192, 128, KT * 128], bada=[128, 192],
                     gvec=[128, 4, KT], win=[21, 128, KT * 128], gq=[128, 8], gkv=[128, 4],
                     wuq=[NH, 128, 8 * 256], wuk=[128, 4 * 3072], wuv=[128, 4 * 3072], wout=[KT, 128, KT * 128],
                     wglu=[8, 128, 8 * 128], wffg=[FT, 128, KT * 128], wffu=[FT, 128, KT * 128],
                     wdn=[KT, 128, FT * 128], convw=[128, FT, 4], ropeq=[64, 2, NOWN], ropek=[64, 2, NKEY],
                     ident=[128, 128], s5lam=[128, 2, 64], s5ls=[128, 64], s5b=[128, 64, 2, 16],
                     s5c=[128, 64, 2, 16], s5d=[32, 32])
    declared = {}

    class _In:
        def __getattr__(self, name):
            if name not in declared:
                declared[name] = din(name, IN_SHAPES[name])
            return declared[name]

    I = _In()
    nc._declared_inputs = declared
    out = nc.dram_tensor("out", [2048, D], F32, kind="ExternalOutput").ap()

    uT = dscr("uT", [1024, UCOLS], BF16)
    qcnTs = dscr("qcnTs", [1024, NOWN], BF16)
    KTs = dscr("KTs", [NH, 128, NKEY], BF16)
    Vs = dscr("Vs", [34, 128, 3072], BF16)
    yactT = dscr("yactT", [1024, 2056], BF16)
    s5outT = dscr("s5outT", [1024, NOWN], BF16)
    attT = dscr("attT", [3072, NOWN], BF16)
    xmidT = dscr("xmidT", [D, NOWN], F32)
    hxT = dscr("hxT", [D, NOWN], BF16)
    fTs = dscr("fTs", [5, D, NT], F32)
    B_uT, B_qcn, B_KT, B_V, B_yact, B_s5o, B_att, B_xmid, B_hx, B_f = [Buf(n, dram=True) for n in
                                                                       "uT qcn KT V yact s5o att xmid hx f".split()]
    wffg_b = dscr("wffg_b", [FT, 128, KT * 128], BF16)
    wffu_b = dscr("wffu_b", [FT, 128, KT * 128], BF16)
    wdn_b = dscr("wdn_b", [KT, 128, FT * 128], BF16)
    wout_b = dscr("wout_b", [KT, 128, KT * 128], BF16)
    B_wcast = Buf("wcast", dram=True)
    B_wcast.bg = True
    B_in = Buf("inputs", dram=True)
    B_out = Buf("out", dram=True)

    sbn = [0]

    def sb(st, name, shape, dt):
        sbn[0] += 1
        return st.enter_context(nc.sbuf_tensor(f"s{sbn[0]}_{name}", list(shape), dt))

    ps = [es.enter_context(nc.psum_tensor(f"ps{i}", [128, 512], F32)) for i in range(8)]
    PB = [Buf(f"ps{i}") for i in range(8)]

    ident = sb(es, "ident", [128, 128], F32)
    identb = sb(es, "identb", [128, 128], BF16)
    onesb = sb(es, "onesb", [128, 128], BF16)
    onesf = sb(es, "onesf", [128, 128], F32)
    modc = sb(es, "modc", [128, 192, 2], F32)
    gv = sb(es, "gv", [128, 4, KT], F32)
    vecs = sb(es, "vecs", [128, 8, KT], F32)
    gqs = sb(es, "gqs", [128, 8], F32)
    gkvs = sb(es, "gkvs", [128, 4], F32)
    sc2 = sb(es, "sc2", [128, KT, 2], BF16)
    badas = sb(es, "badas", [128, 192], F32)
    B_c = Buf("consts")
    B_mod = Buf("mod")
    B_vecs = Buf("vecs")
    B_krT = Buf("krT")
    B_kvcn = Buf("kvcn")
    ccs = sb(es, "ccs", [128, KT, 2], F32)
    kvst = contextlib.ExitStack()
    _EXTRA.clear()
    _EXTRA.append(kvst)
    krT = sb(kvst, "krT", [64, NKEY], BF16)
    kvcnT = sb(kvst, "kvcnT", [128, 4, NKEY], BF16)

    def mm(o, l, r, start, stop, reads, writes=(), parts=(), signal=False):
        return kb.op(pe, lambda: T.matmul(o, l, r, start=start, stop=stop), reads=reads, writes=writes,
                     parts=parts, signal=signal)

    kb.dma(sp, ident[:], I.ident, B_c, B_in)
    kb.dma(sp, gv[:], I.gvec, B_c, B_in, part=True)
    kb.dma(sp, gqs[:], I.gq, B_c, B_in, part=True)
    kb.dma(sp, gkvs[:], I.gkv, B_c, B_in, part=True)
    kb.dma(sp, badas[:], I.bada, B_c, B_in, part=True)
    kb.dma(sp, ccs[:], I.ccol, B_c, B_in, part=True)
    kb.op(dve, lambda: V.tensor_copy(identb[:], ident[:]), reads=[B_c], parts=[B_c])
    kb.op(dve, lambda: V.memset(onesb[:], 1.0), parts=[B_c])
    kb.op(dve, lambda: V.memset(onesf[:], 1.0), parts=[B_c])
    kb.op(act, lambda: S.activation(sc2[:], ccs[:], AF.Silu), reads=[B_c], parts=[B_c])

    wst = contextlib.ExitStack()
    NWS = 3
    wslot = [sb(wst, f"wslot{i}", [128, KT * 128], BF16) for i in range(NWS)]
    WB = [Buf(f"wslot{i}") for i in range(NWS)]
    wctr = [0]

    def load_w(src_ap):
        i = wctr[0] % NWS
        wctr[0] += 1
        kb.dma(pool, wslot[i][:], src_ap, WB[i], B_in)
        return wslot[i], WB[i]

    ada_next = [0]

    def adaln(n):
        for _ in range(n):
            m = ada_next[0]
            if m >= 192:
                return
            ada_next[0] += 1
            w, wb = load_w(I.wada[m])
            pb = 7
            for k in range(KT):
                mm(ps[pb][:, 0:2], w[:, k * 128:(k + 1) * 128], sc2[:, k, :], k == 0, k == KT - 1,
                   reads=[wb, B_c], writes=[PB[pb]] if k == 0 else (), parts=() if k == 0 else [PB[pb]],
                   signal=(k == KT - 1))
            kb.op(dve, lambda: V.tensor_scalar(modc[:, m, :], ps[pb][:, 0:2], badas[:, m:m + 1], None, ALU.add),
                  reads=[PB[pb], B_c], parts=[B_mod])

    adaln(64)
    def vec_scale(dst, gi, mlo, col):
        kb.op(dve, lambda: V.scalar_tensor_tensor(vecs[:, dst, :], modc[:, mlo:mlo + KT, col], 1.0, gv[:, gi, :],
                                                  ALU.add, ALU.mult), reads=[B_mod, B_c], parts=[B_vecs])

    def vec_copy(dst, mlo, col):
        kb.op(dve, lambda: V.tensor_copy(vecs[:, dst, :], modc[:, mlo:mlo + KT, col]), reads=[B_mod], parts=[B_vecs])

    def vec_mul(dst, gi, mlo, col):
        kb.op(dve, lambda: V.tensor_tensor(vecs[:, dst, :], modc[:, mlo:mlo + KT, col], gv[:, gi, :], ALU.mult),
              reads=[B_mod, B_c], parts=[B_vecs])

    vec_scale(0, 0, 32, 0)
    vec_copy(1, 0, 0)
    vec_scale(2, 0, 32, 1)
    vec_copy(3, 0, 1)

    if STOP_AFTER == "p0":
        d3 = nc.dram_tensor("dbg_vecs", [128, 8, KT], F32, kind="ExternalOutput").ap()
        kb.dma(sp, d3, vecs[:], B_out, B_vecs)
        wst.close()
        return finish(nc, es, kb, out, B_out)
    st = contextlib.ExitStack()
    xch = [sb(st, f"xch{i}", [128, D], F32) for i in range(2)]
    XB = [Buf(f"xch{i}") for i in range(2)]
    hmod = sb(st, "hmod", [128, KT, 512], BF16)
    B_h = Buf("hmod")
    ssq = sb(st, "ssq", [128, 2], F32)
    rs = sb(st, "rs", [128, 2], F32)
    B_ss = [Buf("ss0"), Buf("ss1")]
    junk = sb(st, "junk", [128, D], BF16)
    B_junk = Buf("junk")
    ust = [sb(st, f"ust{i}", [128, 512], BF16) for i in range(3)]
    UB = [Buf(f"ust{i}") for i in range(3)]
    qcT = sb(st, "qcT", [128, 8, 512], F32)
    B_qc = Buf("qcT")
    sqb = [sb(st, f"sqb{i}", [128, 512], BF16) for i in range(2)]
    SQB = [Buf("sqb0"), Buf("sqb1")]
    rstd = sb(st, "rstd", [128, 512], F32)
    B_rstd = Buf("rstd")
    rtab = sb(st, "rtab", [64, 2, 512], F32)
    B_rtab = Buf("rtab")
    rtmp = sb(st, "rtmp", [64, 512], F32)
    B_rtmp = Buf("rtmp")
    uctr = [0]
    xctr = [0]
    sqctr = [0]

    def p1_tile(kind, t0, w):
        src = I.ctxl if kind == "ctx" else I.xl
        vs, vb = (2, 3) if kind == "ctx" else (0, 1)
        c0 = 0
        while c0 < w:
            cw = min(128, w - c0)
            xi = xctr[0] % 2
            xctr[0] += 1
            xc, xb = xch[xi], XB[xi]
            kb.dma(sp, xc[0:cw, :], src[t0 + c0:t0 + c0 + cw, :], xb, B_in)
            kb.op(act, lambda: S.activation(junk[0:cw, :], xc[0:cw, :], AF.Square, accum_out=ssq[0:cw, xi:xi + 1]),
                  reads=[xb], writes=[B_junk, B_ss[xi]])
            kb.op(act, lambda: S.activation(rs[0:cw, xi:xi + 1], ssq[0:cw, xi:xi + 1], AF.Sqrt, bias=EPS,
                                            scale=1.0 / D), reads=[B_ss[xi]], parts=[B_ss[xi]])
            kb.op(dve, lambda: V.reciprocal(rs[0:cw, xi:xi + 1], rs[0:cw, xi:xi + 1]), reads=[B_ss[xi]],
                  parts=[B_ss[xi]])
            kb.op(dve, lambda: V.tensor_scalar(xc[0:cw, :], xc[0:cw, :], rs[0:cw, xi:xi + 1], None, ALU.mult),
                  reads=[B_ss[xi]], parts=[xb])
            for k4 in range(8):
                pb = k4 % 2
                for j in range(4):
                    k = k4 * 4 + j
                    kb.op(pe, lambda: T.transpose(ps[pb][:, j * 128:j * 128 + cw], xc[0:cw, k * 128:(k + 1) * 128],
                                                  ident[0:cw, 0:cw]),
                          reads=[xb, B_c], writes=[PB[pb]] if j == 0 else (), parts=() if j == 0 else [PB[pb]],
                          signal=(j == 3))
                for j in range(4):
                    k = k4 * 4 + j
                    eng = dve
                    if eng is dve:
                        kb.op(dve, lambda: V.tensor_scalar(hmod[:, k, c0:c0 + cw], ps[pb][:, j * 128:j * 128 + cw],
                                                           vecs[:, vs, k:k + 1], vecs[:, vb, k:k + 1], ALU.mult,
                                                           ALU.add),
                              reads=[PB[pb], B_vecs], parts=[B_h])
                    else:
                        kb.op(act, lambda: S.activation(hmod[:, k, c0:c0 + cw], ps[pb][:, j * 128:j * 128 + cw],
                                                        AF.Identity, bias=vecs[:, vb, k:k + 1],
                                                        scale=vecs[:, vs, k:k + 1]),
                              reads=[PB[pb], B_vecs], parts=[B_h])
            c0 += cw
        mlist = list(range(21)) if kind == "own" else (list(range(8)) + list(range(16, 21)))
        if P1_STAGE == 1:
            return
        if P1_STAGE == 2:
            mlist = [0, 1]
        if P1_STAGE in (3, 4, 5):
            mlist = [0, 1, 16, 17, 18, 19]
        if kind == "ctx":
            keyc = SEQ
        else:
            keyc = t0
        for m in mlist:
            wt, wb = load_w(I.win[m])
            if m < 20:
                pb = 2 + (m % 2)
                for k in range(KT):
                    mm(ps[pb][:, 0:w], wt[:, k * 128:(k + 1) * 128], hmod[:, k, 0:w], k == 0, k == KT - 1,
                       reads=[wb, B_h], writes=[PB[pb]] if k == 0 else (), parts=() if k == 0 else [PB[pb]],
                       signal=(k == KT - 1))
                if m < 8:
                    ui = uctr[0] % 3
                    uctr[0] += 1
                    kb.op(act, lambda: S.copy(ust[ui][:, 0:w], ps[pb][:, 0:w]), reads=[PB[pb]], writes=[UB[ui]])
                    if kind == "ctx":
                        kb.dma(sp, uT[m * 128:(m + 1) * 128, 0:CTX], ust[ui][:, 0:w], B_uT, UB[ui])
                        kb.dma(sp, uT[m * 128:(m + 1) * 128, CTX + SEQ:UCOLS], ust[ui][:, 0:w], B_uT, UB[ui])
                    else:
                        kb.dma(sp, uT[m * 128:(m + 1) * 128, CTX + t0:CTX + t0 + w], ust[ui][:, 0:w], B_uT, UB[ui])
                else:
                    j = m - 8 if m < 16 else m - 16
                    nj = 8 if m < 16 else 4
                    kb.op(dve, lambda: V.tensor_copy(qcT[:, j, 0:w], ps[pb][:, 0:w]), reads=[PB[pb]], parts=[B_qc])
                    si = sqctr[0] % 2
                    sqctr[0] += 1
                    kb.op(act, lambda: S.activation(sqb[si][:, 0:w], qcT[:, j, 0:w], AF.Square), reads=[B_qc],
                          writes=[SQB[si]])
                    if P1_STAGE != 5:
                        mm(ps[4][:, 0:w], onesb[:], sqb[si][:, 0:w], j == 0, j == nj - 1, reads=[SQB[si], B_c],
                           writes=[PB[4]] if j == 0 else (), parts=() if j == 0 else [PB[4]], signal=True)
                    if j == nj - 1 and P1_STAGE not in (4, 5):
                        nfeat = 1024.0 if m < 16 else 512.0
                        kb.op(act, lambda: S.activation(rstd[:, 0:w], ps[4][:, 0:w], AF.Sqrt, bias=EPS,
                                                        scale=1.0 / nfeat), reads=[PB[4]], writes=[B_rstd])
                        kb.op(dve, lambda: V.reciprocal(rstd[:, 0:w], rstd[:, 0:w]), reads=[B_rstd], parts=[B_rstd])
                        for jj in range(nj):
                            if m < 16:
                                ui = uctr[0] % 3
                                uctr[0] += 1
                                kb.op(dve, lambda: V.scalar_tensor_tensor(ust[ui][:, 0:w], qcT[:, jj, 0:w],
                                                                          gqs[:, jj:jj + 1], rstd[:, 0:w], ALU.mult,
                                                                          ALU.mult),
                                      reads=[B_qc, B_rstd, B_c], writes=[UB[ui]])
                                kb.dma(sp, qcnTs[jj * 128:(jj + 1) * 128, t0:t0 + w], ust[ui][:, 0:w], B_qcn, UB[ui])
                            else:
                                kb.op(dve, lambda: V.scalar_tensor_tensor(kvcnT[:, jj, keyc:keyc + w],
                                                                          qcT[:, jj, 0:w], gkvs[:, jj:jj + 1],
                                                                          rstd[:, 0:w], ALU.mult, ALU.mult),
                                      reads=[B_qc, B_rstd, B_c], parts=[B_kvcn])
            else:
                kb.dma(sp, rtab[:, :, 0:w], I.ropek[:, :, keyc:keyc + w], B_rtab, B_in)
                for half in range(2):
                    pb = 5 + half
                    for k in range(KT):
                        mm(ps[pb][0:64, 0:w], wt[:, k * 128 + half * 64:k * 128 + half * 64 + 64], hmod[:, k, 0:w],
                           k == 0, k == KT - 1, reads=[wb, B_h], writes=[PB[pb]] if k == 0 else (),
                           parts=() if k == 0 else [PB[pb]], signal=(k == KT - 1))
                kb.op(dve, lambda: V.tensor_tensor(rtmp[:, 0:w], ps[5][0:64, 0:w], rtab[:, 0, 0:w], ALU.mult),
                      reads=[PB[5], B_rtab], writes=[B_rtmp])
                kb.op(dve, lambda: V.tensor_tensor(rtab[:, 1, 0:w], ps[6][0:64, 0:w], rtab[:, 1, 0:w], ALU.mult),
                      reads=[PB[6], B_rtab], parts=[B_rtab])
                kb.op(dve, lambda: V.tensor_tensor(krT[:, keyc:keyc + w], rtmp[:, 0:w], rtab[:, 1, 0:w], ALU.add),
                      reads=[B_rtmp, B_rtab], parts=[B_krT])

    tiles = [("ctx", 0, CTX)] + [("own", a, b) for a, b in OWN_TILES] + [("rest", a, b) for a, b in REST_TILES]
    if P1_LIMIT is not None:
        tiles = tiles[:P1_LIMIT]
    if "p1" in SKIP:
        tiles = []
    for (kind, t0, w) in tiles:
        p1_tile(kind, t0, w)
        adaln(13)
    adaln(200)
    vec_mul(4, 1, 64, 0)
    vec_scale(5, 2, 128, 0)
    vec_copy(6, 96, 0)
    vec_mul(7, 3, 160, 0)
    kb.barrier()
    st.close()
    wst.close()
    if STOP_AFTER == "p1":
        d1 = nc.dram_tensor("dbg_kvcn", [128, 4, NKEY], BF16, kind="ExternalOutput").ap()
        d2 = nc.dram_tensor("dbg_krT", [64, NKEY], BF16, kind="ExternalOutput").ap()
        d3 = nc.dram_tensor("dbg_vecs", [128, 8, KT], F32, kind="ExternalOutput").ap()
        if P1_LIMIT is None and P1_STAGE == 0:
            kb.dma(sp, d1, kvcnT[:], B_out, B_kvcn)
            kb.dma(sp, d2, krT[:], B_out, B_krT)
        kb.dma(sp, d3, vecs[:], B_out, B_vecs)
        return finish(nc, es, kb, out, B_out)

    st = contextlib.ExitStack()
    wk = sb(st, "wk", [128, 4 * 3072], BF16)
    wv = sb(st, "wv", [128, 4 * 3072], BF16)
    B_wk, B_wv = Buf("wk"), Buf("wv")
    kb.dma(pool, wk[:], I.wuk, B_wk, B_in)
    kb.dma(pool, wv[:], I.wuv, B_wv, B_in)
    kst = [sb(st, f"kst{i}", [128, 512], BF16) for i in range(4)]
    KSB = [Buf(f"kst{i}") for i in range(4)]
    ctr = 0
    for h in range(0 if "p2" in SKIP else NH):
        for c0 in range(0, NKEY, 512):
            w = min(512, NKEY - c0)
            pb = ctr % 2
            si = ctr % 4
            ctr += 1
            for rk in range(4):
                mm(ps[pb][:, 0:w], wk[:, rk * 3072 + h * 128:rk * 3072 + (h + 1) * 128], kvcnT[:, rk, c0:c0 + w],
                   rk == 0, rk == 3, reads=[B_wk, B_kvcn], writes=[PB[pb]] if rk == 0 else (),
                   parts=() if rk == 0 else [PB[pb]], signal=(rk == 3))
            if ctr % 2 == 0:
                kb.op(act, lambda: S.copy(kst[si][:, 0:w], ps[pb][:, 0:w]), reads=[PB[pb]], writes=[KSB[si]])
            else:
                kb.op(dve, lambda: V.tensor_copy(kst[si][:, 0:w], ps[pb][:, 0:w]), reads=[PB[pb]], writes=[KSB[si]])
            kb.dma(sp, KTs[h, :, c0:c0 + w], kst[si][:, 0:w], B_KT, KSB[si])
    for kt in range(0 if "p2" in SKIP else 34):
        for hg in range(6):
            pb = ctr % 2
            si = ctr % 4
            ctr += 1
            for rk in range(4):
                mm(ps[pb][:, 0:512], kvcnT[:, rk, kt * 128:(kt + 1) * 128],
                   wv[:, rk * 3072 + hg * 512:rk * 3072 + (hg + 1) * 512], rk == 0, rk == 3,
                   reads=[B_wv, B_kvcn], writes=[PB[pb]] if rk == 0 else (), parts=() if rk == 0 else [PB[pb]],
                   signal=(rk == 3))
            if ctr % 2 == 0:
                kb.op(act, lambda: S.copy(kst[si][:, :], ps[pb][:, :]), reads=[PB[pb]], writes=[KSB[si]])
            else:
                kb.op(dve, lambda: V.tensor_copy(kst[si][:, :], ps[pb][:, :]), reads=[PB[pb]], writes=[KSB[si]])
            kb.dma(sp, Vs[kt, :, hg * 512:(hg + 1) * 512], kst[si][:, :], B_V, KSB[si])
    kb.barrier()
    st.close()
    if STOP_AFTER == "p2":
        return finish(nc, es, kb, out, B_out)

    st = contextlib.ExitStack()
    TWO_PI = 6.283185307179586
    lam = sb(st, "lam", [128, 2, 64], F32)
    lsd = sb(st, "lsd", [128, 64], F32)
    Ball = sb(st, "Ball", [128, 64, 2, 32], F32)
    Call = sb(st, "Call", [128, 64, 2, 32], F32)
    dcol = sb(st, "dcol", [32, 32], F32)
    B_s5 = Buf("s5setup")
    B_BC = Buf("BC")
    kb.dma(sp, lam[:], I.s5lam, B_s5, B_in)
    kb.dma(sp, lsd[:], I.s5ls, B_s5, B_in, part=True)
    kb.dma(sp, dcol[:], I.s5d, B_s5, B_in, part=True)
    kb.op(dve, lambda: V.memset(Ball[:], 0.0), writes=[B_BC])
    kb.op(dve, lambda: V.memset(Call[:], 0.0), parts=[B_BC])
    kb.dma(sp, Ball[0:64, :, :, 0:16], I.s5b[0:64], B_BC, B_in)
    kb.dma(sp, Ball[64:128, :, :, 16:32], I.s5b[64:128], B_BC, B_in, part=True)
    kb.dma(sp, Call[0:64, :, :, 0:16], I.s5c[0:64], B_BC, B_in, part=True)
    kb.dma(sp, Call[64:128, :, :, 16:32], I.s5c[64:128], B_BC, B_in, part=True)
    nsc = [0]

    def stile(dt=F32, shape=(128, 64)):
        nsc[0] += 1
        return sb(st, f"s5t{nsc[0]}", list(shape), dt)

    def dv(f):
        kb.op(dve, f, reads=[B_s5], parts=[B_s5])

    def ac(f):
        kb.op(act, f, reads=[B_s5], parts=[B_s5])

    lr, li = lam[:, 0, :], lam[:, 1, :]
    dtt, mag, ang, nf, s2, s4, ch, sinr, cosr, t1, t2 = [stile() for _ in range(11)]
    ni = stile(mybir.dt.int32)
    ac(lambda: S.activation(dtt[:], lsd[:], AF.Exp))
    dv(lambda: V.tensor_tensor(t1[:], lr, dtt[:], ALU.mult))
    ac(lambda: S.activation(mag[:], t1[:], AF.Exp))
    dv(lambda: V.tensor_tensor(ang[:], li, dtt[:], ALU.mult))
    dv(lambda: V.tensor_scalar(t1[:], ang[:], 1.0 / TWO_PI, None, ALU.mult))
    dv(lambda: V.tensor_copy(ni[:], t1[:]))
    dv(lambda: V.tensor_copy(nf[:], ni[:]))
    dv(lambda: V.scalar_tensor_tensor(t2[:], nf[:], -TWO_PI, ang[:], ALU.mult, ALU.add))
    ac(lambda: S.activation(s2[:], t2[:], AF.Sin, scale=0.5))
    ac(lambda: S.activation(s4[:], t2[:], AF.Sin, scale=0.25))
    dv(lambda: V.tensor_tensor(t1[:], s4[:], s4[:], ALU.mult))
    dv(lambda: V.tensor_scalar(ch[:], t1[:], -2.0, 1.0, ALU.mult, ALU.add))
    dv(lambda: V.tensor_tensor(t1[:], s2[:], ch[:], ALU.mult))
    dv(lambda: V.tensor_scalar(sinr[:], t1[:], 2.0, None, ALU.mult))
    dv(lambda: V.tensor_tensor(t1[:], s2[:], s2[:], ALU.mult))
    dv(lambda: V.tensor_scalar(cosr[:], t1[:], -2.0, 1.0, ALU.mult, ALU.add))
    apw = sb(st, "apw", [128, 9, 2, 64], F32)
    napw = sb(st, "napw", [128, 9, 2, 64], F32)
    lev = sb(st, "lev", [128, 10, 2, 64], F32)
    nlev = sb(st, "nlev", [128, 10, 64], F32)
    ff = sb(st, "ff", [128, 2, 64], F32)
    nfi = stile()
    dv(lambda: V.memset(apw[:, 0, 0, :], 1.0))
    dv(lambda: V.memset(apw[:, 0, 1, :], 0.0))
    dv(lambda: V.tensor_tensor(apw[:, 1, 0, :], mag[:], cosr[:], ALU.mult))
    dv(lambda: V.tensor_tensor(apw[:, 1, 1, :], mag[:], sinr[:], ALU.mult))
    ar, ai = apw[:, 1, 0, :], apw[:, 1, 1, :]
    nr, den = stile(), stile()
    dv(lambda: V.tensor_scalar(nr[:], ar, -1.0, None, ALU.add))
    dv(lambda: V.tensor_tensor(t1[:], lr, lr, ALU.mult))
    dv(lambda: V.tensor_tensor(t2[:], li, li, ALU.mult))
    dv(lambda: V.tensor_tensor(den[:], t1[:], t2[:], ALU.add))
    dv(lambda: V.reciprocal(den[:], den[:]))
    dv(lambda: V.tensor_tensor(t1[:], nr[:], lr, ALU.mult))
    dv(lambda: V.tensor_tensor(t2[:], ai, li, ALU.mult))
    dv(lambda: V.tensor_tensor(t1[:], t1[:], t2[:], ALU.add))
    dv(lambda: V.tensor_tensor(ff[:, 0, :], t1[:], den[:], ALU.mult))
    dv(lambda: V.tensor_tensor(t1[:], ai, lr, ALU.mult))
    dv(lambda: V.tensor_tensor(t2[:], nr[:], li, ALU.mult))
    dv(lambda: V.tensor_tensor(t1[:], t1[:], t2[:], ALU.subtract))
    dv(lambda: V.tensor_tensor(ff[:, 1, :], t1[:], den[:], ALU.mult))
    dv(lambda: V.tensor_scalar(nfi[:], ff[:, 1, :], -1.0, None, ALU.mult))

    def cmul(o_r, o_i, a_r, a_i, b_r, b_i):
        dv(lambda: V.tensor_tensor(t1[:], a_r, b_r, ALU.mult))
        dv(lambda: V.tensor_tensor(t2[:], a_i, b_i, ALU.mult))
        dv(lambda: V.tensor_tensor(den[:], a_r, b_i, ALU.mult))
        dv(lambda: V.tensor_tensor(nr[:], a_i, b_r, ALU.mult))
        dv(lambda: V.tensor_tensor(o_r, t1[:], t2[:], ALU.subtract))
        dv(lambda: V.tensor_tensor(o_i, den[:], nr[:], ALU.add))

    for k in range(2, 9):
        cmul(apw[:, k, 0, :], apw[:, k, 1, :], apw[:, k - 1, 0, :], apw[:, k - 1, 1, :], ar, ai)
    dv(lambda: V.tensor_scalar(napw[:], apw[:], -1.0, None, ALU.mult))
    dv(lambda: V.tensor_copy(lev[:, 0, :, :], apw[:, 8, :, :]))
    for l in range(1, 10):
        cmul(lev[:, l, 0, :], lev[:, l, 1, :], lev[:, l - 1, 0, :], lev[:, l - 1, 1, :], lev[:, l - 1, 0, :],
             lev[:, l - 1, 1, :])
    dv(lambda: V.tensor_scalar(nlev[:], lev[:, :, 1, :], -1.0, None, ALU.mult))

    uTp = [sb(st, f"uTp{i}", [32, UCOLS], BF16) for i in range(2)]
    B_uTp = [Buf("uTp0"), Buf("uTp1")]
    bbar = sb(st, "bbar", [128, 2, 32], F32)
    B_bbar = Buf("bbar")
    Eb = sb(st, "Eb", [128, 8, 2, 32], BF16)
    B_Eb = Buf("Eb")
    Fb = [sb(st, f"Fb{i}", [128, 8, 2, 32], BF16) for i in range(2)]
    B_Fb = [Buf("Fb0"), Buf("Fb1")]
    Cb = sb(st, "Cb", [128, 2, 32], BF16)
    B_Cb = Buf("Cb")
    Bw = [sb(st, f"Bw{i}", [32, 8, 2, 128], BF16) for i in range(2)]
    B_Bw = [Buf("Bw0"), Buf("Bw1")]
    Kt = [sb(st, f"Kt{i}", [32, 8, 32], BF16) for i in range(2)]
    B_Kt = [Buf("Kt0"), Buf("Kt1")]
    K0 = sb(st, "K0", [32, 32], BF16)
    K0f = sb(st, "K0f", [32, 32], F32)
    B_K0 = Buf("K0")
    tmpE = [sb(st, f"tmpE{i}", [128, 32], F32) for i in range(4)]
    B_tmpE = [Buf(f"tmpE{i}") for i in range(4)]
    XA = [sb(st, f"XA{i}", [128, 2, NBA], F32) for i in range(2)]
    XBt = [sb(st, f"XB{i}", [128, 2, NBB], F32) for i in range(2)]
    B_XA = [Buf("XA0"), Buf("XA1")]
    B_XB = [Buf("XB0"), Buf("XB1")]
    SA = sb(st, "SA", [128, 2, NBA], BF16)
    SB_ = sb(st, "SB", [128, 2, NBB], BF16)
    B_SA, B_SB = Buf("SA"), Buf("SB")
    ys = [sb(st, f"ys{i}", [32, 512], F32) for i in range(3)]
    B_ys = [Buf(f"ys{i}") for i in range(3)]
    yo = [sb(st, f"yo{i}", [32, 512], BF16) for i in range(2)]
    B_yo = [Buf("yo0"), Buf("yo1")]
    yfs = [sb(st, f"yf{i}", [32, 512], F32) for i in range(2)]
    B_yfs = [Buf("yf0"), Buf("yf1")]
    tec = [0]
    psb6 = ps[6][:].bitcast(BF16)

    def two_term(out_ap, a_ap, sa, b_ap, sbb, reads, out_buf, part=True):
        i = tec[0] % 4
        tec[0] += 1
        kb.op(dve, lambda: V.tensor_scalar(tmpE[i][:], b_ap, sbb, None, ALU.mult), reads=reads + [B_s5],
              writes=[B_tmpE[i]])
        kb.op(dve, lambda: V.scalar_tensor_tensor(out_ap, a_ap, sa, tmpE[i][:], ALU.mult, ALU.add),
              reads=reads + [B_s5, B_tmpE[i]], parts=[out_buf])

    def hs_scan(X, BX, nblk, dp, forward):
        cur, s, l = 0, 1, 0
        while s < nblk:
            Pr, Pi, nPi = lev[:, l, 0, dp:dp + 1], lev[:, l, 1, dp:dp + 1], nlev[:, l, dp:dp + 1]
            o, n = X[cur], X[1 - cur]
            bo, bn = BX[cur], BX[1 - cur]
            if forward:
                d0, d1, s0, s1, k0, k1 = s, nblk, 0, nblk - s, 0, s
            else:
                d0, d1, s0, s1, k0, k1 = 0, nblk - s, s, nblk, nblk - s, nblk
            kb.op(dve, lambda: V.scalar_tensor_tensor(n[:, 0, d0:d1], o[:, 0, s0:s1], Pr, o[:, 0, d0:d1], ALU.mult,
                                                      ALU.add), reads=[bo, B_s5], writes=[bn])
            kb.op(dve, lambda: V.scalar_tensor_tensor(n[:, 0, d0:d1], o[:, 1, s0:s1], nPi, n[:, 0, d0:d1], ALU.mult,
                                                      ALU.add), reads=[bo, B_s5, bn], parts=[bn])
            kb.op(dve, lambda: V.scalar_tensor_tensor(n[:, 1, d0:d1], o[:, 0, s0:s1], Pi, o[:, 1, d0:d1], ALU.mult,
                                                      ALU.add), reads=[bo, B_s5], parts=[bn])
            kb.op(dve, lambda: V.scalar_tensor_tensor(n[:, 1, d0:d1], o[:, 1, s0:s1], Pr, n[:, 1, d0:d1], ALU.mult,
                                                      ALU.add), reads=[bo, B_s5, bn], parts=[bn])
            kb.op(act, lambda: S.copy(n[:, :, k0:k1], o[:, :, k0:k1]), reads=[bo], parts=[bn])
            cur, s, l = 1 - cur, s * 2, l + 1
        return cur

    YCH = [(0, 64), (64, 64), (128, 64), (192, 64), (256, 1)]
    ychk = [0]
    for pair in range(0 if "p3" in SKIP else 32):
        ui = pair % 2
        kb.dma(sp, uTp[ui][:], uT[pair * 32:(pair + 1) * 32, :], B_uTp[ui], B_uT)
        for dr in range(2):
            dp = dr * 32 + pair
            Br, Bi = Ball[:, dp, 0, :], Ball[:, dp, 1, :]
            Cr, Ci = Call[:, dp, 0, :], Call[:, dp, 1, :]
            fr, fi, nfi_ = ff[:, 0, dp:dp + 1], ff[:, 1, dp:dp + 1], nfi[:, dp:dp + 1]
            kb.op(dve, lambda: V.tensor_scalar(bbar[:, 0, :], Br, fr, None, ALU.mult), reads=[B_BC, B_s5],
                  writes=[B_bbar])
            kb.op(dve, lambda: V.scalar_tensor_tensor(bbar[:, 0, :], Bi, nfi_, bbar[:, 0, :], ALU.mult, ALU.add),
                  reads=[B_BC, B_s5, B_bbar], parts=[B_bbar])
            kb.op(dve, lambda: V.tensor_scalar(bbar[:, 1, :], Br, fi, None, ALU.mult), reads=[B_BC, B_s5],
                  parts=[B_bbar])
            kb.op(dve, lambda: V.scalar_tensor_tensor(bbar[:, 1, :], Bi, fr, bbar[:, 1, :], ALU.mult, ALU.add),
                  reads=[B_BC, B_s5, B_bbar], parts=[B_bbar])
            for k in range(8):
                akr, aki, naki = apw[:, k, 0, dp:dp + 1], apw[:, k, 1, dp:dp + 1], napw[:, k, 1, dp:dp + 1]
                two_term(Eb[:, k, 0, :], bbar[:, 0, :], akr, bbar[:, 1, :], naki, [B_bbar], B_Eb)
                two_term(Eb[:, k, 1, :], bbar[:, 0, :], aki, bbar[:, 1, :], akr, [B_bbar], B_Eb)
            for k in range(1, 9):
                akr, aki = apw[:, k, 0, dp:dp + 1], apw[:, k, 1, dp:dp + 1]
                nakr, naki = napw[:, k, 0, dp:dp + 1], napw[:, k, 1, dp:dp + 1]
                two_term(Fb[dr][:, k - 1, 0, :], Cr, akr, Ci, naki, [B_BC], B_Fb[dr])
                two_term(Fb[dr][:, k - 1, 1, :], Cr, naki, Ci, nakr, [B_BC], B_Fb[dr])
            kb.op(dve, lambda: V.tensor_copy(Cb[:, 0, :], Cr), reads=[B_BC], parts=[B_Cb])
            kb.op(dve, lambda: V.tensor_scalar(Cb[:, 1, :], Ci, -1.0, None, ALU.mult), reads=[B_BC], parts=[B_Cb])
            for ri in range(2):
                for sg_ in range(8):
                    k = (7 - sg_) if dr == 0 else sg_
                    kb.op(pe, lambda: T.transpose(psb6[0:32, sg_ * 128:(sg_ + 1) * 128], Eb[:, k, ri, :], identb[:, :]),
                          reads=[B_Eb, B_c], writes=[PB[6]] if sg_ == 0 else (), parts=() if sg_ == 0 else [PB[6]],
                          signal=(sg_ == 7))
                kb.op(act, lambda: S.copy(Bw[dr][:, :, ri, :], psb6[0:32, 0:1024].rearrange("p (s c) -> p s c", c=128)),
                      reads=[PB[6]], parts=[B_Bw[dr]])
            for tau in range(8):
                mm(ps[7][0:32, tau * 32:(tau + 1) * 32], Eb[:, tau, 0, :], Cb[:, 0, :], True, False,
                   reads=[B_Eb, B_Cb], writes=[PB[7]] if tau == 0 else (), parts=() if tau == 0 else [PB[7]])
                mm(ps[7][0:32, tau * 32:(tau + 1) * 32], Eb[:, tau, 1, :], Cb[:, 1, :], False, True,
                   reads=[B_Eb, B_Cb], parts=[PB[7]], signal=(tau == 7))
            kb.op(dve, lambda: V.tensor_copy(Kt[dr][:], ps[7][0:32, 0:256].rearrange("p (t c) -> p t c", c=32)),
                  reads=[PB[7]], writes=[B_Kt[dr]])
            if dr == 0:
                kb.op(dve, lambda: V.tensor_copy(K0f[:], ps[7][0:32, 0:32]), reads=[PB[7]], writes=[B_K0])
            else:
                kb.op(dve, lambda: V.tensor_tensor(K0f[:], K0f[:], ps[7][0:32, 0:32], ALU.add), reads=[PB[7], B_K0],
                      parts=[B_K0])
                kb.op(dve, lambda: V.scalar_tensor_tensor(K0f[:], ident[0:32, 0:32], dcol[:, pair:pair + 1], K0f[:],
                                                          ALU.mult, ALU.add), reads=[B_K0, B_c, B_s5], parts=[B_K0])
                kb.op(dve, lambda: V.tensor_copy(K0[:], K0f[:]), reads=[B_K0], parts=[B_K0])
        for ri in range(2):
            for sg_ in range(8):
                mm(ps[ri][:, 0:NBA], Bw[0][:, sg_, ri, :], uTp[ui][:, sg_:8 * NBA:8], sg_ == 0, sg_ == 7,
                   reads=[B_Bw[0], B_uTp[ui]], writes=[PB[ri]] if sg_ == 0 else (), parts=() if sg_ == 0 else [PB[ri]],
                   signal=(sg_ == 7))
            kb.op(act if ri == 0 else dve,
                  (lambda: S.copy(XA[0][:, ri, :], ps[ri][:, 0:NBA])) if ri == 0 else
                  (lambda: V.tensor_copy(XA[0][:, ri, :], ps[ri][:, 0:NBA])),
                  reads=[PB[ri]], writes=[B_XA[0]] if ri == 0 else (), parts=() if ri == 0 else [B_XA[0]])
        for ri in range(2):
            for c in range(2):
                pbk = 2 + 2 * ri + c
                base = CTX + 8 * 272 * c
                for sg_ in range(8):
                    mm(ps[pbk][:, 0:272], Bw[1][:, sg_, ri, :], uTp[ui][:, base + sg_:base + 8 * 272:8], sg_ == 0,
                       sg_ == 7, reads=[B_Bw[1], B_uTp[ui]], writes=[PB[pbk]] if sg_ == 0 else (),
                       parts=() if sg_ == 0 else [PB[pbk]], signal=(sg_ == 7))
                first = (ri == 0 and c == 0)
                kb.op(dve, lambda: V.tensor_copy(XBt[0][:, ri, 272 * c:272 * (c + 1)], ps[pbk][:, 0:272]),
                      reads=[PB[pbk]], writes=[B_XB[0]] if first else (), parts=() if first else [B_XB[0]])
        ca = hs_scan(XA, B_XA, NBA, pair, True)
        cb = hs_scan(XBt, B_XB, NBB, 32 + pair, False)
        kb.op(dve, lambda: V.tensor_copy(SA[:], XA[ca][:]), reads=[B_XA[ca]], writes=[B_SA])
        kb.op(dve, lambda: V.tensor_copy(SB_[:], XBt[cb][:]), reads=[B_XB[cb]], writes=[B_SB])
        for (jb0, nb) in YCH:
            yb = 6 + (ychk[0] % 2)
            ychk[0] += 1
            for sg_ in range(8):
                o_ap = ps[yb][0:32, sg_:8 * nb:8]
                kA, kB = sg_, 7 - sg_
                mm(o_ap, Fb[0][:, kA, 0, :], SA[:, 0, 31 + jb0:31 + jb0 + nb], True, False,
                   reads=[B_Fb[0], B_SA], writes=[PB[yb]] if sg_ == 0 else (), parts=() if sg_ == 0 else [PB[yb]])
                mm(o_ap, Fb[0][:, kA, 1, :], SA[:, 1, 31 + jb0:31 + jb0 + nb], False, False,
                   reads=[B_Fb[0], B_SA], parts=[PB[yb]])
                mm(o_ap, Fb[1][:, kB, 0, :], SB_[:, 0, jb0 + 1:jb0 + 1 + nb], False, False,
                   reads=[B_Fb[1], B_SB], parts=[PB[yb]])
                mm(o_ap, Fb[1][:, kB, 1, :], SB_[:, 1, jb0 + 1:jb0 + 1 + nb], False, False,
                   reads=[B_Fb[1], B_SB], parts=[PB[yb]])
                for sp_ in range(8):
                    if sp_ < sg_:
                        l_ap = Kt[0][:, sg_ - sp_, :]
                    elif sp_ > sg_:
                        l_ap = Kt[1][:, sp_ - sg_, :]
                    else:
                        l_ap = K0[:, :]
                    c0 = CTX + 8 * jb0 + sp_
                    mm(o_ap, l_ap, uTp[ui][:, c0:CTX + 8 * (jb0 + nb):8], False, sp_ == 7,
                       reads=[B_Kt[0], B_Kt[1], B_K0, B_uTp[ui]], parts=[PB[yb]], signal=(sp_ == 7 and sg_ == 7))
            n8 = 8 * nb
            yi = ychk[0] % 3
            yf = yfs[ychk[0] % 2]
            B_yf = B_yfs[ychk[0] % 2]
            kb.op(dve, lambda: V.tensor_copy(yf[:, 0:n8], ps[yb][0:32, 0:n8]), reads=[PB[yb]], writes=[B_yf])
            kb.op(act, lambda: S.activation(ys[yi][:, 0:n8], yf[:, 0:n8], AF.Square), reads=[B_yf],
                  writes=[B_ys[yi]])
            kb.op(dve, lambda: V.tensor_scalar(ys[yi][:, 0:n8], ys[yi][:, 0:n8], 0.044715, 1.0, ALU.mult, ALU.add),
                  reads=[B_ys[yi]], parts=[B_ys[yi]])
            kb.op(dve, lambda: V.tensor_tensor(ys[yi][:, 0:n8], ys[yi][:, 0:n8], yf[:, 0:n8], ALU.mult),
                  reads=[B_ys[yi], B_yf], parts=[B_ys[yi]])
            kb.op(act, lambda: S.activation(ys[yi][:, 0:n8], ys[yi][:, 0:n8], AF.Sigmoid, scale=1.5957691216),
                  reads=[B_ys[yi]], parts=[B_ys[yi]])
            oi = ychk[0] % 2
            kb.op(dve, lambda: V.tensor_tensor(yo[oi][:, 0:n8], ys[yi][:, 0:n8], yf[:, 0:n8], ALU.mult),
                  reads=[B_ys[yi], B_yf], writes=[B_yo[oi]])
            kb.dma(sp, yactT[pair * 32:(pair + 1) * 32, 8 * jb0:8 * jb0 + n8], yo[oi][:, 0:n8], B_yact, B_yo[oi])
    kb.barrier()
    st.close()
    if STOP_AFTER == "p3a":
        return finish(nc, es, kb, out, B_out)
    st = contextlib.ExitStack()
    wg = sb(st, "wg", [128, 8, 8 * 128], BF16)
    B_wg = Buf("wg")
    for m in range(8):
        kb.dma(pool, wg[:, m, :], I.wglu[m], B_wg, B_in, part=(m > 0))
    ya = [sb(st, f"ya{i}", [128, 8, NT], BF16) for i in range(2)]
    B_ya = [Buf("ya0"), Buf("ya1")]
    sgt = [sb(st, f"sgt{i}", [128, NT], F32) for i in range(2)]
    B_sgt = [Buf("sgt0"), Buf("sgt1")]
    go = [sb(st, f"go{i}", [128, NT], BF16) for i in range(2)]
    B_go = [Buf("go0"), Buf("go1")]
    gctr = 0
    for ti, (t0, w) in enumerate([] if "p3" in SKIP else OWN_TILES):
        yi = ti % 2
        for k in range(8):
            kb.dma(sp, ya[yi][:, k, 0:w], yactT[k * 128:(k + 1) * 128, t0:t0 + w], B_ya[yi], B_yact, part=(k > 0))
        for m in range(8):
            pb = m % 2
            for k in range(8):
                mm(ps[pb][:, 0:w], wg[:, m, k * 128:(k + 1) * 128], ya[yi][:, k, 0:w], k == 0, k == 7,
                   reads=[B_wg, B_ya[yi]], writes=[PB[pb]] if k == 0 else (), parts=() if k == 0 else [PB[pb]],
                   signal=(k == 7))
            gi = gctr % 2
            gctr += 1
            kb.op(act, lambda: S.activation(sgt[gi][:, 0:w], ps[pb][:, 0:w], AF.Sigmoid), reads=[PB[pb]],
                  writes=[B_sgt[gi]])
            kb.op(dve, lambda: V.tensor_tensor(go[gi][:, 0:w], sgt[gi][:, 0:w], ya[yi][:, m, 0:w], ALU.mult),
                  reads=[B_sgt[gi], B_ya[yi]], writes=[B_go[gi]])
            kb.dma(sp, s5outT[m * 128:(m + 1) * 128, t0:t0 + w], go[gi][:, 0:w], B_s5o, B_go[gi])
    kb.barrier()
    st.close()
    if STOP_AFTER == "p3":
        return finish(nc, es, kb, out, B_out)

    st = contextlib.ExitStack()
    qcn = sb(st, "qcn", [128, 8, NOWN], BF16)
    B_qcnS = Buf("qcnS")
    for j in range(0 if "p4" in SKIP else 8):
        kb.dma(sp, qcn[:, j, :], qcnTs[j * 128:(j + 1) * 128, :], B_qcnS, B_qcn, part=(j > 0))
    rq = sb(st, "rq", [64, 2, NOWN], F32)
    B_rq = Buf("rq")
    kb.dma(sp, rq[:], I.ropeq, B_rq, B_in)
    kth = [sb(st, f"kth{i}", [128, NKEY], BF16) for i in range(2)]
    vh = [sb(st, f"vh{i}", [128, 34, 128], BF16) for i in range(2)]
    wq = [sb(st, f"wq{i}", [128, 8 * 256], BF16) for i in range(2)]
    B_kth = [Buf("kth0"), Buf("kth1")]
    B_vh = [Buf("vh0"), Buf("vh1")]
    B_wq = [Buf("wq0"), Buf("wq1")]
    qn = [sb(st, f"qn{i}", [128, NT], BF16) for i in range(2)]
    qr = [sb(st, f"qr{i}", [64, NT], BF16) for i in range(2)]
    B_qn = [Buf("qn0"), Buf("qn1")]
    B_qr = [Buf("qr0"), Buf("qr1")]
    qtmp = sb(st, "qtmp", [64, NT], F32)
    qtmp2 = sb(st, "qtmp2", [64, NT], F32)
    B_qtmp, B_qtmp2 = Buf("qtmp"), Buf("qtmp2")
    NPS = 4
    pT = [sb(st, f"pT{i}", [128, NT], BF16) for i in range(NPS)]
    B_pT = [Buf(f"pT{i}") for i in range(NPS)]
    acc = sb(st, "acc", [128, NT], F32)
    B_acc = Buf("acc")
    rinv = sb(st, "rinv", [128, NT], F32)
    B_rinv = Buf("rinv")
    ast = [sb(st, f"ast{i}", [128, NT], BF16) for i in range(2)]
    B_ast = [Buf("ast0"), Buf("ast1")]

    def load_head(h):
        i = h % 2
        kb.dma(sp, kth[i][:], KTs[h], B_kth[i], B_KT)
        kb.dma(sp, vh[i][:], Vs[:, :, h * 128:(h + 1) * 128].rearrange("k p d -> p k d"), B_vh[i], B_V)
        kb.dma(pool, wq[i][:], I.wuq[h], B_wq[i], B_in)

    work = [(h, t0, w) for h in range(0 if "p4" in SKIP else NH) for (t0, w) in OWN_TILES]

    def emit_qproj(idx):
        h, t0, w = work[idx]
        hi, qi = h % 2, idx % 2
        for k in range(8):
            mm(ps[0][:, 0:w], wq[hi][:, k * 256:k * 256 + 128], qcn[:, k, t0:t0 + w], k == 0, k == 7,
               reads=[B_wq[hi], B_qcnS], writes=[PB[0]] if k == 0 else (), parts=() if k == 0 else [PB[0]],
               signal=(k == 7))
        for half in range(2):
            pb = 1 + half
            for k in range(8):
                mm(ps[pb][0:64, 0:w], wq[hi][:, k * 256 + 128 + half * 64:k * 256 + 192 + half * 64],
                   qcn[:, k, t0:t0 + w], k == 0, k == 7, reads=[B_wq[hi], B_qcnS],
                   writes=[PB[pb]] if k == 0 else (), parts=() if k == 0 else [PB[pb]], signal=(k == 7))
        kb.op(act, lambda: S.copy(qn[qi][:, 0:w], ps[0][:, 0:w]), reads=[PB[0]], writes=[B_qn[qi]])
        kb.op(dve, lambda: V.tensor_tensor(qtmp[:, 0:w], ps[1][0:64, 0:w], rq[:, 0, t0:t0 + w], ALU.mult),
              reads=[PB[1], B_rq], writes=[B_qtmp])
        kb.op(dve, lambda: V.tensor_tensor(qtmp2[:, 0:w], ps[2][0:64, 0:w], rq[:, 1, t0:t0 + w], ALU.mult),
              reads=[PB[2], B_rq], writes=[B_qtmp2])
        kb.op(dve, lambda: V.tensor_tensor(qr[qi][:, 0:w], qtmp[:, 0:w], qtmp2[:, 0:w], ALU.add),
              reads=[B_qtmp, B_qtmp2], writes=[B_qr[qi]])

    if work:
        load_head(0)
    cast_jobs = []
    for m in range(KT):
        cast_jobs.append((wout_b[m].rearrange("p (a b) -> (p a) b", b=2048),
                          I.wout[m].rearrange("p (a b) -> (p a) b", b=2048)))
    for ft in range(FT):
        cast_jobs.append((wffg_b[ft].rearrange("p (a b) -> (p a) b", b=2048),
                          I.wffg[ft].rearrange("p (a b) -> (p a) b", b=2048)))
        cast_jobs.append((wffu_b[ft].rearrange("p (a b) -> (p a) b", b=2048),
                          I.wffu[ft].rearrange("p (a b) -> (p a) b", b=2048)))
    for m in range(KT):
        cast_jobs.append((wdn_b[m].rearrange("p (a b) -> (p a) b", b=1376),
                          I.wdn[m].rearrange("p (a b) -> (p a) b", b=1376)))

    def issue_casts(n):
        for _ in range(n):
            if cast_jobs:
                o_, i_ = cast_jobs.pop(0)
                kb.dma(pool, o_, i_, B_wcast, B_in)

    if work:
        emit_qproj(0)
    pctr = 0
    ob = 3
    for idx, (h, t0, w) in enumerate(work):
        hi, qi = h % 2, idx % 2
        if t0 == 0 and h + 1 < NH:
            load_head(h + 1)
        issue_casts(2)

        def score(kt):
            sbk = 4 + (kt % 3)
            mm(ps[sbk][:, 0:w], kth[hi][:, kt * 128:(kt + 1) * 128], qn[qi][:, 0:w], True, False,
               reads=[B_kth[hi], B_qn[qi]], writes=[PB[sbk]])
            mm(ps[sbk][:, 0:w], krT[:, kt * 128:(kt + 1) * 128], qr[qi][:, 0:w], False, True,
               reads=[B_krT, B_qr[qi]], parts=[PB[sbk]], signal=True)

        score(0)
        score(1)
        for kt in range(34):
            sbk = 4 + (kt % 3)
            pi = pctr % NPS
            pctr += 1
            kb.op(act, lambda: S.activation(pT[pi][:, 0:w], ps[sbk][:, 0:w], AF.Exp, scale=MLA_SCALE),
                  reads=[PB[sbk]], writes=[B_pT[pi]])
            if kt == 0:
                kb.op(dve, lambda: V.tensor_copy(acc[:, 0:w], pT[pi][:, 0:w]), reads=[B_pT[pi]], writes=[B_acc])
            else:
                kb.op(dve, lambda: V.tensor_tensor(acc[:, 0:w], acc[:, 0:w], pT[pi][:, 0:w], ALU.add),
                      reads=[B_pT[pi], B_acc], parts=[B_acc])
            if kt + 2 < 34:
                score(kt + 2)
            if kt == 12 and idx + 1 < len(work):
                emit_qproj(idx + 1)
            mm(ps[ob][:, 0:w], vh[hi][:, kt, :], pT[pi][:, 0:w], kt == 0, kt == 33,
               reads=[B_vh[hi], B_pT[pi]], writes=[PB[ob]] if kt == 0 else (), parts=() if kt == 0 else [PB[ob]],
               signal=(kt == 33))
        mm(ps[7][:, 0:w], onesf[:], acc[:, 0:w], True, True, reads=[B_c, B_acc], writes=[PB[7]], signal=True)
        kb.op(dve, lambda: V.reciprocal(rinv[:, 0:w], ps[7][:, 0:w]), reads=[PB[7]], writes=[B_rinv])
        ai = idx % 2
        kb.op(dve, lambda: V.tensor_tensor(ast[ai][:, 0:w], ps[ob][:, 0:w], rinv[:, 0:w], ALU.mult),
              reads=[PB[ob], B_rinv], writes=[B_ast[ai]])
        kb.dma(sp, attT[h * 128:(h + 1) * 128, t0:t0 + w], ast[ai][:, 0:w], B_att, B_ast[ai])
    issue_casts(10000)
    kb.barrier()
    st.close()
    kvst.close()
    if STOP_AFTER == "p4":
        return finish(nc, es, kb, out, B_out)

    st = contextlib.ExitStack()
    mixin = sb(st, "mixin", [128, KT, NT], BF16)
    B_mixin = Buf("mixin")
    mixT = sb(st, "mixT", [128, KT, NT], F32)
    B_mixT = Buf("mixT")
    xT = sb(st, "xT", [128, KT, NT], F32)
    B_xT = Buf("xT")
    xch = [sb(st, f"xch{i}", [128, D], F32) for i in range(2)]
    XB = [Buf(f"xch{i}") for i in range(2)]
    NWS = 3
    wslot = [sb(st, f"wslot{i}", [128, KT * 128], BF16) for i in range(NWS)]
    WB = [Buf(f"wslot{i}") for i in range(NWS)]
    sqb = [sb(st, f"sqb{i}", [128, NT], BF16) for i in range(2)]
    SQB = [Buf("sqb0"), Buf("sqb1")]
    rstd = sb(st, "rstd", [128, NT], F32)
    B_rstd = Buf("rstd")
    tmpf = [sb(st, f"tmpf{i}", [128, NT], F32) for i in range(2)]
    B_tmpf = [Buf("tmpf0"), Buf("tmpf1")]
    hst = [sb(st, f"hst{i}", [128, NT], BF16) for i in range(2)]
    B_hst = [Buf("hst0"), Buf("hst1")]
    wctr[0] = 0

    def load_w5(src_ap):
        i = wctr[0] % NWS
        wctr[0] += 1
        kb.dma(pool, wslot[i][:], src_ap, WB[i], B_wcast)
        return wslot[i], WB[i]

    def rstd_from(psb, w, nfeat):
        kb.op(act, lambda: S.activation(rstd[:, 0:w], ps[psb][:, 0:w], AF.Sqrt, bias=EPS, scale=1.0 / nfeat),
              reads=[PB[psb]], writes=[B_rstd])
        kb.op(dve, lambda: V.reciprocal(rstd[:, 0:w], rstd[:, 0:w]), reads=[B_rstd], parts=[B_rstd])

    xctr[0] = 0
    sq5 = [0]
    for (t0, w) in ([] if "p5a" in SKIP else OWN_TILES[:P5_LIMIT]):
        for k in range(KT):
            src = s5outT[k * 128:(k + 1) * 128, t0:t0 + w] if k < 8 else attT[(k - 8) * 128:(k - 7) * 128, t0:t0 + w]
            kb.dma(sp, mixin[:, k, 0:w], src, B_mixin, B_s5o if k < 8 else B_att, part=(k > 0))
        c0 = 0
        while c0 < w:
            cw = min(128, w - c0)
            xi = xctr[0] % 2
            xctr[0] += 1
            kb.dma(sp, xch[xi][0:cw, :], I.xl[t0 + c0:t0 + c0 + cw, :], XB[xi], B_in)
            for k4 in range(8):
                pb = k4 % 2
                for j in range(4):
                    k = k4 * 4 + j
                    kb.op(pe, lambda: T.transpose(ps[pb][:, j * 128:j * 128 + cw],
                                                  xch[xi][0:cw, k * 128:(k + 1) * 128], ident[0:cw, 0:cw]),
                          reads=[XB[xi], B_c], writes=[PB[pb]] if j == 0 else (), parts=() if j == 0 else [PB[pb]],
                          signal=(j == 3))
                for j in range(4):
                    k = k4 * 4 + j
                    if pb == 0:
                        kb.op(dve, lambda: V.tensor_copy(xT[:, k, c0:c0 + cw], ps[pb][:, j * 128:j * 128 + cw]),
                              reads=[PB[pb]], parts=[B_xT])
                    else:
                        kb.op(act, lambda: S.copy(xT[:, k, c0:c0 + cw], ps[pb][:, j * 128:j * 128 + cw]),
                              reads=[PB[pb]], parts=[B_xT])
            c0 += cw
        for m in range(KT):
            wt, wb = load_w5(wout_b[m])
            pb = 2 + (m % 2)
            for k in range(KT):
                mm(ps[pb][:, 0:w], wt[:, k * 128:(k + 1) * 128], mixin[:, k, 0:w], k == 0, k == KT - 1,
                   reads=[wb, B_mixin], writes=[PB[pb]] if k == 0 else (), parts=() if k == 0 else [PB[pb]],
                   signal=(k == KT - 1))
            kb.op(dve, lambda: V.tensor_copy(mixT[:, m, 0:w], ps[pb][:, 0:w]), reads=[PB[pb]], parts=[B_mixT])
            si = sq5[0] % 2
            sq5[0] += 1
            kb.op(act, lambda: S.activation(sqb[si][:, 0:w], mixT[:, m, 0:w], AF.Square), reads=[B_mixT],
                  writes=[SQB[si]])
            mm(ps[4][:, 0:w], onesb[:], sqb[si][:, 0:w], m == 0, m == KT - 1, reads=[SQB[si], B_c],
               writes=[PB[4]] if m == 0 else (), parts=() if m == 0 else [PB[4]], signal=True)
        rstd_from(4, w, float(D))
        for m in range(KT):
            ti = m % 2
            kb.op(dve, lambda: V.tensor_tensor(tmpf[ti][:, 0:w], mixT[:, m, 0:w], rstd[:, 0:w], ALU.mult),
                  reads=[B_mixT, B_rstd], writes=[B_tmpf[ti]])
            kb.op(dve, lambda: V.scalar_tensor_tensor(xT[:, m, 0:w], tmpf[ti][:, 0:w], vecs[:, 4, m:m + 1],
                                                      xT[:, m, 0:w], ALU.mult, ALU.add),
                  reads=[B_tmpf[ti], B_vecs, B_xT], parts=[B_xT])
            kb.dma(sp, xmidT[m * 128:(m + 1) * 128, t0:t0 + w], xT[:, m, 0:w], B_xmid, B_xT)
            si = sq5[0] % 2
            sq5[0] += 1
            kb.op(act, lambda: S.activation(sqb[si][:, 0:w], xT[:, m, 0:w], AF.Square), reads=[B_xT],
                  writes=[SQB[si]])
            mm(ps[5][:, 0:w], onesb[:], sqb[si][:, 0:w], m == 0, m == KT - 1, reads=[SQB[si], B_c],
               writes=[PB[5]] if m == 0 else (), parts=() if m == 0 else [PB[5]], signal=True)
        rstd_from(5, w, float(D))
        for m in range(KT):
            ti = m % 2
            kb.op(dve, lambda: V.tensor_tensor(tmpf[ti][:, 0:w], xT[:, m, 0:w], rstd[:, 0:w], ALU.mult),
                  reads=[B_xT, B_rstd], writes=[B_tmpf[ti]])
            kb.op(dve, lambda: V.tensor_scalar(hst[ti][:, 0:w], tmpf[ti][:, 0:w], vecs[:, 5, m:m + 1],
                                               vecs[:, 6, m:m + 1], ALU.mult, ALU.add),
                  reads=[B_tmpf[ti], B_vecs], writes=[B_hst[ti]])
            kb.dma(sp, hxT[m * 128:(m + 1) * 128, t0:t0 + w], hst[ti][:, 0:w], B_hx, B_hst[ti])
    kb.barrier()
    st.close()
    if STOP_AFTER == "p5a":
        return finish(nc, es, kb, out, B_out)

    st = contextlib.ExitStack()
    FT_TILES = [(0, 410), (410, 410), (820, 410), (1230, 410), (1640, 408)]
    hx = sb(st, "hx", [128, KT, NT + 2], BF16)
    B_hxs = Buf("hxs")
    actT = sb(st, "actT", [128, FT, NT], BF16)
    B_act = Buf("actT")
    NWS = 8
    wpool = sb(st, "wpool", [128, 33024], BF16)
    wslot = [wpool[:, i * 4096:(i + 1) * 4096] for i in range(NWS)]
    WB = [Buf(f"wslot{i}") for i in range(NWS)]
    dslot = [wpool[:, j * 11008:(j + 1) * 11008] for j in range(3)]
    DB = [Buf(f"dslot{j}") for j in range(3)]

    def fence(q, bufs):
        for b_ in bufs:
            q.wait_all(b_.r)
            q.wait_all(b_.w)
    cws = sb(st, "cws", [128, FT, 4], F32)
    B_cw = Buf("cw")
    kb.dma(sp, cws[:], I.convw, B_cw, B_in)
    cv = [sb(st, f"cv{i}", [128, NT], F32) for i in range(2)]
    B_cv = [Buf("cv0"), Buf("cv1")]
    sg = [sb(st, f"sg{i}", [128, NT], F32) for i in range(2)]
    B_sg = [Buf("sg0"), Buf("sg1")]
    sqb = [sb(st, f"sqb{i}", [128, NT], BF16) for i in range(2)]
    SQB = [Buf("sqb0"), Buf("sqb1")]
    rstds = [sb(st, f"rstd{i}", [128, NT], F32) for i in range(2)]
    B_rstds = [Buf("rstd0"), Buf("rstd1")]
    fst = [sb(st, f"fst{i}", [128, NT], F32) for i in range(2)]
    B_fst = [Buf("fst0"), Buf("fst1")]
    xm2s = [sb(st, f"xm2_{i}", [128, 2, NT], F32) for i in range(2)]
    B_xm2s = [Buf("xm2_0"), Buf("xm2_1")]
    f2s = [sb(st, f"f2_{i}", [128, 2, NT], F32) for i in range(2)]
    B_f2s = [Buf("f2_0"), Buf("f2_1")]
    ost = [sb(st, f"ost{i}", [128, 512], F32) for i in range(2)]
    B_ost = [Buf("ost0"), Buf("ost1")]
    wctr[0] = 0
    dctr = 0
    sqc = 0
    octr = 0
    pending_out = []
    def out_a(ti5, t0, w, mg, rsel, bi):
        rstd = rstds[rsel]
        B_rstd = B_rstds[rsel]
        f2, xm2, B_f2, B_xm2 = f2s[bi], xm2s[bi], B_f2s[bi], B_xm2s[bi]
        kb.dma(sp, f2[:, :, 0:w], fTs[ti5, mg * 256:(mg + 1) * 256, 0:w].rearrange("(a p) t -> p a t", p=128),
               B_f2, B_f)
        kb.dma(sp, xm2[:, :, 0:w], xmidT[mg * 256:(mg + 1) * 256, t0:t0 + w].rearrange("(a p) t -> p a t", p=128),
               B_xm2, B_xmid)
        for a in range(2):
            m = mg * 2 + a
            kb.op(dve, lambda: V.tensor_tensor(f2[:, a, 0:w], f2[:, a, 0:w], rstd[:, 0:w], ALU.mult),
                  reads=[B_f2, B_rstd], parts=[B_f2])
            kb.op(dve, lambda: V.scalar_tensor_tensor(f2[:, a, 0:w], f2[:, a, 0:w], vecs[:, 7, m:m + 1],
                                                      xm2[:, a, 0:w], ALU.mult, ALU.add),
                  reads=[B_f2, B_xm2, B_vecs], parts=[B_f2])

    def out_b(ti5, t0, w, mg, rsel, bi):
        nonlocal octr
        f2, B_f2 = f2s[bi], B_f2s[bi]
        c0 = 0
        while c0 < w:
            cw = min(128, w - c0)
            for a in range(2):
                kb.op(pe, lambda: T.transpose(ps[7][0:cw, a * 128:(a + 1) * 128], f2[:, a, c0:c0 + cw],
                                              ident[:, :]),
                      reads=[B_f2, B_c], writes=[PB[7]] if a == 0 else (), parts=() if a == 0 else [PB[7]],
                      signal=(a == 1))
            oi = octr % 2
            octr += 1
            kb.op(act, lambda: S.copy(ost[oi][0:cw, 0:256], ps[7][0:cw, 0:256]), reads=[PB[7]],
                  writes=[B_ost[oi]])
            kb.dma(sp, out[t0 + c0:t0 + c0 + cw, mg * 256:(mg + 1) * 256], ost[oi][0:cw, 0:256], B_out, B_ost[oi])
            c0 += cw

    in_flight = []

    def out_pump():
        nxt = pending_out.pop(0) if pending_out else None
        if nxt is not None:
            out_a(*nxt)
        if in_flight:
            out_b(*in_flight.pop(0))
        if nxt is not None:
            in_flight.append(nxt)

    for ti5, (t0, w) in enumerate(FT_TILES):
        rsel = ti5 % 2
        rstd = rstds[rsel]
        B_rstd = B_rstds[rsel]
        if t0 == 0:
            kb.op(dve, lambda: V.memset(hx[:, :, 0:1], 0.0), writes=[B_hxs])
            for k in range(KT):
                kb.dma(sp, hx[:, k, 1:w + 2], hxT[k * 128:(k + 1) * 128, 0:w + 1], B_hxs, B_hx, part=True)
        else:
            for k in range(KT):
                kb.dma(sp, hx[:, k, 0:w + 2], hxT[k * 128:(k + 1) * 128, t0 - 1:t0 + w + 1], B_hxs, B_hx,
                       part=(k > 0))
        for ft in range(FT):
            if ft % 5 == 2 and (pending_out or in_flight):
                out_pump()
            if ft == 0:
                fence(pool, DB)
            i = wctr[0] % NWS
            wctr[0] += 2
            kb.dma(pool, wslot[i], wffg_b[ft], WB[i], B_wcast)
            kb.dma(pool, wslot[i + 1], wffu_b[ft], WB[i + 1], B_wcast)
            gb, ub = (0, 1) if ft % 2 == 0 else (2, 3)
            for k in range(KT):
                mm(ps[gb][:, 0:w + 2], wslot[i][:, k * 128:(k + 1) * 128], hx[:, k, 0:w + 2], k == 0, k == KT - 1,
                   reads=[WB[i], B_hxs], writes=[PB[gb]] if k == 0 else (), parts=() if k == 0 else [PB[gb]],
                   signal=(k == KT - 1))
            for k in range(KT):
                mm(ps[ub][:, 0:w], wslot[i + 1][:, k * 128:(k + 1) * 128], hx[:, k, 1:w + 1], k == 0, k == KT - 1,
                   reads=[WB[i + 1], B_hxs], writes=[PB[ub]] if k == 0 else (), parts=() if k == 0 else [PB[ub]],
                   signal=(k == KT - 1))
            ci = ft % 2
            kb.op(dve, lambda: V.tensor_scalar(cv[ci][:, 0:w], ps[gb][:, 1:w + 1], cws[:, ft, 1:2], cws[:, ft, 3:4],
                                               ALU.mult, ALU.add), reads=[PB[gb], B_cw], writes=[B_cv[ci]])
            kb.op(dve, lambda: V.scalar_tensor_tensor(cv[ci][:, 0:w], ps[gb][:, 0:w], cws[:, ft, 0:1],
                                                      cv[ci][:, 0:w], ALU.mult, ALU.add),
                  reads=[PB[gb], B_cw, B_cv[ci]], parts=[B_cv[ci]])
            kb.op(dve, lambda: V.scalar_tensor_tensor(cv[ci][:, 0:w], ps[gb][:, 2:w + 2], cws[:, ft, 2:3],
                                                      cv[ci][:, 0:w], ALU.mult, ALU.add),
                  reads=[PB[gb], B_cw, B_cv[ci]], parts=[B_cv[ci]])
            kb.op(act, lambda: S.activation(sg[ci][:, 0:w], cv[ci][:, 0:w], AF.Silu), reads=[B_cv[ci]],
                  writes=[B_sg[ci]])
            kb.op(dve, lambda: V.tensor_tensor(actT[:, ft, 0:w], sg[ci][:, 0:w], ps[ub][:, 0:w], ALU.mult),
                  reads=[B_sg[ci], PB[ub]], parts=[B_act])
        for m in range(KT):
            di = dctr % 3
            dctr += 1
            if m == 0:
                fence(pool, WB)
            kb.dma(pool, dslot[di], wdn_b[m], DB[di], B_wcast)
            pb = 4 + (m % 2)
            for k in range(FT):
                mm(ps[pb][:, 0:w], dslot[di][:, k * 128:(k + 1) * 128], actT[:, k, 0:w], k == 0, k == FT - 1,
                   reads=[DB[di], B_act], writes=[PB[pb]] if k == 0 else (), parts=() if k == 0 else [PB[pb]],
                   signal=(k == FT - 1))
            fi = m % 2
            kb.op(dve, lambda: V.tensor_copy(fst[fi][:, 0:w], ps[pb][:, 0:w]), reads=[PB[pb]], writes=[B_fst[fi]])
            kb.dma(sp, fTs[ti5, m * 128:(m + 1) * 128, 0:w], fst[fi][:, 0:w], B_f, B_fst[fi])
            si = sqc % 2
            sqc += 1
            kb.op(act, lambda: S.activation(sqb[si][:, 0:w], fst[fi][:, 0:w], AF.Square), reads=[B_fst[fi]],
                  writes=[SQB[si]])
            mm(ps[6][:, 0:w], onesb[:], sqb[si][:, 0:w], m == 0, m == KT - 1, reads=[SQB[si], B_c],
               writes=[PB[6]] if m == 0 else (), parts=() if m == 0 else [PB[6]], signal=True)
        kb.op(act, lambda: S.activation(rstd[:, 0:w], ps[6][:, 0:w], AF.Sqrt, bias=EPS, scale=1.0 / D),
              reads=[PB[6]], writes=[B_rstd])
        kb.op(dve, lambda: V.reciprocal(rstd[:, 0:w], rstd[:, 0:w]), reads=[B_rstd], parts=[B_rstd])
        pending_out.extend([(ti5, t0, w, mg, rsel, mg % 2) for mg in range(16)])
    while pending_out or in_flight:
        out_pump()
    kb.barrier()
    st.close()
    return finish(nc, es, kb, out, B_out)


_EXTRA = []


def finish(nc, es, kb, out, B_out):
    kb.barrier()
    for s_ in _EXTRA:
        s_.close()
    es.close()
    return nc


def _cols(v):
    return np.ascontiguousarray(v.reshape(-1, 128).T)


def _wtiles(w):
    K, M = w.shape
    return np.ascontiguousarray(w.reshape(K // 128, 128, M // 128, 128).transpose(2, 1, 0, 3)).reshape(
        M // 128, 128, (K // 128) * 128)


def _rope_tables(tpos, is_x):
    inv = (10000.0 ** (-np.arange(16, dtype=np.float32) / 16)).astype(np.float32)
    t = tpos.astype(np.float32)
    row = np.floor(t / 64).astype(np.float32)
    col = (t - row * 64).astype(np.float32)
    ang = np.concatenate([row[:, None] * inv, col[:, None] * inv], axis=-1).astype(np.float32)
    cos, sin = np.cos(ang).astype(np.float32), np.sin(ang).astype(np.float32)
    cos = np.where(is_x[:, None], cos, 1.0).astype(np.float32)
    sin = np.where(is_x[:, None], sin, 0.0).astype(np.float32)
    cc = np.concatenate([cos.T, cos.T], axis=0)
    ss = np.concatenate([-sin.T, sin.T], axis=0)
    return np.ascontiguousarray(np.stack([cc, ss], axis=1)).astype(np.float32)


def _prep_shared(inp):
    sh = {}
    sh["wada"] = _wtiles(inp["w_ada"][0])
    sh["bada"] = _cols(inp["b_ada"][0])
    sh["gvec"] = np.ascontiguousarray(np.stack([_cols(inp[k][0]) for k in
                                                ("g_pre_mix", "g_post_mix", "g_pre_ffn", "g_post_ffn")], axis=1))
    w_in = inp["w_in"][0]
    kr = w_in[:, 2560:2624]
    ev, od = kr[:, 0::2], kr[:, 1::2]
    sh["win"] = _wtiles(np.concatenate([w_in[:, :2560], ev, od, od, ev], axis=1))
    sh["gq"] = _cols(inp["mla_g_q"][0])
    sh["gkv"] = _cols(inp["mla_g_kv"][0])
    wq = inp["mla_w_uq"][0].reshape(1024, NH, 192)
    nope, rp = wq[:, :, :128], wq[:, :, 128:]
    ev, od = rp[:, :, 0::2], rp[:, :, 1::2]
    wqp = np.concatenate([nope, ev, od, od, ev], axis=2)
    sh["wuq"] = np.ascontiguousarray(wqp.reshape(8, 128, NH, 256).transpose(2, 1, 0, 3)).reshape(NH, 128, 8 * 256)
    wkv = inp["mla_w_ukv"][0].reshape(512, NH, 256)
    wk = wkv[:, :, :128].reshape(4, 128, NH * 128)
    wv = wkv[:, :, 128:].reshape(4, 128, NH * 128)
    sh["wuk"] = np.ascontiguousarray(wk.transpose(1, 0, 2)).reshape(128, 4 * 3072)
    sh["wuv"] = np.ascontiguousarray(wv.transpose(1, 0, 2)).reshape(128, 4 * 3072)
    sh["wout"] = _wtiles(inp["w_out"][0])
    sh["wglu"] = _wtiles(inp["s5_w_glu"][0])
    fw = inp["ffn_w_in"][0]
    sh["wffg"] = _wtiles(fw[:, :DFF])
    sh["wffu"] = _wtiles(fw[:, DFF:])
    sh["wdn"] = _wtiles(inp["ffn_w_down"][0])
    sh["ident"] = np.eye(128, dtype=np.float32)
    return sh


def _prep_core(inp, sh, b, half):
    m = dict(sh)
    x = inp["x"][b]
    ctx = inp["ctx"][b]
    if half == 1:
        x = x[::-1]
        ctx = ctx[::-1]
    m["xl"] = np.ascontiguousarray(x)
    m["ctxl"] = np.ascontiguousarray(ctx)
    m["ccol"] = np.ascontiguousarray(np.stack([_cols(inp["c"][b]), _cols(inp["c_ctx"])], axis=-1))
    cw = inp["ffn_conv_w"][0]
    if half == 1:
        cw = cw[::-1]
    m["convw"] = np.ascontiguousarray(np.stack([_cols(cw[0]), _cols(cw[1]), _cols(cw[2]),
                                                _cols(inp["ffn_conv_b"][0])], axis=-1))
    loc = np.arange(SEQ)
    tpos = loc if half == 0 else (SEQ - 1 - loc)
    kp = np.concatenate([tpos, np.zeros(CTX, dtype=tpos.dtype)])
    isx = np.concatenate([np.ones(SEQ, bool), np.zeros(CTX, bool)])
    m["ropek"] = _rope_tables(kp, isx)
    m["ropeq"] = _rope_tables(tpos[:NOWN], np.ones(NOWN, bool))
    dsel = [0, 1] if half == 0 else [1, 0]

    def gp(a):
        a = a[dsel].reshape(2, 32, 2, 64)
        return a.transpose(2, 3, 0, 1).reshape(128, 64)

    m["s5lam"] = np.ascontiguousarray(np.stack([gp(inp["s5_lambda_re"][0]), gp(inp["s5_lambda_im"][0])], axis=1))
    ls = np.broadcast_to(inp["s5_log_step"][0][:, :, None], (2, 64, 64))
    m["s5ls"] = np.ascontiguousarray(gp(ls))

    def gb(ar, ai):
        a = np.stack([ar, ai], axis=-2)[dsel]
        a = a.reshape(2, 32, 2, 64, 2, 16)
        return np.ascontiguousarray(a.transpose(2, 3, 0, 1, 4, 5)).reshape(128, 64, 2, 16)

    m["s5b"] = gb(inp["s5_b_re"][0], inp["s5_b_im"][0])
    m["s5c"] = gb(inp["s5_c_re"][0].transpose(0, 1, 3, 2), inp["s5_c_im"][0].transpose(0, 1, 3, 2))
    m["s5d"] = np.ascontiguousarray(inp["s5_d"][0].reshape(32, 2, 16).transpose(1, 2, 0)).reshape(32, 32)
    return {k: np.ascontiguousarray(v, dtype=np.float32) for k, v in m.items()}


def kernel(**inputs):
    inp = {k: np.asarray(v) for k, v in inputs.items()}
    sh = _prep_shared(inp)
    in_maps = [_prep_core(inp, sh, c // 2, c % 2) for c in range(8)]
    nc = _build()
    in_maps = [{k: v for k, v in m.items() if k in nc._declared_inputs} for m in in_maps]
    res = run_bass_kernel_spmd(nc, in_maps, core_ids=list(range(8)))
    outp = np.empty((4, SEQ, D), dtype=np.float32)
    for c in range(8):
        o = res.results[c]["out"]
        b, half = c // 2, c % 2
        if half == 0:
            outp[b, :2048] = o
        else:
            outp[b, 2048:] = o[::-1]
    return outp
```

```python
import contextlib
import numpy as np
import concourse.bass as bass
import concourse.mybir as mybir
from concourse.bass_utils import run_bass_kernel_spmd

F32 = mybir.dt.float32
BF16 = mybir.dt.bfloat16
AF = mybir.ActivationFunctionType
ALU = mybir.AluOpType

D = 4096
KT = 32
SEQ = 4096
CTX = 256
NKEY = SEQ + CTX
NOWN = 2050
NT = 410
OWN_TILES = [(i * NT, NT) for i in range(5)]
REST_TILES = [(2050, 510), (2560, 512), (3072, 512), (3584, 512)]
UCOLS = CTX + SEQ + CTX
NH = 24
DFF = 11008
FT = 86
EPS = 1e-6
MLA_SCALE = 192.0 ** -0.5
NBA = 32 + 257
NBB = 544
NYB = 257
STOP_AFTER = None
DEBUG = False
P1_LIMIT = None
P1_STAGE = 0
SKIP = set()
P5_LIMIT = None


class Tok:
    __slots__ = ("sem", "val", "key")

    def __init__(self, sem, val, key):
        self.sem, self.val, self.key = sem, val, key


class Buf:
    __slots__ = ("w", "r", "dsem", "dcnt", "dkey", "last_dma", "name", "dram", "bg")

    def __init__(self, name="", dram=False):
        self.w, self.r = {}, {}
        self.dsem = None
        self.dcnt = 0
        self.last_dma = None
        self.name = name
        self.dram = dram
        self.bg = False

    @staticmethod
    def _add(d, tok):
        o = d.get(tok.key)
        if o is None or o.val < tok.val:
            d[tok.key] = tok


class Eng:
    def __init__(self, kb, h, name, is_pe=False, compute=True):
        self.kb, self.h, self.name, self.is_pe, self.compute = kb, h, name, is_pe, compute
        self.sem = kb.new_sem("e_" + name)
        self.key = "e_" + name
        self.cnt = 0
        self.waited = {}
        self.pending = False

    def wait(self, tok):
        if tok.key == self.key:
            if self.is_pe:
                return
        if self.waited.get(tok.key, 0) >= tok.val:
            return
        self.h.wait_ge(tok.sem, tok.val)
        self.waited[tok.key] = tok.val

    def wait_all(self, d):
        for t in list(d.values()):
            self.wait(t)

    def mark(self, inst, signal):
        if signal:
            self.cnt += 1
            inst.then_inc(self.sem, 1)
            self.pending = False
            return Tok(self.sem, self.cnt, self.key)
        self.pending = True
        return Tok(self.sem, self.cnt + 1, self.key)


class KB:
    def __init__(self, nc, es):
        self.nc, self.es = nc, es
        self.nsem = 0
        self.pe = Eng(self, nc.tensor, "pe", is_pe=True)
        self.dve = Eng(self, nc.vector, "dve")
        self.act = Eng(self, nc.scalar, "act")
        self.pool = Eng(self, nc.gpsimd, "pool")
        self.sp = Eng(self, nc.sync, "sp", compute=False)
        self.engs = [self.pe, self.dve, self.act, self.pool, self.sp]
        self.dbufs = []
        self.retired = []

    def new_sem(self, name):
        self.nsem += 1
        return self.es.enter_context(self.nc.semaphore(f"{name}_{self.nsem}"))

    def op(self, eng, build, reads=(), writes=(), parts=(), signal=True):
        for b in reads:
            eng.wait_all(b.w)
        for b in writes:
            eng.wait_all(b.r)
            eng.wait_all(b.w)
        for b in parts:
            eng.wait_all(b.r)
        inst = build()
        tok = eng.mark(inst, signal)
        for b in reads:
            if not b.dram:
                Buf._add(b.r, tok)
        for b in writes:
            b.w = {tok.key: tok}
            b.r = {}
        for b in parts:
            Buf._add(b.w, tok)
        return tok

    def dma(self, q, out_ap, in_ap, out_buf, in_buf, part=False, **kw):
        sb = out_buf if not out_buf.dram else (in_buf if not in_buf.dram else out_buf)
        if sb.dsem is not None and sb.dcnt >= 30000:
            if not sb.bg:
                self.retired.append(Tok(sb.dsem, sb.dcnt, sb.dkey))
            sb.dsem = self.new_sem("d")
            sb.dkey = f"d{self.nsem}"
            sb.dcnt = 0
        if sb.dsem is None:
            sb.dsem = self.new_sem("d")
            sb.dkey = f"d{self.nsem}"
            self.dbufs.append(sb)
        if sb.last_dma is not None and not sb.dram:
            q.wait(sb.last_dma)
        q.wait_all(in_buf.w)
        if not out_buf.dram:
            q.wait_all(out_buf.r)
            if not part:
                q.wait_all(out_buf.w)
        inst = q.h.dma_start(out=out_ap, in_=in_ap, **kw)
        sb.dcnt += 16
        inst.then_inc(sb.dsem, 16)
        tok = Tok(sb.dsem, sb.dcnt, sb.dkey)
        sb.last_dma = tok
        if not in_buf.dram:
            Buf._add(in_buf.r, tok)
        if out_buf.dram or part:
            Buf._add(out_buf.w, tok)
        else:
            out_buf.w = {tok.key: tok}
            out_buf.r = {}
        return tok

    def barrier(self):
        assert not self.pe.pending
        toks = [Tok(e.sem, e.cnt, e.key) for e in self.engs if e.compute and e.cnt > 0]
        toks += [Tok(b.dsem, b.dcnt, b.dkey) for b in self.dbufs if b.dcnt > 0 and not b.bg]
        toks += self.retired
        for e in self.engs:
            for t in toks:
                e.wait(t)


def _build(debug_out=None):
    nc = bass.Bass("TRN2", target_bir_lowering=False)
    es = contextlib.ExitStack()
    kb = KB(nc, es)
    pe, dve, act, pool, sp = kb.pe, kb.dve, kb.act, kb.pool, kb.sp
    V, S, T, G = nc.vector, nc.scalar, nc.tensor, nc.gpsimd

    def din(name, shape, dt=F32):
        return nc.dram_tensor(name, list(shape), dt, kind="ExternalInput").ap()

    dbg = debug_out or ()

    def dscr(name, shape, dt):
        kind = "ExternalOutput" if name in dbg else "Internal"
        return nc.dram_tensor(name, list(shape), dt, kind=kind).ap()

    IN_SHAPES = dict(xl=[SEQ, D], ctxl=[CTX, D], ccol=[128, KT, 2], wada=[192, 128, KT * 128], bada=[128, 192],
                     gvec=[128, 4, KT], win=[21, 128, KT * 128], gq=[128, 8], gkv=[128, 4],
                     wuq=[NH, 128, 8 * 256], wuk=[128, 4 * 3072], wuv=[128, 4 * 3072], wout=[KT, 128, KT * 128],
                     wglu=[8, 128, 8 * 128], wffg=[FT, 128, KT * 128], wffu=[FT, 128, KT * 128],
                     wdn=[KT, 128, FT * 128], convw=[128, FT, 4], ropeq=[64, 2, NOWN], ropek=[64, 2, NKEY],
                     ident=[128, 128], s5lam=[128, 2, 64], s5ls=[128, 64], s5b=[128, 64, 2, 16],
                     s5c=[128, 64, 2, 16], s5d=[32, 32])
    declared = {}

    class _In:
        def __getattr__(self, name):
            if name not in declared:
                declared[name] = din(name, IN_SHAPES[name])
            return declared[name]

    I = _In()
    nc._declared_inputs = declared
    out = nc.dram_tensor("out", [2048, D], F32, kind="ExternalOutput").ap()

    uT = dscr("uT", [1024, UCOLS], BF16)
    qcnTs = dscr("qcnTs", [1024, NOWN], BF16)
    KTs = dscr("KTs", [NH, 128, NKEY], BF16)
    Vs = dscr("Vs", [34, 128, 3072], BF16)
    yactT = dscr("yactT", [1024, 2056], BF16)
    s5outT = dscr("s5outT", [1024, NOWN], BF16)
    attT = dscr("attT", [3072, NOWN], BF16)
    xmidT = dscr("xmidT", [D, NOWN], F32)
    hxT = dscr("hxT", [D, NOWN], BF16)
    fTs = dscr("fTs", [5, D, NT], F32)
    B_uT, B_qcn, B_KT, B_V, B_yact, B_s5o, B_att, B_xmid, B_hx, B_f = [Buf(n, dram=True) for n in
                                                                       "uT qcn KT V yact s5o att xmid hx f".split()]
    wffg_b = dscr("wffg_b", [FT, 128, KT * 128], BF16)
    wffu_b = dscr("wffu_b", [FT, 128, KT * 128], BF16)
    wdn_b = dscr("wdn_b", [KT, 128, FT * 128], BF16)
    wout_b = dscr("wout_b", [KT, 128, KT * 128], BF16)
    B_wcast = Buf("wcast", dram=True)
    B_wcast.bg = True
    B_in = Buf("inputs", dram=True)
    B_out = Buf("out", dram=True)

    sbn = [0]

    def sb(st, name, shape, dt):
        sbn[0] += 1
        return st.enter_context(nc.sbuf_tensor(f"s{sbn[0]}_{name}", list(shape), dt))

    ps = [es.enter_context(nc.psum_tensor(f"ps{i}", [128, 512], F32)) for i in range(8)]
    PB = [Buf(f"ps{i}") for i in range(8)]

    ident = sb(es, "ident", [128, 128], F32)
    identb = sb(es, "identb", [128, 128], BF16)
    onesb = sb(es, "onesb", [128, 128], BF16)
    onesf = sb(es, "onesf", [128, 128], F32)
    modc = sb(es, "modc", [128, 192, 2], F32)
    gv = sb(es, "gv", [128, 4, KT], F32)
    vecs = sb(es, "vecs", [128, 8, KT], F32)
    gqs = sb(es, "gqs", [128, 8], F32)
    gkvs = sb(es, "gkvs", [128, 4], F32)
    sc2 = sb(es, "sc2", [128, KT, 2], BF16)
    badas = sb(es, "badas", [128, 192], F32)
    B_c = Buf("consts")
    B_mod = Buf("mod")
    B_vecs = Buf("vecs")
    B_krT = Buf("krT")
    B_kvcn = Buf("kvcn")
    ccs = sb(es, "ccs", [128, KT, 2], F32)
    kvst = contextlib.ExitStack()
    _EXTRA.clear()
    _EXTRA.append(kvst)
    krT = sb(kvst, "krT", [64, NKEY], BF16)
    kvcnT = sb(kvst, "kvcnT", [128, 4, NKEY], BF16)

    def mm(o, l, r, start, stop, reads, writes=(), parts=(), signal=False):
        return kb.op(pe, lambda: T.matmul(o, l, r, start=start, stop=stop), reads=reads, writes=writes,
                     parts=parts, signal=signal)

    kb.dma(sp, ident[:], I.ident, B_c, B_in)
    kb.dma(sp, gv[:], I.gvec, B_c, B_in, part=True)
    kb.dma(sp, gqs[:], I.gq, B_c, B_in, part=True)
    kb.dma(sp, gkvs[:], I.gkv, B_c, B_in, part=True)
    kb.dma(sp, badas[:], I.bada, B_c, B_in, part=True)
    kb.dma(sp, ccs[:], I.ccol, B_c, B_in, part=True)
    kb.op(dve, lambda: V.tensor_copy(identb[:], ident[:]), reads=[B_c], parts=[B_c])
    kb.op(dve, lambda: V.memset(onesb[:], 1.0), parts=[B_c])
    kb.op(dve, lambda: V.memset(onesf[:], 1.0), parts=[B_c])
    kb.op(act, lambda: S.activation(sc2[:], ccs[:], AF.Silu), reads=[B_c], parts=[B_c])

    wst = contextlib.ExitStack()
    NWS = 3
    wslot = [sb(wst, f"wslot{i}", [128, KT * 128], BF16) for i in range(NWS)]
    WB = [Buf(f"wslot{i}") for i in range(NWS)]
    wctr = [0]

    def load_w(src_ap):
        i = wctr[0] % NWS
        wctr[0] += 1
        kb.dma(pool, wslot[i][:], src_ap, WB[i], B_in)
        return wslot[i], WB[i]

    ada_next = [0]

    def adaln(n):
        for _ in range(n):
            m = ada_next[0]
            if m >= 192:
                return
            ada_next[0] += 1
            w, wb = load_w(I.wada[m])
            pb = 7
            for k in range(KT):
                mm(ps[pb][:, 0:2], w[:, k * 128:(k + 1) * 128], sc2[:, k, :], k == 0, k == KT - 1,
                   reads=[wb, B_c], writes=[PB[pb]] if k == 0 else (), parts=() if k == 0 else [PB[pb]],
                   signal=(k == KT - 1))
            kb.op(dve, lambda: V.tensor_scalar(modc[:, m, :], ps[pb][:, 0:2], badas[:, m:m + 1], None, ALU.add),
                  reads=[PB[pb], B_c], parts=[B_mod])

    adaln(64)
    def vec_scale(dst, gi, mlo, col):
        kb.op(dve, lambda: V.scalar_tensor_tensor(vecs[:, dst, :], modc[:, mlo:mlo + KT, col], 1.0, gv[:, gi, :],
                                                  ALU.add, ALU.mult), reads=[B_mod, B_c], parts=[B_vecs])

    def vec_copy(dst, mlo, col):
        kb.op(dve, lambda: V.tensor_copy(vecs[:, dst, :], modc[:, mlo:mlo + KT, col]), reads=[B_mod], parts=[B_vecs])

    def vec_mul(dst, gi, mlo, col):
        kb.op(dve, lambda: V.tensor_tensor(vecs[:, dst, :], modc[:, mlo:mlo + KT, col], gv[:, gi, :], ALU.mult),
              reads=[B_mod, B_c], parts=[B_vecs])

    vec_scale(0, 0, 32, 0)
    vec_copy(1, 0, 0)
    vec_scale(2, 0, 32, 1)
    vec_copy(3, 0, 1)

    if STOP_AFTER == "p0":
        d3 = nc.dram_tensor("dbg_vecs", [128, 8, KT], F32, kind="ExternalOutput").ap()
        kb.dma(sp, d3, vecs[:], B_out, B_vecs)
        wst.close()
        return finish(nc, es, kb, out, B_out)
    st = contextlib.ExitStack()
    xch = [sb(st, f"xch{i}", [128, D], F32) for i in range(2)]
    XB = [Buf(f"xch{i}") for i in range(2)]
    hmod = sb(st, "hmod", [128, KT, 512], BF16)
    B_h = Buf("hmod")
    ssq = sb(st, "ssq", [128, 2], F32)
    rs = sb(st, "rs", [128, 2], F32)
    B_ss = [Buf("ss0"), Buf("ss1")]
    junk = sb(st, "junk", [128, D], BF16)
    B_junk = Buf("junk")
    ust = [sb(st, f"ust{i}", [128, 512], BF16) for i in range(3)]
    UB = [Buf(f"ust{i}") for i in range(3)]
    qcT = sb(st, "qcT", [128, 8, 512], F32)
    B_qc = Buf("qcT")
    sqb = [sb(st, f"sqb{i}", [128, 512], BF16) for i in range(2)]
    SQB = [Buf("sqb0"), Buf("sqb1")]
    rstd = sb(st, "rstd", [128, 512], F32)
    B_rstd = Buf("rstd")
    rtab = sb(st, "rtab", [64, 2, 512], F32)
    B_rtab = Buf("rtab")
    rtmp = sb(st, "rtmp", [64, 512], F32)
    B_rtmp = Buf("rtmp")
    uctr = [0]
    xctr = [0]
    sqctr = [0]

    def p1_tile(kind, t0, w):
        src = I.ctxl if kind == "ctx" else I.xl
        vs, vb = (2, 3) if kind == "ctx" else (0, 1)
        c0 = 0
        while c0 < w:
            cw = min(128, w - c0)
            xi = xctr[0] % 2
            xctr[0] += 1
            xc, xb = xch[xi], XB[xi]
            kb.dma(sp, xc[0:cw, :], src[t0 + c0:t0 + c0 + cw, :], xb, B_in)
            kb.op(act, lambda: S.activation(junk[0:cw, :], xc[0:cw, :], AF.Square, accum_out=ssq[0:cw, xi:xi + 1]),
                  reads=[xb], writes=[B_junk, B_ss[xi]])
            kb.op(act, lambda: S.activation(rs[0:cw, xi:xi + 1], ssq[0:cw, xi:xi + 1], AF.Sqrt, bias=EPS,
                                            scale=1.0 / D), reads=[B_ss[xi]], parts=[B_ss[xi]])
            kb.op(dve, lambda: V.reciprocal(rs[0:cw, xi:xi + 1], rs[0:cw, xi:xi + 1]), reads=[B_ss[xi]],
                  parts=[B_ss[xi]])
            kb.op(dve, lambda: V.tensor_scalar(xc[0:cw, :], xc[0:cw, :], rs[0:cw, xi:xi + 1], None, ALU.mult),
                  reads=[B_ss[xi]], parts=[xb])
            for k4 in range(8):
                pb = k4 % 2
                for j in range(4):
                    k = k4 * 4 + j
                    kb.op(pe, lambda: T.transpose(ps[pb][:, j * 128:j * 128 + cw], xc[0:cw, k * 128:(k + 1) * 128],
                                                  ident[0:cw, 0:cw]),
                          reads=[xb, B_c], writes=[PB[pb]] if j == 0 else (), parts=() if j == 0 else [PB[pb]],
                          signal=(j == 3))
                for j in range(4):
                    k = k4 * 4 + j
                    eng = dve
                    if eng is dve:
                        kb.op(dve, lambda: V.tensor_scalar(hmod[:, k, c0:c0 + cw], ps[pb][:, j * 128:j * 128 + cw],
                                                           vecs[:, vs, k:k + 1], vecs[:, vb, k:k + 1], ALU.mult,
                                                           ALU.add),
                              reads=[PB[pb], B_vecs], parts=[B_h])
                    else:
                        kb.op(act, lambda: S.activation(hmod[:, k, c0:c0 + cw], ps[pb][:, j * 128:j * 128 + cw],
                                                        AF.Identity, bias=vecs[:, vb, k:k + 1],
                                                        scale=vecs[:, vs, k:k + 1]),
                              reads=[PB[pb], B_vecs], parts=[B_h])
            c0 += cw
        mlist = list(range(21)) if kind == "own" else (list(range(8)) + list(range(16, 21)))
        if P1_STAGE == 1:
            return
        if P1_STAGE == 2:
            mlist = [0, 1]
        if P1_STAGE in (3, 4, 5):
            mlist = [0, 1, 16, 17, 18, 19]
        if kind == "ctx":
            keyc = SEQ
        else:
            keyc = t0
        for m in mlist:
            wt, wb = load_w(I.win[m])
            if m < 20:
                pb = 2 + (m % 2)
                for k in range(KT):
                    mm(ps[pb][:, 0:w], wt[:, k * 128:(k + 1) * 128], hmod[:, k, 0:w], k == 0, k == KT - 1,
                       reads=[wb, B_h], writes=[PB[pb]] if k == 0 else (), parts=() if k == 0 else [PB[pb]],
                       signal=(k == KT - 1))
                if m < 8:
                    ui = uctr[0] % 3
                    uctr[0] += 1
                    kb.op(act, lambda: S.copy(ust[ui][:, 0:w], ps[pb][:, 0:w]), reads=[PB[pb]], writes=[UB[ui]])
                    if kind == "ctx":
                        kb.dma(sp, uT[m * 128:(m + 1) * 128, 0:CTX], ust[ui][:, 0:w], B_uT, UB[ui])
                        kb.dma(sp, uT[m * 128:(m + 1) * 128, CTX + SEQ:UCOLS], ust[ui][:, 0:w], B_uT, UB[ui])
                    else:
                        kb.dma(sp, uT[m * 128:(m + 1) * 128, CTX + t0:CTX + t0 + w], ust[ui][:, 0:w], B_uT, UB[ui])
                else:
                    j = m - 8 if m < 16 else m - 16
                    nj = 8 if m < 16 else 4
                    kb.op(dve, lambda: V.tensor_copy(qcT[:, j, 0:w], ps[pb][:, 0:w]), reads=[PB[pb]], parts=[B_qc])
                    si = sqctr[0] % 2
                    sqctr[0] += 1
                    kb.op(act, lambda: S.activation(sqb[si][:, 0:w], qcT[:, j, 0:w], AF.Square), reads=[B_qc],
                          writes=[SQB[si]])
                    if P1_STAGE != 5:
                        mm(ps[4][:, 0:w], onesb[:], sqb[si][:, 0:w], j == 0, j == nj - 1, reads=[SQB[si], B_c],
                           writes=[PB[4]] if j == 0 else (), parts=() if j == 0 else [PB[4]], signal=True)
                    if j == nj - 1 and P1_STAGE not in (4, 5):
                        nfeat = 1024.0 if m < 16 else 512.0
                        kb.op(act, lambda: S.activation(rstd[:, 0:w], ps[4][:, 0:w], AF.Sqrt, bias=EPS,
                                                        scale=1.0 / nfeat), reads=[PB[4]], writes=[B_rstd])
                        kb.op(dve, lambda: V.reciprocal(rstd[:, 0:w], rstd[:, 0:w]), reads=[B_rstd], parts=[B_rstd])
                        for jj in range(nj):
                            if m < 16:
                                ui = uctr[0] % 3
                                uctr[0] += 1
                                kb.op(dve, lambda: V.scalar_tensor_tensor(ust[ui][:, 0:w], qcT[:, jj, 0:w],
                                                                          gqs[:, jj:jj + 1], rstd[:, 0:w], ALU.mult,
                                                                          ALU.mult),
                                      reads=[B_qc, B_rstd, B_c], writes=[UB[ui]])
                                kb.dma(sp, qcnTs[jj * 128:(jj + 1) * 128, t0:t0 + w], ust[ui][:, 0:w], B_qcn, UB[ui])
                            else:
                                kb.op(dve, lambda: V.scalar_tensor_tensor(kvcnT[:, jj, keyc:keyc + w],
                                                                          qcT[:, jj, 0:w], gkvs[:, jj:jj + 1],
                                                                          rstd[:, 0:w], ALU.mult, ALU.mult),
                                      reads=[B_qc, B_rstd, B_c], parts=[B_kvcn])
            else:
                kb.dma(sp, rtab[:, :, 0:w], I.ropek[:, :, keyc:keyc + w], B_rtab, B_in)
                for half in range(2):
                    pb = 5 + half
                    for k in range(KT):
                        mm(ps[pb][0:64, 0:w], wt[:, k * 128 + half * 64:k * 128 + half * 64 + 64], hmod[:, k, 0:w],
                           k == 0, k == KT - 1, reads=[wb, B_h], writes=[PB[pb]] if k == 0 else (),
                           parts=() if k == 0 else [PB[pb]], signal=(k == KT - 1))
                kb.op(dve, lambda: V.tensor_tensor(rtmp[:, 0:w], ps[5][0:64, 0:w], rtab[:, 0, 0:w], ALU.mult),
                      reads=[PB[5], B_rtab], writes=[B_rtmp])
                kb.op(dve, lambda: V.tensor_tensor(rtab[:, 1, 0:w], ps[6][0:64, 0:w], rtab[:, 1, 0:w], ALU.mult),
                      reads=[PB[6], B_rtab], parts=[B_rtab])
                kb.op(dve, lambda: V.tensor_tensor(krT[:, keyc:keyc + w], rtmp[:, 0:w], rtab[:, 1, 0:w], ALU.add),
                      reads=[B_rtmp, B_rtab], parts=[B_krT])

    tiles = [("ctx", 0, CTX)] + [("own", a, b) for a, b in OWN_TILES] + [("rest", a, b) for a, b in REST_TILES]
    if P1_LIMIT is not None:
        tiles = tiles[:P1_LIMIT]
    if "p1" in SKIP:
        tiles = []
    for (kind, t0, w) in tiles:
        p1_tile(kind, t0, w)
        adaln(13)
    adaln(200)
    vec_mul(4, 1, 64, 0)
    vec_scale(5, 2, 128, 0)
    vec_copy(6, 96, 0)
    vec_mul(7, 3, 160, 0)
    kb.barrier()
    st.close()
    wst.close()
    if STOP_AFTER == "p1":
        d1 = nc.dram_tensor("dbg_kvcn", [128, 4, NKEY], BF16, kind="ExternalOutput").ap()
        d2 = nc.dram_tensor("dbg_krT", [64, NKEY], BF16, kind="ExternalOutput").ap()
        d3 = nc.dram_tensor("dbg_vecs", [128, 8, KT], F32, kind="ExternalOutput").ap()
        if P1_LIMIT is None and P1_STAGE == 0:
            kb.dma(sp, d1, kvcnT[:], B_out, B_kvcn)
            kb.dma(sp, d2, krT[:], B_out, B_krT)
        kb.dma(sp, d3, vecs[:], B_out, B_vecs)
        return finish(nc, es, kb, out, B_out)

    st = contextlib.ExitStack()
    wk = sb(st, "wk", [128, 4 * 3072], BF16)
    wv = sb(st, "wv", [128, 4 * 3072], BF16)
    B_wk, B_wv = Buf("wk"), Buf("wv")
    kb.dma(pool, wk[:], I.wuk, B_wk, B_in)
    kb.dma(pool, wv[:], I.wuv, B_wv, B_in)
    kst = [sb(st, f"kst{i}", [128, 512], BF16) for i in range(4)]
    KSB = [Buf(f"kst{i}") for i in range(4)]
    ctr = 0
    for h in range(0 if "p2" in SKIP else NH):
        for c0 in range(0, NKEY, 512):
            w = min(512, NKEY - c0)
            pb = ctr % 2
            si = ctr % 4
            ctr += 1
            for rk in range(4):
                mm(ps[pb][:, 0:w], wk[:, rk * 3072 + h * 128:rk * 3072 + (h + 1) * 128], kvcnT[:, rk, c0:c0 + w],
                   rk == 0, rk == 3, reads=[B_wk, B_kvcn], writes=[PB[pb]] if rk == 0 else (),
                   parts=() if rk == 0 else [PB[pb]], signal=(rk == 3))
            if ctr % 2 == 0:
                kb.op(act, lambda: S.copy(kst[si][:, 0:w], ps[pb][:, 0:w]), reads=[PB[pb]], writes=[KSB[si]])
            else:
                kb.op(dve, lambda: V.tensor_copy(kst[si][:, 0:w], ps[pb][:, 0:w]), reads=[PB[pb]], writes=[KSB[si]])
            kb.dma(sp, KTs[h, :, c0:c0 + w], kst[si][:, 0:w], B_KT, KSB[si])
    for kt in range(0 if "p2" in SKIP else 34):
        for hg in range(6):
            pb = ctr % 2
            si = ctr % 4
            ctr += 1
            for rk in range(4):
                mm(ps[pb][:, 0:512], kvcnT[:, rk, kt * 128:(kt + 1) * 128],
                   wv[:, rk * 3072 + hg * 512:rk * 3072 + (hg + 1) * 512], rk == 0, rk == 3,
                   reads=[B_wv, B_kvcn], writes=[PB[pb]] if rk == 0 else (), parts=() if rk == 0 else [PB[pb]],
                   signal=(rk == 3))
            if ctr % 2 == 0:
                kb.op(act, lambda: S.copy(kst[si][:, :], ps[pb][:, :]), reads=[PB[pb]], writes=[KSB[si]])
            else:
                kb.op(dve, lambda: V.tensor_copy(kst[si][:, :], ps[pb][:, :]), reads=[PB[pb]], writes=[KSB[si]])
            kb.dma(sp, Vs[kt, :, hg * 512:(hg + 1) * 512], kst[si][:, :], B_V, KSB[si])
    kb.barrier()
    st.close()
    if STOP_AFTER == "p2":
        return finish(nc, es, kb, out, B_out)

    st = contextlib.ExitStack()
    TWO_PI = 6.283185307179586
    lam = sb(st, "lam", [128, 2, 64], F32)
    lsd = sb(st, "lsd", [128, 64], F32)
    Ball = sb(st, "Ball", [128, 64, 2, 32], F32)
    Call = sb(st, "Call", [128, 64, 2, 32], F32)
    dcol = sb(st, "dcol", [32, 32], F32)
    B_s5 = Buf("s5setup")
    B_BC = Buf("BC")
    kb.dma(sp, lam[:], I.s5lam, B_s5, B_in)
    kb.dma(sp, lsd[:], I.s5ls, B_s5, B_in, part=True)
    kb.dma(sp, dcol[:], I.s5d, B_s5, B_in, part=True)
    kb.op(dve, lambda: V.memset(Ball[:], 0.0), writes=[B_BC])
    kb.op(dve, lambda: V.memset(Call[:], 0.0), parts=[B_BC])
    kb.dma(sp, Ball[0:64, :, :, 0:16], I.s5b[0:64], B_BC, B_in)
    kb.dma(sp, Ball[64:128, :, :, 16:32], I.s5b[64:128], B_BC, B_in, part=True)
    kb.dma(sp, Call[0:64, :, :, 0:16], I.s5c[0:64], B_BC, B_in, part=True)
    kb.dma(sp, Call[64:128, :, :, 16:32], I.s5c[64:128], B_BC, B_in, part=True)
    nsc = [0]

    def stile(dt=F32, shape=(128, 64)):
        nsc[0] += 1
        return sb(st, f"s5t{nsc[0]}", list(shape), dt)

    def dv(f):
        kb.op(dve, f, reads=[B_s5], parts=[B_s5])

    def ac(f):
        kb.op(act, f, reads=[B_s5], parts=[B_s5])

    lr, li = lam[:, 0, :], lam[:, 1, :]
    dtt, mag, ang, nf, s2, s4, ch, sinr, cosr, t1, t2 = [stile() for _ in range(11)]
    ni = stile(mybir.dt.int32)
    ac(lambda: S.activation(dtt[:], lsd[:], AF.Exp))
    dv(lambda: V.tensor_tensor(t1[:], lr, dtt[:], ALU.mult))
    ac(lambda: S.activation(mag[:], t1[:], AF.Exp))
    dv(lambda: V.tensor_tensor(ang[:], li, dtt[:], ALU.mult))
    dv(lambda: V.tensor_scalar(t1[:], ang[:], 1.0 / TWO_PI, None, ALU.mult))
    dv(lambda: V.tensor_copy(ni[:], t1[:]))
    dv(lambda: V.tensor_copy(nf[:], ni[:]))
    dv(lambda: V.scalar_tensor_tensor(t2[:], nf[:], -TWO_PI, ang[:], ALU.mult, ALU.add))
    ac(lambda: S.activation(s2[:], t2[:], AF.Sin, scale=0.5))
    ac(lambda: S.activation(s4[:], t2[:], AF.Sin, scale=0.25))
    dv(lambda: V.tensor_tensor(t1[:], s4[:], s4[:], ALU.mult))
    dv(lambda: V.tensor_scalar(ch[:], t1[:], -2.0, 1.0, ALU.mult, ALU.add))
    dv(lambda: V.tensor_tensor(t1[:], s2[:], ch[:], ALU.mult))
    dv(lambda: V.tensor_scalar(sinr[:], t1[:], 2.0, None, ALU.mult))
    dv(lambda: V.tensor_tensor(t1[:], s2[:], s2[:], ALU.mult))
    dv(lambda: V.tensor_scalar(cosr[:], t1[:], -2.0, 1.0, ALU.mult, ALU.add))
    apw = sb(st, "apw", [128, 9, 2, 64], F32)
    napw = sb(st, "napw", [128, 9, 2, 64], F32)
    lev = sb(st, "lev", [128, 10, 2, 64], F32)
    nlev = sb(st, "nlev", [128, 10, 64], F32)
    ff = sb(st, "ff", [128, 2, 64], F32)
    nfi = stile()
    dv(lambda: V.memset(apw[:, 0, 0, :], 1.0))
    dv(lambda: V.memset(apw[:, 0, 1, :], 0.0))
    dv(lambda: V.tensor_tensor(apw[:, 1, 0, :], mag[:], cosr[:], ALU.mult))
    dv(lambda: V.tensor_tensor(apw[:, 1, 1, :], mag[:], sinr[:], ALU.mult))
    ar, ai = apw[:, 1, 0, :], apw[:, 1, 1, :]
    nr, den = stile(), stile()
    dv(lambda: V.tensor_scalar(nr[:], ar, -1.0, None, ALU.add))
    dv(lambda: V.tensor_tensor(t1[:], lr, lr, ALU.mult))
    dv(lambda: V.tensor_tensor(t2[:], li, li, ALU.mult))
    dv(lambda: V.tensor_tensor(den[:], t1[:], t2[:], ALU.add))
    dv(lambda: V.reciprocal(den[:], den[:]))
    dv(lambda: V.tensor_tensor(t1[:], nr[:], lr, ALU.mult))
    dv(lambda: V.tensor_tensor(t2[:], ai, li, ALU.mult))
    dv(lambda: V.tensor_tensor(t1[:], t1[:], t2[:], ALU.add))
    dv(lambda: V.tensor_tensor(ff[:, 0, :], t1[:], den[:], ALU.mult))
    dv(lambda: V.tensor_tensor(t1[:], ai, lr, ALU.mult))
    dv(lambda: V.tensor_tensor(t2[:], nr[:], li, ALU.mult))
    dv(lambda: V.tensor_tensor(t1[:], t1[:], t2[:], ALU.subtract))
    dv(lambda: V.tensor_tensor(ff[:, 1, :], t1[:], den[:], ALU.mult))
    dv(lambda: V.tensor_scalar(nfi[:], ff[:, 1, :], -1.0, None, ALU.mult))

    def cmul(o_r, o_i, a_r, a_i, b_r, b_i):
        dv(lambda: V.tensor_tensor(t1[:], a_r, b_r, ALU.mult))
        dv(lambda: V.tensor_tensor(t2[:], a_i, b_i, ALU.mult))
        dv(lambda: V.tensor_tensor(den[:], a_r, b_i, ALU.mult))
        dv(lambda: V.tensor_tensor(nr[:], a_i, b_r, ALU.mult))
        dv(lambda: V.tensor_tensor(o_r, t1[:], t2[:], ALU.subtract))
        dv(lambda: V.tensor_tensor(o_i, den[:], nr[:], ALU.add))

    for k in range(2, 9):
        cmul(apw[:, k, 0, :], apw[:, k, 1, :], apw[:, k - 1, 0, :], apw[:, k - 1, 1, :], ar, ai)
    dv(lambda: V.tensor_scalar(napw[:], apw[:], -1.0, None, ALU.mult))
    dv(lambda: V.tensor_copy(lev[:, 0, :, :], apw[:, 8, :, :]))
    for l in range(1, 10):
        cmul(lev[:, l, 0, :], lev[:, l, 1, :], lev[:, l - 1, 0, :], lev[:, l - 1, 1, :], lev[:, l - 1, 0, :],
             lev[:, l - 1, 1, :])
    dv(lambda: V.tensor_scalar(nlev[:], lev[:, :, 1, :], -1.0, None, ALU.mult))

    uTp = [sb(st, f"uTp{i}", [32, UCOLS], BF16) for i in range(2)]
    B_uTp = [Buf("uTp0"), Buf("uTp1")]
    bbar = sb(st, "bbar", [128, 2, 32], F32)
    B_bbar = Buf("bbar")
    Eb = sb(st, "Eb", [128, 8, 2, 32], BF16)
    B_Eb = Buf("Eb")
    Fb = [sb(st, f"Fb{i}", [128, 8, 2, 32], BF16) for i in range(2)]
    B_Fb = [Buf("Fb0"), Buf("Fb1")]
    Cb = sb(st, "Cb", [128, 2, 32], BF16)
    B_Cb = Buf("Cb")
    Bw = [sb(st, f"Bw{i}", [32, 8, 2, 128], BF16) for i in range(2)]
    B_Bw = [Buf("Bw0"), Buf("Bw1")]
    Kt = [sb(st, f"Kt{i}", [32, 8, 32], BF16) for i in range(2)]
    B_Kt = [Buf("Kt0"), Buf("Kt1")]
    K0 = sb(st, "K0", [32, 32], BF16)
    K0f = sb(st, "K0f", [32, 32], F32)
    B_K0 = Buf("K0")
    tmpE = [sb(st, f"tmpE{i}", [128, 32], F32) for i in range(4)]
    B_tmpE = [Buf(f"tmpE{i}") for i in range(4)]
    XA = [sb(st, f"XA{i}", [128, 2, NBA], F32) for i in range(2)]
    XBt = [sb(st, f"XB{i}", [128, 2, NBB], F32) for i in range(2)]
    B_XA = [Buf("XA0"), Buf("XA1")]
    B_XB = [Buf("XB0"), Buf("XB1")]
    SA = sb(st, "SA", [128, 2, NBA], BF16)
    SB_ = sb(st, "SB", [128, 2, NBB], BF16)
    B_SA, B_SB = Buf("SA"), Buf("SB")
    ys = [sb(st, f"ys{i}", [32, 512], F32) for i in range(3)]
    B_ys = [Buf(f"ys{i}") for i in range(3)]
    yo = [sb(st, f"yo{i}", [32, 512], BF16) for i in range(2)]
    B_yo = [Buf("yo0"), Buf("yo1")]
    yfs = [sb(st, f"yf{i}", [32, 512], F32) for i in range(2)]
    B_yfs = [Buf("yf0"), Buf("yf1")]
    tec = [0]
    psb6 = ps[6][:].bitcast(BF16)

    def two_term(out_ap, a_ap, sa, b_ap, sbb, reads, out_buf, part=True):
        i = tec[0] % 4
        tec[0] += 1
        kb.op(dve, lambda: V.tensor_scalar(tmpE[i][:], b_ap, sbb, None, ALU.mult), reads=reads + [B_s5],
              writes=[B_tmpE[i]])
        kb.op(dve, lambda: V.scalar_tensor_tensor(out_ap, a_ap, sa, tmpE[i][:], ALU.mult, ALU.add),
              reads=reads + [B_s5, B_tmpE[i]], parts=[out_buf])

    def hs_scan(X, BX, nblk, dp, forward):
        cur, s, l = 0, 1, 0
        while s < nblk:
            Pr, Pi, nPi = lev[:, l, 0, dp:dp + 1], lev[:, l, 1, dp:dp + 1], nlev[:, l, dp:dp + 1]
            o, n = X[cur], X[1 - cur]
            bo, bn = BX[cur], BX[1 - cur]
            if forward:
                d0, d1, s0, s1, k0, k1 = s, nblk, 0, nblk - s, 0, s
            else:
                d0, d1, s0, s1, k0, k1 = 0, nblk - s, s, nblk, nblk - s, nblk
            kb.op(dve, lambda: V.scalar_tensor_tensor(n[:, 0, d0:d1], o[:, 0, s0:s1], Pr, o[:, 0, d0:d1], ALU.mult,
                                                      ALU.add), reads=[bo, B_s5], writes=[bn])
            yield None
            kb.op(dve, lambda: V.scalar_tensor_tensor(n[:, 1, d0:d1], o[:, 0, s0:s1], Pi, o[:, 1, d0:d1], ALU.mult,
                                                      ALU.add), reads=[bo, B_s5], parts=[bn])
            yield None
            kb.op(dve, lambda: V.scalar_tensor_tensor(n[:, 0, d0:d1], o[:, 1, s0:s1], nPi, n[:, 0, d0:d1], ALU.mult,
                                                      ALU.add), reads=[bo, B_s5, bn], parts=[bn])
            yield None
            kb.op(dve, lambda: V.scalar_tensor_tensor(n[:, 1, d0:d1], o[:, 1, s0:s1], Pr, n[:, 1, d0:d1], ALU.mult,
                                                      ALU.add), reads=[bo, B_s5, bn], parts=[bn])
            kb.op(act, lambda: S.copy(n[:, :, k0:k1], o[:, :, k0:k1]), reads=[bo], parts=[bn])
            yield None
            cur, s, l = 1 - cur, s * 2, l + 1
        yield ("done", cur)

    YCH = [(0, 64), (64, 64), (128, 64), (192, 64), (256, 1)]
    ychk = [0]
    for pair in range(0 if "p3" in SKIP else 32):
        ui = pair % 2
        kb.dma(sp, uTp[ui][:], uT[pair * 32:(pair + 1) * 32, :], B_uTp[ui], B_uT)
        for dr in range(2):
            dp = dr * 32 + pair
            Br, Bi = Ball[:, dp, 0, :], Ball[:, dp, 1, :]
            Cr, Ci = Call[:, dp, 0, :], Call[:, dp, 1, :]
            fr, fi, nfi_ = ff[:, 0, dp:dp + 1], ff[:, 1, dp:dp + 1], nfi[:, dp:dp + 1]
            kb.op(dve, lambda: V.tensor_scalar(bbar[:, 0, :], Br, fr, None, ALU.mult), reads=[B_BC, B_s5],
                  writes=[B_bbar])
            kb.op(dve, lambda: V.scalar_tensor_tensor(bbar[:, 0, :], Bi, nfi_, bbar[:, 0, :], ALU.mult, ALU.add),
                  reads=[B_BC, B_s5, B_bbar], parts=[B_bbar])
            kb.op(dve, lambda: V.tensor_scalar(bbar[:, 1, :], Br, fi, None, ALU.mult), reads=[B_BC, B_s5],
                  parts=[B_bbar])
            kb.op(dve, lambda: V.scalar_tensor_tensor(bbar[:, 1, :], Bi, fr, bbar[:, 1, :], ALU.mult, ALU.add),
                  reads=[B_BC, B_s5, B_bbar], parts=[B_bbar])
            for k in range(8):
                akr, aki, naki = apw[:, k, 0, dp:dp + 1], apw[:, k, 1, dp:dp + 1], napw[:, k, 1, dp:dp + 1]
                two_term(Eb[:, k, 0, :], bbar[:, 0, :], akr, bbar[:, 1, :], naki, [B_bbar], B_Eb)
                two_term(Eb[:, k, 1, :], bbar[:, 0, :], aki, bbar[:, 1, :], akr, [B_bbar], B_Eb)
            for k in range(1, 9):
                akr, aki = apw[:, k, 0, dp:dp + 1], apw[:, k, 1, dp:dp + 1]
                nakr, naki = napw[:, k, 0, dp:dp + 1], napw[:, k, 1, dp:dp + 1]
                two_term(Fb[dr][:, k - 1, 0, :], Cr, akr, Ci, naki, [B_BC], B_Fb[dr])
                two_term(Fb[dr][:, k - 1, 1, :], Cr, naki, Ci, nakr, [B_BC], B_Fb[dr])
            kb.op(dve, lambda: V.tensor_copy(Cb[:, 0, :], Cr), reads=[B_BC], parts=[B_Cb])
            kb.op(dve, lambda: V.tensor_scalar(Cb[:, 1, :], Ci, -1.0, None, ALU.mult), reads=[B_BC], parts=[B_Cb])
            for ri in range(2):
                for sg_ in range(8):
                    k = (7 - sg_) if dr == 0 else sg_
                    kb.op(pe, lambda: T.transpose(psb6[0:32, sg_ * 128:(sg_ + 1) * 128], Eb[:, k, ri, :], identb[:, :]),
                          reads=[B_Eb, B_c], writes=[PB[6]] if sg_ == 0 else (), parts=() if sg_ == 0 else [PB[6]],
                          signal=(sg_ == 7))
                kb.op(act, lambda: S.copy(Bw[dr][:, :, ri, :], psb6[0:32, 0:1024].rearrange("p (s c) -> p s c", c=128)),
                      reads=[PB[6]], parts=[B_Bw[dr]])
            for tau in range(8):
                mm(ps[7][0:32, tau * 32:(tau + 1) * 32], Eb[:, tau, 0, :], Cb[:, 0, :], True, False,
                   reads=[B_Eb, B_Cb], writes=[PB[7]] if tau == 0 else (), parts=() if tau == 0 else [PB[7]])
                mm(ps[7][0:32, tau * 32:(tau + 1) * 32], Eb[:, tau, 1, :], Cb[:, 1, :], False, True,
                   reads=[B_Eb, B_Cb], parts=[PB[7]], signal=(tau == 7))
            kb.op(dve, lambda: V.tensor_copy(Kt[dr][:], ps[7][0:32, 0:256].rearrange("p (t c) -> p t c", c=32)),
                  reads=[PB[7]], writes=[B_Kt[dr]])
            if dr == 0:
                kb.op(dve, lambda: V.tensor_copy(K0f[:], ps[7][0:32, 0:32]), reads=[PB[7]], writes=[B_K0])
            else:
                kb.op(dve, lambda: V.tensor_tensor(K0f[:], K0f[:], ps[7][0:32, 0:32], ALU.add), reads=[PB[7], B_K0],
                      parts=[B_K0])
                kb.op(dve, lambda: V.scalar_tensor_tensor(K0f[:], ident[0:32, 0:32], dcol[:, pair:pair + 1], K0f[:],
                                                          ALU.mult, ALU.add), reads=[B_K0, B_c, B_s5], parts=[B_K0])
                kb.op(dve, lambda: V.tensor_copy(K0[:], K0f[:]), reads=[B_K0], parts=[B_K0])
        for ri in range(2):
            for sg_ in range(8):
                mm(ps[ri][:, 0:NBA], Bw[0][:, sg_, ri, :], uTp[ui][:, sg_:8 * NBA:8], sg_ == 0, sg_ == 7,
                   reads=[B_Bw[0], B_uTp[ui]], writes=[PB[ri]] if sg_ == 0 else (), parts=() if sg_ == 0 else [PB[ri]],
                   signal=(sg_ == 7))
            kb.op(act if ri == 0 else dve,
                  (lambda: S.copy(XA[0][:, ri, :], ps[ri][:, 0:NBA])) if ri == 0 else
                  (lambda: V.tensor_copy(XA[0][:, ri, :], ps[ri][:, 0:NBA])),
                  reads=[PB[ri]], writes=[B_XA[0]] if ri == 0 else (), parts=() if ri == 0 else [B_XA[0]])
        for ri in range(2):
            for c in range(2):
                pbk = 2 + 2 * ri + c
                base = CTX + 8 * 272 * c
                for sg_ in range(8):
                    mm(ps[pbk][:, 0:272], Bw[1][:, sg_, ri, :], uTp[ui][:, base + sg_:base + 8 * 272:8], sg_ == 0,
                       sg_ == 7, reads=[B_Bw[1], B_uTp[ui]], writes=[PB[pbk]] if sg_ == 0 else (),
                       parts=() if sg_ == 0 else [PB[pbk]], signal=(sg_ == 7))
                first = (ri == 0 and c == 0)
                kb.op(dve, lambda: V.tensor_copy(XBt[0][:, ri, 272 * c:272 * (c + 1)], ps[pbk][:, 0:272]),
                      reads=[PB[pbk]], writes=[B_XB[0]] if first else (), parts=() if first else [B_XB[0]])
        gens = [hs_scan(XA, B_XA, NBA, pair, True), hs_scan(XBt, B_XB, NBB, 32 + pair, False)]
        res_ = [None, None]
        while any(r_ is None for r_ in res_):
            for gi_ in range(2):
                if res_[gi_] is None:
                    v_ = next(gens[gi_])
                    if v_ is not None:
                        res_[gi_] = v_[1]
        ca, cb = res_
        kb.op(dve, lambda: V.tensor_copy(SA[:], XA[ca][:]), reads=[B_XA[ca]], writes=[B_SA])
        kb.op(dve, lambda: V.tensor_copy(SB_[:], XBt[cb][:]), reads=[B_XB[cb]], writes=[B_SB])
        for (jb0, nb) in YCH:
            yb = 6 + (ychk[0] % 2)
            ychk[0] += 1
            for sg_ in range(8):
                o_ap = ps[yb][0:32, sg_:8 * nb:8]
                kA, kB = sg_, 7 - sg_
                mm(o_ap, Fb[0][:, kA, 0, :], SA[:, 0, 31 + jb0:31 + jb0 + nb], True, False,
                   reads=[B_Fb[0], B_SA], writes=[PB[yb]] if sg_ == 0 else (), parts=() if sg_ == 0 else [PB[yb]])
                mm(o_ap, Fb[0][:, kA, 1, :], SA[:, 1, 31 + jb0:31 + jb0 + nb], False, False,
                   reads=[B_Fb[0], B_SA], parts=[PB[yb]])
                mm(o_ap, Fb[1][:, kB, 0, :], SB_[:, 0, jb0 + 1:jb0 + 1 + nb], False, False,
                   reads=[B_Fb[1], B_SB], parts=[PB[yb]])
                mm(o_ap, Fb[1][:, kB, 1, :], SB_[:, 1, jb0 + 1:jb0 + 1 + nb], False, False,
                   reads=[B_Fb[1], B_SB], parts=[PB[yb]])
                for sp_ in range(8):
                    if sp_ < sg_:
                        l_ap = Kt[0][:, sg_ - sp_, :]
                    elif sp_ > sg_:
                        l_ap = Kt[1][:, sp_ - sg_, :]
                    else:
                        l_ap = K0[:, :]
                    c0 = CTX + 8 * jb0 + sp_
                    mm(o_ap, l_ap, uTp[ui][:, c0:CTX + 8 * (jb0 + nb):8], False, sp_ == 7,
                       reads=[B_Kt[0], B_Kt[1], B_K0, B_uTp[ui]], parts=[PB[yb]], signal=(sp_ == 7 and sg_ == 7))
            n8 = 8 * nb
            yi = ychk[0] % 3
            yf = yfs[ychk[0] % 2]
            B_yf = B_yfs[ychk[0] % 2]
            kb.op(dve, lambda: V.tensor_copy(yf[:, 0:n8], ps[yb][0:32, 0:n8]), reads=[PB[yb]], writes=[B_yf])
            kb.op(act, lambda: S.activation(ys[yi][:, 0:n8], yf[:, 0:n8], AF.Square), reads=[B_yf],
                  writes=[B_ys[yi]])
            kb.op(dve, lambda: V.tensor_scalar(ys[yi][:, 0:n8], ys[yi][:, 0:n8], 0.044715, 1.0, ALU.mult, ALU.add),
                  reads=[B_ys[yi]], parts=[B_ys[yi]])
            kb.op(dve, lambda: V.tensor_tensor(ys[yi][:, 0:n8], ys[yi][:, 0:n8], yf[:, 0:n8], ALU.mult),
                  reads=[B_ys[yi], B_yf], parts=[B_ys[yi]])
            kb.op(act, lambda: S.activation(ys[yi][:, 0:n8], ys[yi][:, 0:n8], AF.Sigmoid, scale=1.5957691216),
                  reads=[B_ys[yi]], parts=[B_ys[yi]])
            oi = ychk[0] % 2
            kb.op(dve, lambda: V.tensor_tensor(yo[oi][:, 0:n8], ys[yi][:, 0:n8], yf[:, 0:n8], ALU.mult),
                  reads=[B_ys[yi], B_yf], writes=[B_yo[oi]])
            kb.dma(sp, yactT[pair * 32:(pair + 1) * 32, 8 * jb0:8 * jb0 + n8], yo[oi][:, 0:n8], B_yact, B_yo[oi])
    kb.barrier()
    st.close()
    if STOP_AFTER == "p3a":
        return finish(nc, es, kb, out, B_out)
    st = contextlib.ExitStack()
    wg = sb(st, "wg", [128, 8, 8 * 128], BF16)
    B_wg = Buf("wg")
    for m in range(8):
        kb.dma(pool, wg[:, m, :], I.wglu[m], B_wg, B_in, part=(m > 0))
    ya = [sb(st, f"ya{i}", [128, 8, NT], BF16) for i in range(2)]
    B_ya = [Buf("ya0"), Buf("ya1")]
    sgt = [sb(st, f"sgt{i}", [128, NT], F32) for i in range(2)]
    B_sgt = [Buf("sgt0"), Buf("sgt1")]
    go = [sb(st, f"go{i}", [128, NT], BF16) for i in range(2)]
    B_go = [Buf("go0"), Buf("go1")]
    gctr = 0
    for ti, (t0, w) in enumerate([] if "p3" in SKIP else OWN_TILES):
        yi = ti % 2
        for k in range(8):
            kb.dma(sp, ya[yi][:, k, 0:w], yactT[k * 128:(k + 1) * 128, t0:t0 + w], B_ya[yi], B_yact, part=(k > 0))
        for m in range(8):
            pb = m % 2
            for k in range(8):
                mm(ps[pb][:, 0:w], wg[:, m, k * 128:(k + 1) * 128], ya[yi][:, k, 0:w], k == 0, k == 7,
                   reads=[B_wg, B_ya[yi]], writes=[PB[pb]] if k == 0 else (), parts=() if k == 0 else [PB[pb]],
                   signal=(k == 7))
            gi = gctr % 2
            gctr += 1
            kb.op(act, lambda: S.activation(sgt[gi][:, 0:w], ps[pb][:, 0:w], AF.Sigmoid), reads=[PB[pb]],
                  writes=[B_sgt[gi]])
            kb.op(dve, lambda: V.tensor_tensor(go[gi][:, 0:w], sgt[gi][:, 0:w], ya[yi][:, m, 0:w], ALU.mult),
                  reads=[B_sgt[gi], B_ya[yi]], writes=[B_go[gi]])
            kb.dma(sp, s5outT[m * 128:(m + 1) * 128, t0:t0 + w], go[gi][:, 0:w], B_s5o, B_go[gi])
    kb.barrier()
    st.close()
    if STOP_AFTER == "p3":
        return finish(nc, es, kb, out, B_out)

    st = contextlib.ExitStack()
    qcn = sb(st, "qcn", [128, 8, NOWN], BF16)
    B_qcnS = Buf("qcnS")
    for j in range(0 if "p4" in SKIP else 8):
        kb.dma(sp, qcn[:, j, :], qcnTs[j * 128:(j + 1) * 128, :], B_qcnS, B_qcn, part=(j > 0))
    rq = sb(st, "rq", [64, 2, NOWN], F32)
    B_rq = Buf("rq")
    kb.dma(sp, rq[:], I.ropeq, B_rq, B_in)
    kth = [sb(st, f"kth{i}", [128, NKEY], BF16) for i in range(2)]
    vh = [sb(st, f"vh{i}", [128, 34, 128], BF16) for i in range(2)]
    wq = [sb(st, f"wq{i}", [128, 8 * 256], BF16) for i in range(2)]
    B_kth = [Buf("kth0"), Buf("kth1")]
    B_vh = [Buf("vh0"), Buf("vh1")]
    B_wq = [Buf("wq0"), Buf("wq1")]
    qn = [sb(st, f"qn{i}", [128, NT], BF16) for i in range(2)]
    qr = [sb(st, f"qr{i}", [64, NT], BF16) for i in range(2)]
    B_qn = [Buf("qn0"), Buf("qn1")]
    B_qr = [Buf("qr0"), Buf("qr1")]
    qtmp = sb(st, "qtmp", [64, NT], F32)
    qtmp2 = sb(st, "qtmp2", [64, NT], F32)
    B_qtmp, B_qtmp2 = Buf("qtmp"), Buf("qtmp2")
    NPS = 4
    pT = [sb(st, f"pT{i}", [128, NT], BF16) for i in range(NPS)]
    B_pT = [Buf(f"pT{i}") for i in range(NPS)]
    accs = [sb(st, f"acc{i}", [128, NT], F32) for i in range(2)]
    B_accs = [Buf("acc0"), Buf("acc1")]
    rinv = sb(st, "rinv", [128, NT], F32)
    B_rinv = Buf("rinv")
    ast = [sb(st, f"ast{i}", [128, NT], BF16) for i in range(2)]
    B_ast = [Buf("ast0"), Buf("ast1")]

    def load_head(h):
        i = h % 2
        kb.dma(sp, kth[i][:], KTs[h], B_kth[i], B_KT)
        kb.dma(sp, vh[i][:], Vs[:, :, h * 128:(h + 1) * 128].rearrange("k p d -> p k d"), B_vh[i], B_V)
        kb.dma(pool, wq[i][:], I.wuq[h], B_wq[i], B_in)

    work = [(h, t0, w) for h in range(0 if "p4" in SKIP else NH) for (t0, w) in OWN_TILES]

    def emit_qproj(idx):
        h, t0, w = work[idx]
        hi, qi = h % 2, idx % 2
        for k in range(8):
            mm(ps[0][:, 0:w], wq[hi][:, k * 256:k * 256 + 128], qcn[:, k, t0:t0 + w], k == 0, k == 7,
               reads=[B_wq[hi], B_qcnS], writes=[PB[0]] if k == 0 else (), parts=() if k == 0 else [PB[0]],
               signal=(k == 7))
        for half in range(2):
            pb = 1 + half
            for k in range(8):
                mm(ps[pb][0:64, 0:w], wq[hi][:, k * 256 + 128 + half * 64:k * 256 + 192 + half * 64],
                   qcn[:, k, t0:t0 + w], k == 0, k == 7, reads=[B_wq[hi], B_qcnS],
                   writes=[PB[pb]] if k == 0 else (), parts=() if k == 0 else [PB[pb]], signal=(k == 7))
        kb.op(act, lambda: S.copy(qn[qi][:, 0:w], ps[0][:, 0:w]), reads=[PB[0]], writes=[B_qn[qi]])
        kb.op(dve, lambda: V.tensor_tensor(qtmp[:, 0:w], ps[1][0:64, 0:w], rq[:, 0, t0:t0 + w], ALU.mult),
              reads=[PB[1], B_rq], writes=[B_qtmp])
        kb.op(dve, lambda: V.tensor_tensor(qtmp2[:, 0:w], ps[2][0:64, 0:w], rq[:, 1, t0:t0 + w], ALU.mult),
              reads=[PB[2], B_rq], writes=[B_qtmp2])
        kb.op(dve, lambda: V.tensor_tensor(qr[qi][:, 0:w], qtmp[:, 0:w], qtmp2[:, 0:w], ALU.add),
              reads=[B_qtmp, B_qtmp2], writes=[B_qr[qi]])

    if work:
        load_head(0)
    cast_jobs = []
    for m in range(KT):
        cast_jobs.append((wout_b[m].rearrange("p (a b) -> (p a) b", b=2048),
                          I.wout[m].rearrange("p (a b) -> (p a) b", b=2048)))
    for ft in range(FT):
        cast_jobs.append((wffg_b[ft].rearrange("p (a b) -> (p a) b", b=2048),
                          I.wffg[ft].rearrange("p (a b) -> (p a) b", b=2048)))
        cast_jobs.append((wffu_b[ft].rearrange("p (a b) -> (p a) b", b=2048),
                          I.wffu[ft].rearrange("p (a b) -> (p a) b", b=2048)))
    for m in range(KT):
        cast_jobs.append((wdn_b[m].rearrange("p (a b) -> (p a) b", b=1376),
                          I.wdn[m].rearrange("p (a b) -> (p a) b", b=1376)))

    def issue_casts(n):
        for _ in range(n):
            if cast_jobs:
                o_, i_ = cast_jobs.pop(0)
                kb.dma(pool, o_, i_, B_wcast, B_in)

    if work:
        emit_qproj(0)
    pctr = 0
    ob = 3
    for idx, (h, t0, w) in enumerate(work):
        hi, qi = h % 2, idx % 2
        if t0 == 0 and h + 1 < NH:
            load_head(h + 1)
        issue_casts(2)

        def score(kt):
            sbk = 4 + (kt % 3)
            mm(ps[sbk][:, 0:w], kth[hi][:, kt * 128:(kt + 1) * 128], qn[qi][:, 0:w], True, False,
               reads=[B_kth[hi], B_qn[qi]], writes=[PB[sbk]])
            mm(ps[sbk][:, 0:w], krT[:, kt * 128:(kt + 1) * 128], qr[qi][:, 0:w], False, True,
               reads=[B_krT, B_qr[qi]], parts=[PB[sbk]], signal=True)

        score(0)
        score(1)
        for kt in range(34):
            sbk = 4 + (kt % 3)
            pi = pctr % NPS
            pctr += 1
            kb.op(act, lambda: S.activation(pT[pi][:, 0:w], ps[sbk][:, 0:w], AF.Exp, scale=MLA_SCALE),
                  reads=[PB[sbk]], writes=[B_pT[pi]])
            acc, B_acc = accs[kt % 2], B_accs[kt % 2]
            if kt < 2:
                kb.op(dve, lambda: V.tensor_copy(acc[:, 0:w], pT[pi][:, 0:w]), reads=[B_pT[pi]], writes=[B_acc])
            else:
                kb.op(dve, lambda: V.tensor_tensor(acc[:, 0:w], acc[:, 0:w], pT[pi][:, 0:w], ALU.add),
                      reads=[B_pT[pi], B_acc], parts=[B_acc])
            if kt + 2 < 34:
                score(kt + 2)
            if kt == 12 and idx + 1 < len(work):
                emit_qproj(idx + 1)
            mm(ps[ob][:, 0:w], vh[hi][:, kt, :], pT[pi][:, 0:w], kt == 0, kt == 33,
               reads=[B_vh[hi], B_pT[pi]], writes=[PB[ob]] if kt == 0 else (), parts=() if kt == 0 else [PB[ob]],
               signal=(kt == 33))
        mm(ps[7][:, 0:w], onesf[:], accs[0][:, 0:w], True, False, reads=[B_c, B_accs[0]], writes=[PB[7]])
        mm(ps[7][:, 0:w], onesf[:], accs[1][:, 0:w], False, True, reads=[B_c, B_accs[1]], parts=[PB[7]],
           signal=True)
        kb.op(dve, lambda: V.reciprocal(rinv[:, 0:w], ps[7][:, 0:w]), reads=[PB[7]], writes=[B_rinv])
        ai = idx % 2
        kb.op(dve, lambda: V.tensor_tensor(ast[ai][:, 0:w], ps[ob][:, 0:w], rinv[:, 0:w], ALU.mult),
              reads=[PB[ob], B_rinv], writes=[B_ast[ai]])
        kb.dma(sp, attT[h * 128:(h + 1) * 128, t0:t0 + w], ast[ai][:, 0:w], B_att, B_ast[ai])
    issue_casts(10000)
    kb.barrier()
    st.close()
    kvst.close()
    if STOP_AFTER == "p4":
        return finish(nc, es, kb, out, B_out)

    st = contextlib.ExitStack()
    mixin = sb(st, "mixin", [128, KT, NT], BF16)
    B_mixin = Buf("mixin")
    mixT = sb(st, "mixT", [128, KT, NT], F32)
    B_mixT = Buf("mixT")
    xT = sb(st, "xT", [128, KT, NT], F32)
    B_xT = Buf("xT")
    xch = [sb(st, f"xch{i}", [128, D], F32) for i in range(2)]
    XB = [Buf(f"xch{i}") for i in range(2)]
    NWS = 3
    wslot = [sb(st, f"wslot{i}", [128, KT * 128], BF16) for i in range(NWS)]
    WB = [Buf(f"wslot{i}") for i in range(NWS)]
    sqb = [sb(st, f"sqb{i}", [128, NT], BF16) for i in range(2)]
    SQB = [Buf("sqb0"), Buf("sqb1")]
    rstd = sb(st, "rstd", [128, NT], F32)
    B_rstd = Buf("rstd")
    tmpf = [sb(st, f"tmpf{i}", [128, NT], F32) for i in range(2)]
    B_tmpf = [Buf("tmpf0"), Buf("tmpf1")]
    hst = [sb(st, f"hst{i}", [128, NT], BF16) for i in range(2)]
    B_hst = [Buf("hst0"), Buf("hst1")]
    wctr[0] = 0

    def load_w5(src_ap):
        i = wctr[0] % NWS
        wctr[0] += 1
        kb.dma(pool, wslot[i][:], src_ap, WB[i], B_wcast)
        return wslot[i], WB[i]

    def rstd_from(psb, w, nfeat):
        kb.op(act, lambda: S.activation(rstd[:, 0:w], ps[psb][:, 0:w], AF.Sqrt, bias=EPS, scale=1.0 / nfeat),
              reads=[PB[psb]], writes=[B_rstd])
        kb.op(dve, lambda: V.reciprocal(rstd[:, 0:w], rstd[:, 0:w]), reads=[B_rstd], parts=[B_rstd])

    xctr[0] = 0
    sq5 = [0]
    for (t0, w) in ([] if "p5a" in SKIP else OWN_TILES[:P5_LIMIT]):
        for k in range(KT):
            src = s5outT[k * 128:(k + 1) * 128, t0:t0 + w] if k < 8 else attT[(k - 8) * 128:(k - 7) * 128, t0:t0 + w]
            kb.dma(sp, mixin[:, k, 0:w], src, B_mixin, B_s5o if k < 8 else B_att, part=(k > 0))
        c0 = 0
        while c0 < w:
            cw = min(128, w - c0)
            xi = xctr[0] % 2
            xctr[0] += 1
            kb.dma(sp, xch[xi][0:cw, :], I.xl[t0 + c0:t0 + c0 + cw, :], XB[xi], B_in)
            for k4 in range(8):
                pb = k4 % 2
                for j in range(4):
                    k = k4 * 4 + j
                    kb.op(pe, lambda: T.transpose(ps[pb][:, j * 128:j * 128 + cw],
                                                  xch[xi][0:cw, k * 128:(k + 1) * 128], ident[0:cw, 0:cw]),
                          reads=[XB[xi], B_c], writes=[PB[pb]] if j == 0 else (), parts=() if j == 0 else [PB[pb]],
                          signal=(j == 3))
                for j in range(4):
                    k = k4 * 4 + j
                    if pb == 0:
                        kb.op(dve, lambda: V.tensor_copy(xT[:, k, c0:c0 + cw], ps[pb][:, j * 128:j * 128 + cw]),
                              reads=[PB[pb]], parts=[B_xT])
                    else:
                        kb.op(act, lambda: S.copy(xT[:, k, c0:c0 + cw], ps[pb][:, j * 128:j * 128 + cw]),
                              reads=[PB[pb]], parts=[B_xT])
            c0 += cw
        for m in range(KT):
            wt, wb = load_w5(wout_b[m])
            pb = 2 + (m % 2)
            for k in range(KT):
                mm(ps[pb][:, 0:w], wt[:, k * 128:(k + 1) * 128], mixin[:, k, 0:w], k == 0, k == KT - 1,
                   reads=[wb, B_mixin], writes=[PB[pb]] if k == 0 else (), parts=() if k == 0 else [PB[pb]],
                   signal=(k == KT - 1))
            kb.op(dve, lambda: V.tensor_copy(mixT[:, m, 0:w], ps[pb][:, 0:w]), reads=[PB[pb]], parts=[B_mixT])
            si = sq5[0] % 2
            sq5[0] += 1
            kb.op(act, lambda: S.activation(sqb[si][:, 0:w], mixT[:, m, 0:w], AF.Square), reads=[B_mixT],
                  writes=[SQB[si]])
            mm(ps[4][:, 0:w], onesb[:], sqb[si][:, 0:w], m == 0, m == KT - 1, reads=[SQB[si], B_c],
               writes=[PB[4]] if m == 0 else (), parts=() if m == 0 else [PB[4]], signal=True)
        rstd_from(4, w, float(D))
        for m in range(KT):
            ti = m % 2
            kb.op(dve, lambda: V.tensor_tensor(tmpf[ti][:, 0:w], mixT[:, m, 0:w], rstd[:, 0:w], ALU.mult),
                  reads=[B_mixT, B_rstd], writes=[B_tmpf[ti]])
            kb.op(dve, lambda: V.scalar_tensor_tensor(xT[:, m, 0:w], tmpf[ti][:, 0:w], vecs[:, 4, m:m + 1],
                                                      xT[:, m, 0:w], ALU.mult, ALU.add),
                  reads=[B_tmpf[ti], B_vecs, B_xT], parts=[B_xT])
            kb.dma(sp, xmidT[m * 128:(m + 1) * 128, t0:t0 + w], xT[:, m, 0:w], B_xmid, B_xT)
            si = sq5[0] % 2
            sq5[0] += 1
            kb.op(act, lambda: S.activation(sqb[si][:, 0:w], xT[:, m, 0:w], AF.Square), reads=[B_xT],
                  writes=[SQB[si]])
            mm(ps[5][:, 0:w], onesb[:], sqb[si][:, 0:w], m == 0, m == KT - 1, reads=[SQB[si], B_c],
               writes=[PB[5]] if m == 0 else (), parts=() if m == 0 else [PB[5]], signal=True)
        rstd_from(5, w, float(D))
        for m in range(KT):
            ti = m % 2
            kb.op(dve, lambda: V.tensor_tensor(tmpf[ti][:, 0:w], xT[:, m, 0:w], rstd[:, 0:w], ALU.mult),
                  reads=[B_xT, B_rstd], writes=[B_tmpf[ti]])
            kb.op(dve, lambda: V.tensor_scalar(hst[ti][:, 0:w], tmpf[ti][:, 0:w], vecs[:, 5, m:m + 1],
                                               vecs[:, 6, m:m + 1], ALU.mult, ALU.add),
                  reads=[B_tmpf[ti], B_vecs], writes=[B_hst[ti]])
            kb.dma(sp, hxT[m * 128:(m + 1) * 128, t0:t0 + w], hst[ti][:, 0:w], B_hx, B_hst[ti])
    kb.barrier()
    st.close()
    if STOP_AFTER == "p5a":
        return finish(nc, es, kb, out, B_out)

    st = contextlib.ExitStack()
    FT_TILES = [(0, 410), (410, 410), (820, 410), (1230, 410), (1640, 408)]
    hx = sb(st, "hx", [128, KT, NT + 2], BF16)
    B_hxs = Buf("hxs")
    actT = sb(st, "actT", [128, FT, NT], BF16)
    B_act = Buf("actT")
    NWS = 8
    wpool = sb(st, "wpool", [128, 33024], BF16)
    wslot = [wpool[:, i * 4096:(i + 1) * 4096] for i in range(NWS)]
    WB = [Buf(f"wslot{i}") for i in range(NWS)]
    dslot = [wpool[:, j * 11008:(j + 1) * 11008] for j in range(3)]
    DB = [Buf(f"dslot{j}") for j in range(3)]

    def fence(q, bufs):
        for b_ in bufs:
            q.wait_all(b_.r)
            q.wait_all(b_.w)
    cws = sb(st, "cws", [128, FT, 4], F32)
    B_cw = Buf("cw")
    kb.dma(sp, cws[:], I.convw, B_cw, B_in)
    cv = [sb(st, f"cv{i}", [128, NT], F32) for i in range(2)]
    B_cv = [Buf("cv0"), Buf("cv1")]
    sg = [sb(st, f"sg{i}", [128, NT], F32) for i in range(2)]
    B_sg = [Buf("sg0"), Buf("sg1")]
    sqb = [sb(st, f"sqb{i}", [128, NT], BF16) for i in range(2)]
    SQB = [Buf("sqb0"), Buf("sqb1")]
    rstds = [sb(st, f"rstd{i}", [128, NT], F32) for i in range(2)]
    B_rstds = [Buf("rstd0"), Buf("rstd1")]
    fst = [sb(st, f"fst{i}", [128, NT], F32) for i in range(2)]
    B_fst = [Buf("fst0"), Buf("fst1")]
    xm2s = [sb(st, f"xm2_{i}", [128, 2, NT], F32) for i in range(2)]
    B_xm2s = [Buf("xm2_0"), Buf("xm2_1")]
    f2s = [sb(st, f"f2_{i}", [128, 2, NT], F32) for i in range(2)]
    B_f2s = [Buf("f2_0"), Buf("f2_1")]
    ost = [sb(st, f"ost{i}", [128, 512], F32) for i in range(2)]
    B_ost = [Buf("ost0"), Buf("ost1")]
    wctr[0] = 0
    dctr = 0
    sqc = 0
    octr = 0
    pending_out = []
    def out_a(ti5, t0, w, mg, rsel, bi):
        rstd = rstds[rsel]
        B_rstd = B_rstds[rsel]
        f2, xm2, B_f2, B_xm2 = f2s[bi], xm2s[bi], B_f2s[bi], B_xm2s[bi]
        kb.dma(sp, f2[:, :, 0:w], fTs[ti5, mg * 256:(mg + 1) * 256, 0:w].rearrange("(a p) t -> p a t", p=128),
               B_f2, B_f)
        kb.dma(sp, xm2[:, :, 0:w], xmidT[mg * 256:(mg + 1) * 256, t0:t0 + w].rearrange("(a p) t -> p a t", p=128),
               B_xm2, B_xmid)
        for a in range(2):
            m = mg * 2 + a
            kb.op(dve, lambda: V.tensor_tensor(f2[:, a, 0:w], f2[:, a, 0:w], rstd[:, 0:w], ALU.mult),
                  reads=[B_f2, B_rstd], parts=[B_f2])
            kb.op(dve, lambda: V.scalar_tensor_tensor(f2[:, a, 0:w], f2[:, a, 0:w], vecs[:, 7, m:m + 1],
                                                      xm2[:, a, 0:w], ALU.mult, ALU.add),
                  reads=[B_f2, B_xm2, B_vecs], parts=[B_f2])

    def out_b(ti5, t0, w, mg, rsel, bi):
        nonlocal octr
        f2, B_f2 = f2s[bi], B_f2s[bi]
        c0 = 0
        while c0 < w:
            cw = min(128, w - c0)
            for a in range(2):
                kb.op(pe, lambda: T.transpose(ps[7][0:cw, a * 128:(a + 1) * 128], f2[:, a, c0:c0 + cw],
                                              ident[:, :]),
                      reads=[B_f2, B_c], writes=[PB[7]] if a == 0 else (), parts=() if a == 0 else [PB[7]],
                      signal=(a == 1))
            oi = octr % 2
            octr += 1
            kb.op(act, lambda: S.copy(ost[oi][0:cw, 0:256], ps[7][0:cw, 0:256]), reads=[PB[7]],
                  writes=[B_ost[oi]])
            kb.dma(sp, out[t0 + c0:t0 + c0 + cw, mg * 256:(mg + 1) * 256], ost[oi][0:cw, 0:256], B_out, B_ost[oi])
            c0 += cw

    in_flight = []

    def out_pump():
        nxt = pending_out.pop(0) if pending_out else None
        if nxt is not None:
            out_a(*nxt)
        if in_flight:
            out_b(*in_flight.pop(0))
        if nxt is not None:
            in_flight.append(nxt)

    for ti5, (t0, w) in enumerate(FT_TILES):
        rsel = ti5 % 2
        rstd = rstds[rsel]
        B_rstd = B_rstds[rsel]
        if t0 == 0:
            kb.op(dve, lambda: V.memset(hx[:, :, 0:1], 0.0), writes=[B_hxs])
            for k in range(KT):
                kb.dma(sp, hx[:, k, 1:w + 2], hxT[k * 128:(k + 1) * 128, 0:w + 1], B_hxs, B_hx, part=True)
        else:
            for k in range(KT):
                kb.dma(sp, hx[:, k, 0:w + 2], hxT[k * 128:(k + 1) * 128, t0 - 1:t0 + w + 1], B_hxs, B_hx,
                       part=(k > 0))
        for ft in range(FT):
            if ft % 5 == 2 and (pending_out or in_flight):
                out_pump()
            if ft == 0:
                fence(pool, DB)
            i = wctr[0] % NWS
            wctr[0] += 2
            kb.dma(pool, wslot[i], wffg_b[ft], WB[i], B_wcast)
            kb.dma(pool, wslot[i + 1], wffu_b[ft], WB[i + 1], B_wcast)
            gb, ub = (0, 1) if ft % 2 == 0 else (2, 3)
            for k in range(KT):
                mm(ps[gb][:, 0:w + 2], wslot[i][:, k * 128:(k + 1) * 128], hx[:, k, 0:w + 2], k == 0, k == KT - 1,
                   reads=[WB[i], B_hxs], writes=[PB[gb]] if k == 0 else (), parts=() if k == 0 else [PB[gb]],
                   signal=(k == KT - 1))
            for k in range(KT):
                mm(ps[ub][:, 0:w], wslot[i + 1][:, k * 128:(k + 1) * 128], hx[:, k, 1:w + 1], k == 0, k == KT - 1,
                   reads=[WB[i + 1], B_hxs], writes=[PB[ub]] if k == 0 else (), parts=() if k == 0 else [PB[ub]],
                   signal=(k == KT - 1))
            ci = ft % 2
            kb.op(dve, lambda: V.tensor_scalar(cv[ci][:, 0:w], ps[gb][:, 1:w + 1], cws[:, ft, 1:2], cws[:, ft, 3:4],
                                               ALU.mult, ALU.add), reads=[PB[gb], B_cw], writes=[B_cv[ci]])
            kb.op(dve, lambda: V.scalar_tensor_tensor(cv[ci][:, 0:w], ps[gb][:, 0:w], cws[:, ft, 0:1],
                                                      cv[ci][:, 0:w], ALU.mult, ALU.add),
                  reads=[PB[gb], B_cw, B_cv[ci]], parts=[B_cv[ci]])
            kb.op(dve, lambda: V.scalar_tensor_tensor(cv[ci][:, 0:w], ps[gb][:, 2:w + 2], cws[:, ft, 2:3],
                                                      cv[ci][:, 0:w], ALU.mult, ALU.add),
                  reads=[PB[gb], B_cw, B_cv[ci]], parts=[B_cv[ci]])
            kb.op(act, lambda: S.activation(sg[ci][:, 0:w], cv[ci][:, 0:w], AF.Silu), reads=[B_cv[ci]],
                  writes=[B_sg[ci]])
            kb.op(dve, lambda: V.tensor_tensor(actT[:, ft, 0:w], sg[ci][:, 0:w], ps[ub][:, 0:w], ALU.mult),
                  reads=[B_sg[ci], PB[ub]], parts=[B_act])
        for m in range(KT):
            di = dctr % 3
            dctr += 1
            if m == 0:
                fence(pool, WB)
            kb.dma(pool, dslot[di], wdn_b[m], DB[di], B_wcast)
            pb = 4 + (m % 2)
            for k in range(FT):
                mm(ps[pb][:, 0:w], dslot[di][:, k * 128:(k + 1) * 128], actT[:, k, 0:w], k == 0, k == FT - 1,
                   reads=[DB[di], B_act], writes=[PB[pb]] if k == 0 else (), parts=() if k == 0 else [PB[pb]],
                   signal=(k == FT - 1))
            fi = m % 2
            kb.op(dve, lambda: V.tensor_copy(fst[fi][:, 0:w], ps[pb][:, 0:w]), reads=[PB[pb]], writes=[B_fst[fi]])
            kb.dma(sp, fTs[ti5, m * 128:(m + 1) * 128, 0:w], fst[fi][:, 0:w], B_f, B_fst[fi])
            si = sqc % 2
            sqc += 1
            kb.op(act, lambda: S.activation(sqb[si][:, 0:w], fst[fi][:, 0:w], AF.Square), reads=[B_fst[fi]],
                  writes=[SQB[si]])
            mm(ps[6][:, 0:w], onesb[:], sqb[si][:, 0:w], m == 0, m == KT - 1, reads=[SQB[si], B_c],
               writes=[PB[6]] if m == 0 else (), parts=() if m == 0 else [PB[6]], signal=True)
        kb.op(act, lambda: S.activation(rstd[:, 0:w], ps[6][:, 0:w], AF.Sqrt, bias=EPS, scale=1.0 / D),
              reads=[PB[6]], writes=[B_rstd])
        kb.op(dve, lambda: V.reciprocal(rstd[:, 0:w], rstd[:, 0:w]), reads=[B_rstd], parts=[B_rstd])
        pending_out.extend([(ti5, t0, w, mg, rsel, mg % 2) for mg in range(16)])
    while pending_out or in_flight:
        out_pump()
    kb.barrier()
    st.close()
    return finish(nc, es, kb, out, B_out)


_EXTRA = []


def finish(nc, es, kb, out, B_out):
    kb.barrier()
    for s_ in _EXTRA:
        s_.close()
    es.close()
    return nc


def _cols(v):
    return np.ascontiguousarray(v.reshape(-1, 128).T)


def _wtiles(w):
    K, M = w.shape
    return np.ascontiguousarray(w.reshape(K // 128, 128, M // 128, 128).transpose(2, 1, 0, 3)).reshape(
        M // 128, 128, (K // 128) * 128)


def _rope_tables(tpos, is_x):
    inv = (10000.0 ** (-np.arange(16, dtype=np.float32) / 16)).astype(np.float32)
    t = tpos.astype(np.float32)
    row = np.floor(t / 64).astype(np.float32)
    col = (t - row * 64).astype(np.float32)
    ang = np.concatenate([row[:, None] * inv, col[:, None] * inv], axis=-1).astype(np.float32)
    cos, sin = np.cos(ang).astype(np.float32), np.sin(ang).astype(np.float32)
    cos = np.where(is_x[:, None], cos, 1.0).astype(np.float32)
    sin = np.where(is_x[:, None], sin, 0.0).astype(np.float32)
    cc = np.concatenate([cos.T, cos.T], axis=0)
    ss = np.concatenate([-sin.T, sin.T], axis=0)
    return np.ascontiguousarray(np.stack([cc, ss], axis=1)).astype(np.float32)


def _prep_shared(inp):
    sh = {}
    sh["wada"] = _wtiles(inp["w_ada"][0])
    sh["bada"] = _cols(inp["b_ada"][0])
    sh["gvec"] = np.ascontiguousarray(np.stack([_cols(inp[k][0]) for k in
                                                ("g_pre_mix", "g_post_mix", "g_pre_ffn", "g_post_ffn")], axis=1))
    w_in = inp["w_in"][0]
    kr = w_in[:, 2560:2624]
    ev, od = kr[:, 0::2], kr[:, 1::2]
    sh["win"] = _wtiles(np.concatenate([w_in[:, :2560], ev, od, od, ev], axis=1))
    sh["gq"] = _cols(inp["mla_g_q"][0])
    sh["gkv"] = _cols(inp["mla_g_kv"][0])
    wq = inp["mla_w_uq"][0].reshape(1024, NH, 192)
    nope, rp = wq[:, :, :128], wq[:, :, 128:]
    ev, od = rp[:, :, 0::2], rp[:, :, 1::2]
    wqp = np.concatenate([nope, ev, od, od, ev], axis=2)
    sh["wuq"] = np.ascontiguousarray(wqp.reshape(8, 128, NH, 256).transpose(2, 1, 0, 3)).reshape(NH, 128, 8 * 256)
    wkv = inp["mla_w_ukv"][0].reshape(512, NH, 256)
    wk = wkv[:, :, :128].reshape(4, 128, NH * 128)
    wv = wkv[:, :, 128:].reshape(4, 128, NH * 128)
    sh["wuk"] = np.ascontiguousarray(wk.transpose(1, 0, 2)).reshape(128, 4 * 3072)
    sh["wuv"] = np.ascontiguousarray(wv.transpose(1, 0, 2)).reshape(128, 4 * 3072)
    sh["wout"] = _wtiles(inp["w_out"][0])
    sh["wglu"] = _wtiles(inp["s5_w_glu"][0])
    fw = inp["ffn_w_in"][0]
    sh["wffg"] = _wtiles(fw[:, :DFF])
    sh["wffu"] = _wtiles(fw[:, DFF:])
    sh["wdn"] = _wtiles(inp["ffn_w_down"][0])
    sh["ident"] = np.eye(128, dtype=np.float32)
    return sh


def _prep_core(inp, sh, b, half):
    m = dict(sh)
    x = inp["x"][b]
    ctx = inp["ctx"][b]
    if half == 1:
        x = x[::-1]
        ctx = ctx[::-1]
    m["xl"] = np.ascontiguousarray(x)
    m["ctxl"] = np.ascontiguousarray(ctx)
    m["ccol"] = np.ascontiguousarray(np.stack([_cols(inp["c"][b]), _cols(inp["c_ctx"])], axis=-1))
    cw = inp["ffn_conv_w"][0]
    if half == 1:
        cw = cw[::-1]
    m["convw"] = np.ascontiguousarray(np.stack([_cols(cw[0]), _cols(cw[1]), _cols(cw[2]),
                                                _cols(inp["ffn_conv_b"][0])], axis=-1))
    loc = np.arange(SEQ)
    tpos = loc if half == 0 else (SEQ - 1 - loc)
    kp = np.concatenate([tpos, np.zeros(CTX, dtype=tpos.dtype)])
    isx = np.concatenate([np.ones(SEQ, bool), np.zeros(CTX, bool)])
    m["ropek"] = _rope_tables(kp, isx)
    m["ropeq"] = _rope_tables(tpos[:NOWN], np.ones(NOWN, bool))
    dsel = [0, 1] if half == 0 else [1, 0]

    def gp(a):
        a = a[dsel].reshape(2, 32, 2, 64)
        return a.transpose(2, 3, 0, 1).reshape(128, 64)

    m["s5lam"] = np.ascontiguousarray(np.stack([gp(inp["s5_lambda_re"][0]), gp(inp["s5_lambda_im"][0])], axis=1))
    ls = np.broadcast_to(inp["s5_log_step"][0][:, :, None], (2, 64, 64))
    m["s5ls"] = np.ascontiguousarray(gp(ls))

    def gb(ar, ai):
        a = np.stack([ar, ai], axis=-2)[dsel]
        a = a.reshape(2, 32, 2, 64, 2, 16)
        return np.ascontiguousarray(a.transpose(2, 3, 0, 1, 4, 5)).reshape(128, 64, 2, 16)

    m["s5b"] = gb(inp["s5_b_re"][0], inp["s5_b_im"][0])
    m["s5c"] = gb(inp["s5_c_re"][0].transpose(0, 1, 3, 2), inp["s5_c_im"][0].transpose(0, 1, 3, 2))
    m["s5d"] = np.ascontiguousarray(inp["s5_d"][0].reshape(32, 2, 16).transpose(1, 2, 0)).reshape(32, 32)
    return {k: np.ascontiguousarray(v, dtype=np.float32) for k, v in m.items()}


def kernel(**inputs):
    inp = {k: np.asarray(v) for k, v in inputs.items()}
    sh = _prep_shared(inp)
    in_maps = [_prep_core(inp, sh, c // 2, c % 2) for c in range(8)]
    nc = _build()
    in_maps = [{k: v for k, v in m.items() if k in nc._declared_inputs} for m in in_maps]
    res = run_bass_kernel_spmd(nc, in_maps, core_ids=list(range(8)))
    outp = np.empty((4, SEQ, D), dtype=np.float32)
    for c in range(8):
        o = res.results[c]["out"]
        b, half = c // 2, c % 2
        if half == 0:
            outp[b, :2048] = o
        else:
            outp[b, 2048:] = o[::-1]
    return outp
```

```python
import contextlib
import numpy as np
import concourse.bass as bass
import concourse.mybir as mybir
from concourse.bass_utils import run_bass_kernel_spmd

F32 = mybir.dt.float32
BF16 = mybir.dt.bfloat16
AF = mybir.ActivationFunctionType
ALU = mybir.AluOpType

D = 4096
KT = 32
SEQ = 4096
CTX = 256
NKEY = SEQ + CTX
NOWN = 2050
NT = 410
OWN_TILES = [(i * NT, NT) for i in range(5)]
REST_TILES = [(2050, 510), (2560, 512), (3072, 512), (3584, 512)]
UCOLS = CTX + SEQ + CTX
NH = 24
DFF = 11008
FT = 86
EPS = 1e-6
MLA_SCALE = 192.0 ** -0.5
NBA = 32 + 257
NBB = 544
NYB = 257
STOP_AFTER = None
DEBUG = False
P1_LIMIT = None
P1_STAGE = 0
SKIP = set()
P5_LIMIT = None


class Tok:
    __slots__ = ("sem", "val", "key")

    def __init__(self, sem, val, key):
        self.sem, self.val, self.key = sem, val, key


class Buf:
    __slots__ = ("w", "r", "dsem", "dcnt", "dkey", "last_dma", "name", "dram", "bg")

    def __init__(self, name="", dram=False):
        self.w, self.r = {}, {}
        self.dsem = None
        self.dcnt = 0
        self.last_dma = None
        self.name = name
        self.dram = dram
        self.bg = False

    @staticmethod
    def _add(d, tok):
        o = d.get(tok.key)
        if o is None or o.val < tok.val:
            d[tok.key] = tok


class Eng:
    def __init__(self, kb, h, name, is_pe=False, compute=True):
        self.kb, self.h, self.name, self.is_pe, self.compute = kb, h, name, is_pe, compute
        self.sem = kb.new_sem("e_" + name)
        self.key = "e_" + name
        self.cnt = 0
        self.waited = {}
        self.pending = False

    def wait(self, tok):
        if tok.key == self.key:
            if self.is_pe:
                return
        if self.waited.get(tok.key, 0) >= tok.val:
            return
        self.h.wait_ge(tok.sem, tok.val)
        self.waited[tok.key] = tok.val

    def wait_all(self, d):
        for t in list(d.values()):
            self.wait(t)

    def mark(self, inst, signal):
        if signal:
            self.cnt += 1
            inst.then_inc(self.sem, 1)
            self.pending = False
            return Tok(self.sem, self.cnt, self.key)
        self.pending = True
        return Tok(self.sem, self.cnt + 1, self.key)


class KB:
    def __init__(self, nc, es):
        self.nc, self.es = nc, es
        self.nsem = 0
        self.pe = Eng(self, nc.tensor, "pe", is_pe=True)
        self.dve = Eng(self, nc.vector, "dve")
        self.act = Eng(self, nc.scalar, "act")
        self.pool = Eng(self, nc.gpsimd, "pool")
        self.sp = Eng(self, nc.sync, "sp", compute=False)
        self.engs = [self.pe, self.dve, self.act, self.pool, self.sp]
        self.dbufs = []
        self.retired = []

    def new_sem(self, name):
        self.nsem += 1
        return self.es.enter_context(self.nc.semaphore(f"{name}_{self.nsem}"))

    def op(self, eng, build, reads=(), writes=(), parts=(), signal=True):
        for b in reads:
            eng.wait_all(b.w)
        for b in writes:
            eng.wait_all(b.r)
            eng.wait_all(b.w)
        for b in parts:
            eng.wait_all(b.r)
        inst = build()
        tok = eng.mark(inst, signal)
        for b in reads:
            if not b.dram:
                Buf._add(b.r, tok)
        for b in writes:
            b.w = {tok.key: tok}
            b.r = {}
        for b in parts:
            Buf._add(b.w, tok)
        return tok

    def dma(self, q, out_ap, in_ap, out_buf, in_buf, part=False, **kw):
        sb = out_buf if not out_buf.dram else (in_buf if not in_buf.dram else out_buf)
        if sb.dsem is not None and sb.dcnt >= 30000:
            if not sb.bg:
                self.retired.append(Tok(sb.dsem, sb.dcnt, sb.dkey))
            sb.dsem = self.new_sem("d")
            sb.dkey = f"d{self.nsem}"
            sb.dcnt = 0
        if sb.dsem is None:
            sb.dsem = self.new_sem("d")
            sb.dkey = f"d{self.nsem}"
            self.dbufs.append(sb)
        if sb.last_dma is not None and not sb.dram:
            q.wait(sb.last_dma)
        q.wait_all(in_buf.w)
        if not out_buf.dram:
            q.wait_all(out_buf.r)
            if not part:
                q.wait_all(out_buf.w)
        inst = q.h.dma_start(out=out_ap, in_=in_ap, **kw)
        sb.dcnt += 16
        inst.then_inc(sb.dsem, 16)
        tok = Tok(sb.dsem, sb.dcnt, sb.dkey)
        sb.last_dma = tok
        if not in_buf.dram:
            Buf._add(in_buf.r, tok)
        if out_buf.dram or part:
            Buf._add(out_buf.w, tok)
        else:
            out_buf.w = {tok.key: tok}
            out_buf.r = {}
        return tok

    def barrier(self):
        assert not self.pe.pending
        toks = [Tok(e.sem, e.cnt, e.key) for e in self.engs if e.compute and e.cnt > 0]
        toks += [Tok(b.dsem, b.dcnt, b.dkey) for b in self.dbufs if b.dcnt > 0 and not b.bg]
        toks += self.retired
        for e in self.engs:
            for t in toks:
                e.wait(t)


def _build(debug_out=None):
    nc = bass.Bass("TRN2", target_bir_lowering=False)
    es = contextlib.ExitStack()
    kb = KB(nc, es)
    pe, dve, act, pool, sp = kb.pe, kb.dve, kb.act, kb.pool, kb.sp
    V, S, T, G = nc.vector, nc.scalar, nc.tensor, nc.gpsimd

    def din(name, shape, dt=F32):
        return nc.dram_tensor(name, list(shape), dt, kind="ExternalInput").ap()

    dbg = debug_out or ()

    def dscr(name, shape, dt):
        kind = "ExternalOutput" if name in dbg else "Internal"
        return nc.dram_tensor(name, list(shape), dt, kind=kind).ap()

    IN_SHAPES = dict(xl=[SEQ, D], ctxl=[CTX, D], ccol=[128, KT, 2], wada=[192, 128, KT * 128], bada=[128, 192],
                     gvec=[128, 4, KT], win=[21, 128, KT * 128], gq=[128, 8], gkv=[128, 4],
                     wuq=[NH, 128, 8 * 256], wuk=[128, 4 * 3072], wuv=[128, 4 * 3072], wout=[KT, 128, KT * 128],
                     wglu=[8, 128, 8 * 128], wffg=[FT, 128, KT * 128], wffu=[FT, 128, KT * 128],
                     wdn=[KT, 128, FT * 128], convw=[128, FT, 4], ropeq=[64, 2, NOWN], ropek=[64, 2, NKEY],
                     ident=[128, 128], s5lam=[128, 2, 64], s5ls=[128, 64], s5b=[128, 64, 2, 16],
                     s5c=[128, 64, 2, 16], s5d=[32, 32])
    declared = {}

    class _In:
        def __getattr__(self, name):
            if name not in declared:
                declared[name] = din(name, IN_SHAPES[name])
            return declared[name]

    I = _In()
    nc._declared_inputs = declared
    out = nc.dram_tensor("out", [2048, D], F32, kind="ExternalOutput").ap()

    uT = dscr("uT", [1024, UCOLS], BF16)
    qcnTs = dscr("qcnTs", [1024, NOWN], BF16)
    KTs = dscr("KTs", [NH, 128, NKEY], BF16)
    Vs = dscr("Vs", [34, 128, 3072], BF16)
    yactT = dscr("yactT", [1024, 2056], BF16)
    s5outT = dscr("s5outT", [1024, NOWN], BF16)
    attT = dscr("attT", [3072, NOWN], BF16)
    xmidT = dscr("xmidT", [D, NOWN], F32)
    hxT = dscr("hxT", [D, NOWN], BF16)
    fTs = dscr("fTs", [5, D, NT], F32)
    B_uT, B_qcn, B_KT, B_V, B_yact, B_s5o, B_att, B_xmid, B_hx, B_f = [Buf(n, dram=True) for n in
                                                                       "uT qcn KT V yact s5o att xmid hx f".split()]
    wffg_b = dscr("wffg_b", [FT, 128, KT * 128], BF16)
    wffu_b = dscr("wffu_b", [FT, 128, KT * 128], BF16)
    wdn_b = dscr("wdn_b", [KT, 128, FT * 128], BF16)
    wout_b = dscr("wout_b", [KT, 128, KT * 128], BF16)
    B_wcast = Buf("wcast", dram=True)
    B_wcast.bg = True
    B_in = Buf("inputs", dram=True)
    B_out = Buf("out", dram=True)

    sbn = [0]

    def sb(st, name, shape, dt):
        sbn[0] += 1
        return st.enter_context(nc.sbuf_tensor(f"s{sbn[0]}_{name}", list(shape), dt))

    ps = [es.enter_context(nc.psum_tensor(f"ps{i}", [128, 512], F32)) for i in range(8)]
    PB = [Buf(f"ps{i}") for i in range(8)]

    ident = sb(es, "ident", [128, 128], F32)
    identb = sb(es, "identb", [128, 128], BF16)
    onesb = sb(es, "onesb", [128, 128], BF16)
    onesf = sb(es, "onesf", [128, 128], F32)
    modc = sb(es, "modc", [128, 192, 2], F32)
    gv = sb(es, "gv", [128, 4, KT], F32)
    vecs = sb(es, "vecs", [128, 8, KT], F32)
    gqs = sb(es, "gqs", [128, 8], F32)
    gkvs = sb(es, "gkvs", [128, 4], F32)
    sc2 = sb(es, "sc2", [128, KT, 2], BF16)
    badas = sb(es, "badas", [128, 192], F32)
    B_c = Buf("consts")
    B_mod = Buf("mod")
    B_vecs = Buf("vecs")
    B_krT = Buf("krT")
    B_kvcn = Buf("kvcn")
    ccs = sb(es, "ccs", [128, KT, 2], F32)
    kvst = contextlib.ExitStack()
    _EXTRA.clear()
    _EXTRA.append(kvst)
    krT = sb(kvst, "krT", [128, NKEY], BF16)
    kvcnT = sb(kvst, "kvcnT", [128, 4, NKEY], BF16)

    def mm(o, l, r, start, stop, reads, writes=(), parts=(), signal=False):
        return kb.op(pe, lambda: T.matmul(o, l, r, start=start, stop=stop), reads=reads, writes=writes,
                     parts=parts, signal=signal)

    kb.dma(sp, ident[:], I.ident, B_c, B_in)
    kb.dma(sp, gv[:], I.gvec, B_c, B_in, part=True)
    kb.dma(sp, gqs[:], I.gq, B_c, B_in, part=True)
    kb.dma(sp, gkvs[:], I.gkv, B_c, B_in, part=True)
    kb.dma(sp, badas[:], I.bada, B_c, B_in, part=True)
    kb.dma(sp, ccs[:], I.ccol, B_c, B_in, part=True)
    kb.op(dve, lambda: V.tensor_copy(identb[:], ident[:]), reads=[B_c], parts=[B_c])
    kb.op(dve, lambda: V.memset(onesb[:], 1.0), parts=[B_c])
    kb.op(dve, lambda: V.memset(onesf[:], 1.0), parts=[B_c])
    kb.op(dve, lambda: V.memset(krT[64:128, :], 0.0), writes=[B_krT])
    kb.op(act, lambda: S.activation(sc2[:], ccs[:], AF.Silu), reads=[B_c], parts=[B_c])

    wst = contextlib.ExitStack()
    NWS = 3
    wslot = [sb(wst, f"wslot{i}", [128, KT * 128], BF16) for i in range(NWS)]
    WB = [Buf(f"wslot{i}") for i in range(NWS)]
    wctr = [0]

    def load_w(src_ap):
        i = wctr[0] % NWS
        wctr[0] += 1
        kb.dma(pool, wslot[i][:], src_ap, WB[i], B_in)
        return wslot[i], WB[i]

    ada_next = [0]

    def adaln(n):
        for _ in range(n):
            m = ada_next[0]
            if m >= 192:
                return
            ada_next[0] += 1
            w, wb = load_w(I.wada[m])
            pb = 7
            for k in range(KT):
                mm(ps[pb][:, 0:2], w[:, k * 128:(k + 1) * 128], sc2[:, k, :], k == 0, k == KT - 1,
                   reads=[wb, B_c], writes=[PB[pb]] if k == 0 else (), parts=() if k == 0 else [PB[pb]],
                   signal=(k == KT - 1))
            kb.op(dve, lambda: V.tensor_scalar(modc[:, m, :], ps[pb][:, 0:2], badas[:, m:m + 1], None, ALU.add),
                  reads=[PB[pb], B_c], parts=[B_mod])

    adaln(64)
    def vec_scale(dst, gi, mlo, col):
        kb.op(dve, lambda: V.scalar_tensor_tensor(vecs[:, dst, :], modc[:, mlo:mlo + KT, col], 1.0, gv[:, gi, :],
                                                  ALU.add, ALU.mult), reads=[B_mod, B_c], parts=[B_vecs])

    def vec_copy(dst, mlo, col):
        kb.op(dve, lambda: V.tensor_copy(vecs[:, dst, :], modc[:, mlo:mlo + KT, col]), reads=[B_mod], parts=[B_vecs])

    def vec_mul(dst, gi, mlo, col):
        kb.op(dve, lambda: V.tensor_tensor(vecs[:, dst, :], modc[:, mlo:mlo + KT, col], gv[:, gi, :], ALU.mult),
              reads=[B_mod, B_c], parts=[B_vecs])

    vec_scale(0, 0, 32, 0)
    vec_copy(1, 0, 0)
    vec_scale(2, 0, 32, 1)
    vec_copy(3, 0, 1)

    if STOP_AFTER == "p0":
        d3 = nc.dram_tensor("dbg_vecs", [128, 8, KT], F32, kind="ExternalOutput").ap()
        kb.dma(sp, d3, vecs[:], B_out, B_vecs)
        wst.close()
        return finish(nc, es, kb, out, B_out)
    st = contextlib.ExitStack()
    xch = [sb(st, f"xch{i}", [128, D], F32) for i in range(2)]
    XB = [Buf(f"xch{i}") for i in range(2)]
    hmod = sb(st, "hmod", [128, KT, 512], BF16)
    B_h = Buf("hmod")
    ssq = sb(st, "ssq", [128, 2], F32)
    rs = sb(st, "rs", [128, 2], F32)
    B_ss = [Buf("ss0"), Buf("ss1")]
    junk = sb(st, "junk", [128, D], BF16)
    B_junk = Buf("junk")
    ust = [sb(st, f"ust{i}", [128, 512], BF16) for i in range(3)]
    UB = [Buf(f"ust{i}") for i in range(3)]
    qcT = sb(st, "qcT", [128, 8, 512], F32)
    B_qc = Buf("qcT")
    sqb = [sb(st, f"sqb{i}", [128, 512], BF16) for i in range(2)]
    SQB = [Buf("sqb0"), Buf("sqb1")]
    rstd = sb(st, "rstd", [128, 512], F32)
    B_rstd = Buf("rstd")
    rtab = sb(st, "rtab", [64, 2, 512], F32)
    B_rtab = Buf("rtab")
    rtmp = sb(st, "rtmp", [64, 512], F32)
    B_rtmp = Buf("rtmp")
    uctr = [0]
    xctr = [0]
    sqctr = [0]

    def p1_tile(kind, t0, w):
        src = I.ctxl if kind == "ctx" else I.xl
        vs, vb = (2, 3) if kind == "ctx" else (0, 1)
        c0 = 0
        while c0 < w:
            cw = min(128, w - c0)
            xi = xctr[0] % 2
            xctr[0] += 1
            xc, xb = xch[xi], XB[xi]
            kb.dma(sp, xc[0:cw, :], src[t0 + c0:t0 + c0 + cw, :], xb, B_in)
            kb.op(act, lambda: S.activation(junk[0:cw, :], xc[0:cw, :], AF.Square, accum_out=ssq[0:cw, xi:xi + 1]),
                  reads=[xb], writes=[B_junk, B_ss[xi]])
            kb.op(act, lambda: S.activation(rs[0:cw, xi:xi + 1], ssq[0:cw, xi:xi + 1], AF.Sqrt, bias=EPS,
                                            scale=1.0 / D), reads=[B_ss[xi]], parts=[B_ss[xi]])
            kb.op(dve, lambda: V.reciprocal(rs[0:cw, xi:xi + 1], rs[0:cw, xi:xi + 1]), reads=[B_ss[xi]],
                  parts=[B_ss[xi]])
            kb.op(dve, lambda: V.tensor_scalar(xc[0:cw, :], xc[0:cw, :], rs[0:cw, xi:xi + 1], None, ALU.mult),
                  reads=[B_ss[xi]], parts=[xb])
            for k4 in range(8):
                pb = k4 % 2
                for j in range(4):
                    k = k4 * 4 + j
                    kb.op(pe, lambda: T.transpose(ps[pb][:, j * 128:j * 128 + cw], xc[0:cw, k * 128:(k + 1) * 128],
                                                  ident[0:cw, 0:cw]),
                          reads=[xb, B_c], writes=[PB[pb]] if j == 0 else (), parts=() if j == 0 else [PB[pb]],
                          signal=(j == 3))
                for j in range(4):
                    k = k4 * 4 + j
                    eng = dve
                    if eng is dve:
                        kb.op(dve, lambda: V.tensor_scalar(hmod[:, k, c0:c0 + cw], ps[pb][:, j * 128:j * 128 + cw],
                                                           vecs[:, vs, k:k + 1], vecs[:, vb, k:k + 1], ALU.mult,
                                                           ALU.add),
                              reads=[PB[pb], B_vecs], parts=[B_h])
                    else:
                        kb.op(act, lambda: S.activation(hmod[:, k, c0:c0 + cw], ps[pb][:, j * 128:j * 128 + cw],
                                                        AF.Identity, bias=vecs[:, vb, k:k + 1],
                                                        scale=vecs[:, vs, k:k + 1]),
                              reads=[PB[pb], B_vecs], parts=[B_h])
            c0 += cw
        mlist = list(range(21)) if kind == "own" else (list(range(8)) + list(range(16, 21)))
        if P1_STAGE == 1:
            return
        if P1_STAGE == 2:
            mlist = [0, 1]
        if P1_STAGE in (3, 4, 5):
            mlist = [0, 1, 16, 17, 18, 19]
        if kind == "ctx":
            keyc = SEQ
        else:
            keyc = t0
        for m in mlist:
            wt, wb = load_w(I.win[m])
            if m < 20:
                pb = 2 + (m % 2)
                for k in range(KT):
                    mm(ps[pb][:, 0:w], wt[:, k * 128:(k + 1) * 128], hmod[:, k, 0:w], k == 0, k == KT - 1,
                       reads=[wb, B_h], writes=[PB[pb]] if k == 0 else (), parts=() if k == 0 else [PB[pb]],
                       signal=(k == KT - 1))
                if m < 8:
                    ui = uctr[0] % 3
                    uctr[0] += 1
                    kb.op(act, lambda: S.copy(ust[ui][:, 0:w], ps[pb][:, 0:w]), reads=[PB[pb]], writes=[UB[ui]])
                    if kind == "ctx":
                        kb.dma(sp, uT[m * 128:(m + 1) * 128, 0:CTX], ust[ui][:, 0:w], B_uT, UB[ui])
                        kb.dma(sp, uT[m * 128:(m + 1) * 128, CTX + SEQ:UCOLS], ust[ui][:, 0:w], B_uT, UB[ui])
                    else:
                        kb.dma(sp, uT[m * 128:(m + 1) * 128, CTX + t0:CTX + t0 + w], ust[ui][:, 0:w], B_uT, UB[ui])
                else:
                    j = m - 8 if m < 16 else m - 16
                    nj = 8 if m < 16 else 4
                    kb.op(dve, lambda: V.tensor_copy(qcT[:, j, 0:w], ps[pb][:, 0:w]), reads=[PB[pb]], parts=[B_qc])
                    si = sqctr[0] % 2
                    sqctr[0] += 1
                    kb.op(act, lambda: S.activation(sqb[si][:, 0:w], qcT[:, j, 0:w], AF.Square), reads=[B_qc],
                          writes=[SQB[si]])
                    if P1_STAGE != 5:
                        mm(ps[4][:, 0:w], onesb[:], sqb[si][:, 0:w], j == 0, j == nj - 1, reads=[SQB[si], B_c],
                           writes=[PB[4]] if j == 0 else (), parts=() if j == 0 else [PB[4]], signal=True)
                    if j == nj - 1 and P1_STAGE not in (4, 5):
                        nfeat = 1024.0 if m < 16 else 512.0
                        kb.op(act, lambda: S.activation(rstd[:, 0:w], ps[4][:, 0:w], AF.Sqrt, bias=EPS,
                                                        scale=1.0 / nfeat), reads=[PB[4]], writes=[B_rstd])
                        kb.op(dve, lambda: V.reciprocal(rstd[:, 0:w], rstd[:, 0:w]), reads=[B_rstd], parts=[B_rstd])
                        for jj in range(nj):
                            if m < 16:
                                ui = uctr[0] % 3
                                uctr[0] += 1
                                kb.op(dve, lambda: V.scalar_tensor_tensor(ust[ui][:, 0:w], qcT[:, jj, 0:w],
                                                                          gqs[:, jj:jj + 1], rstd[:, 0:w], ALU.mult,
                                                                          ALU.mult),
                                      reads=[B_qc, B_rstd, B_c], writes=[UB[ui]])
                                kb.dma(sp, qcnTs[jj * 128:(jj + 1) * 128, t0:t0 + w], ust[ui][:, 0:w], B_qcn, UB[ui])
                            else:
                                kb.op(dve, lambda: V.scalar_tensor_tensor(kvcnT[:, jj, keyc:keyc + w],
                                                                          qcT[:, jj, 0:w], gkvs[:, jj:jj + 1],
                                                                          rstd[:, 0:w], ALU.mult, ALU.mult),
                                      reads=[B_qc, B_rstd, B_c], parts=[B_kvcn])
            else:
                kb.dma(sp, rtab[:, :, 0:w], I.ropek[:, :, keyc:keyc + w], B_rtab, B_in)
                for half in range(2):
                    pb = 5 + half
                    for k in range(KT):
                        mm(ps[pb][0:64, 0:w], wt[:, k * 128 + half * 64:k * 128 + half * 64 + 64], hmod[:, k, 0:w],
                           k == 0, k == KT - 1, reads=[wb, B_h], writes=[PB[pb]] if k == 0 else (),
                           parts=() if k == 0 else [PB[pb]], signal=(k == KT - 1))
                kb.op(dve, lambda: V.tensor_tensor(rtmp[:, 0:w], ps[5][0:64, 0:w], rtab[:, 0, 0:w], ALU.mult),
                      reads=[PB[5], B_rtab], writes=[B_rtmp])
                kb.op(dve, lambda: V.tensor_tensor(rtab[:, 1, 0:w], ps[6][0:64, 0:w], rtab[:, 1, 0:w], ALU.mult),
                      reads=[PB[6], B_rtab], parts=[B_rtab])
                kb.op(dve, lambda: V.tensor_tensor(krT[0:64, keyc:keyc + w], rtmp[:, 0:w], rtab[:, 1, 0:w], ALU.add),
                      reads=[B_rtmp, B_rtab], parts=[B_krT])

    tiles = [("ctx", 0, CTX)] + [("own", a, b) for a, b in OWN_TILES] + [("rest", a, b) for a, b in REST_TILES]
    if P1_LIMIT is not None:
        tiles = tiles[:P1_LIMIT]
    if "p1" in SKIP:
        tiles = []
    for (kind, t0, w) in tiles:
        p1_tile(kind, t0, w)
        adaln(13)
    adaln(200)
    vec_mul(4, 1, 64, 0)
    vec_scale(5, 2, 128, 0)
    vec_copy(6, 96, 0)
    vec_mul(7, 3, 160, 0)
    kb.barrier()
    st.close()
    wst.close()
    if STOP_AFTER == "p1":
        d1 = nc.dram_tensor("dbg_kvcn", [128, 4, NKEY], BF16, kind="ExternalOutput").ap()
        d2 = nc.dram_tensor("dbg_krT", [64, NKEY], BF16, kind="ExternalOutput").ap()
        d3 = nc.dram_tensor("dbg_vecs", [128, 8, KT], F32, kind="ExternalOutput").ap()
        if P1_LIMIT is None and P1_STAGE == 0:
            kb.dma(sp, d1, kvcnT[:], B_out, B_kvcn)
            kb.dma(sp, d2, krT[0:64, :], B_out, B_krT)
        kb.dma(sp, d3, vecs[:], B_out, B_vecs)
        return finish(nc, es, kb, out, B_out)

    st = contextlib.ExitStack()
    wk = sb(st, "wk", [128, 4 * 3072], BF16)
    wv = sb(st, "wv", [128, 4 * 3072], BF16)
    B_wk, B_wv = Buf("wk"), Buf("wv")
    kb.dma(pool, wk[:], I.wuk, B_wk, B_in)
    kb.dma(pool, wv[:], I.wuv, B_wv, B_in)
    kst = [sb(st, f"kst{i}", [128, 512], BF16) for i in range(4)]
    KSB = [Buf(f"kst{i}") for i in range(4)]
    ctr = 0
    for h in range(0 if "p2" in SKIP else NH):
        for c0 in range(0, NKEY, 512):
            w = min(512, NKEY - c0)
            pb = ctr % 2
            si = ctr % 4
            ctr += 1
            for rk in range(4):
                mm(ps[pb][:, 0:w], wk[:, rk * 3072 + h * 128:rk * 3072 + (h + 1) * 128], kvcnT[:, rk, c0:c0 + w],
                   rk == 0, rk == 3, reads=[B_wk, B_kvcn], writes=[PB[pb]] if rk == 0 else (),
                   parts=() if rk == 0 else [PB[pb]], signal=(rk == 3))
            if ctr % 2 == 0:
                kb.op(act, lambda: S.copy(kst[si][:, 0:w], ps[pb][:, 0:w]), reads=[PB[pb]], writes=[KSB[si]])
            else:
                kb.op(dve, lambda: V.tensor_copy(kst[si][:, 0:w], ps[pb][:, 0:w]), reads=[PB[pb]], writes=[KSB[si]])
            kb.dma(sp, KTs[h, :, c0:c0 + w], kst[si][:, 0:w], B_KT, KSB[si])
    for kt in range(0 if "p2" in SKIP else 34):
        for hg in range(6):
            pb = ctr % 2
            si = ctr % 4
            ctr += 1
            for rk in range(4):
                mm(ps[pb][:, 0:512], kvcnT[:, rk, kt * 128:(kt + 1) * 128],
                   wv[:, rk * 3072 + hg * 512:rk * 3072 + (hg + 1) * 512], rk == 0, rk == 3,
                   reads=[B_wv, B_kvcn], writes=[PB[pb]] if rk == 0 else (), parts=() if rk == 0 else [PB[pb]],
                   signal=(rk == 3))
            if ctr % 2 == 0:
                kb.op(act, lambda: S.copy(kst[si][:, :], ps[pb][:, :]), reads=[PB[pb]], writes=[KSB[si]])
            else:
                kb.op(dve, lambda: V.tensor_copy(kst[si][:, :], ps[pb][:, :]), reads=[PB[pb]], writes=[KSB[si]])
            kb.dma(sp, Vs[kt, :, hg * 512:(hg + 1) * 512], kst[si][:, :], B_V, KSB[si])
    kb.barrier()
    st.close()
    if STOP_AFTER == "p2":
        return finish(nc, es, kb, out, B_out)

    st = contextlib.ExitStack()
    TWO_PI = 6.283185307179586
    lam = sb(st, "lam", [128, 2, 64], F32)
    lsd = sb(st, "lsd", [128, 64], F32)
    Ball = sb(st, "Ball", [128, 64, 2, 32], F32)
    Call = sb(st, "Call", [128, 64, 2, 32], F32)
    dcol = sb(st, "dcol", [32, 32], F32)
    B_s5 = Buf("s5setup")
    B_BC = Buf("BC")
    kb.dma(sp, lam[:], I.s5lam, B_s5, B_in)
    kb.dma(sp, lsd[:], I.s5ls, B_s5, B_in, part=True)
    kb.dma(sp, dcol[:], I.s5d, B_s5, B_in, part=True)
    kb.op(dve, lambda: V.memset(Ball[:], 0.0), writes=[B_BC])
    kb.op(dve, lambda: V.memset(Call[:], 0.0), parts=[B_BC])
    kb.dma(sp, Ball[0:64, :, :, 0:16], I.s5b[0:64], B_BC, B_in)
    kb.dma(sp, Ball[64:128, :, :, 16:32], I.s5b[64:128], B_BC, B_in, part=True)
    kb.dma(sp, Call[0:64, :, :, 0:16], I.s5c[0:64], B_BC, B_in, part=True)
    kb.dma(sp, Call[64:128, :, :, 16:32], I.s5c[64:128], B_BC, B_in, part=True)
    nsc = [0]

    def stile(dt=F32, shape=(128, 64)):
        nsc[0] += 1
        return sb(st, f"s5t{nsc[0]}", list(shape), dt)

    def dv(f):
        kb.op(dve, f, reads=[B_s5], parts=[B_s5])

    def ac(f):
        kb.op(act, f, reads=[B_s5], parts=[B_s5])

    lr, li = lam[:, 0, :], lam[:, 1, :]
    dtt, mag, ang, nf, s2, s4, ch, sinr, cosr, t1, t2 = [stile() for _ in range(11)]
    ni = stile(mybir.dt.int32)
    ac(lambda: S.activation(dtt[:], lsd[:], AF.Exp))
    dv(lambda: V.tensor_tensor(t1[:], lr, dtt[:], ALU.mult))
    ac(lambda: S.activation(mag[:], t1[:], AF.Exp))
    dv(lambda: V.tensor_tensor(ang[:], li, dtt[:], ALU.mult))
    dv(lambda: V.tensor_scalar(t1[:], ang[:], 1.0 / TWO_PI, None, ALU.mult))
    dv(lambda: V.tensor_copy(ni[:], t1[:]))
    dv(lambda: V.tensor_copy(nf[:], ni[:]))
    dv(lambda: V.scalar_tensor_tensor(t2[:], nf[:], -TWO_PI, ang[:], ALU.mult, ALU.add))
    ac(lambda: S.activation(s2[:], t2[:], AF.Sin, scale=0.5))
    ac(lambda: S.activation(s4[:], t2[:], AF.Sin, scale=0.25))
    dv(lambda: V.tensor_tensor(t1[:], s4[:], s4[:], ALU.mult))
    dv(lambda: V.tensor_scalar(ch[:], t1[:], -2.0, 1.0, ALU.mult, ALU.add))
    dv(lambda: V.tensor_tensor(t1[:], s2[:], ch[:], ALU.mult))
    dv(lambda: V.tensor_scalar(sinr[:], t1[:], 2.0, None, ALU.mult))
    dv(lambda: V.tensor_tensor(t1[:], s2[:], s2[:], ALU.mult))
    dv(lambda: V.tensor_scalar(cosr[:], t1[:], -2.0, 1.0, ALU.mult, ALU.add))
    apw = sb(st, "apw", [128, 9, 2, 64], F32)
    napw = sb(st, "napw", [128, 9, 2, 64], F32)
    lev = sb(st, "lev", [128, 10, 2, 64], F32)
    nlev = sb(st, "nlev", [128, 10, 64], F32)
    ff = sb(st, "ff", [128, 2, 64], F32)
    nfi = stile()
    dv(lambda: V.memset(apw[:, 0, 0, :], 1.0))
    dv(lambda: V.memset(apw[:, 0, 1, :], 0.0))
    dv(lambda: V.tensor_tensor(apw[:, 1, 0, :], mag[:], cosr[:], ALU.mult))
    dv(lambda: V.tensor_tensor(apw[:, 1, 1, :], mag[:], sinr[:], ALU.mult))
    ar, ai = apw[:, 1, 0, :], apw[:, 1, 1, :]
    nr, den = stile(), stile()
    dv(lambda: V.tensor_scalar(nr[:], ar, -1.0, None, ALU.add))
    dv(lambda: V.tensor_tensor(t1[:], lr, lr, ALU.mult))
    dv(lambda: V.tensor_tensor(t2[:], li, li, ALU.mult))
    dv(lambda: V.tensor_tensor(den[:], t1[:], t2[:], ALU.add))
    dv(lambda: V.reciprocal(den[:], den[:]))
    dv(lambda: V.tensor_tensor(t1[:], nr[:], lr, ALU.mult))
    dv(lambda: V.tensor_tensor(t2[:], ai, li, ALU.mult))
    dv(lambda: V.tensor_tensor(t1[:], t1[:], t2[:], ALU.add))
    dv(lambda: V.tensor_tensor(ff[:, 0, :], t1[:], den[:], ALU.mult))
    dv(lambda: V.tensor_tensor(t1[:], ai, lr, ALU.mult))
    dv(lambda: V.tensor_tensor(t2[:], nr[:], li, ALU.mult))
    dv(lambda: V.tensor_tensor(t1[:], t1[:], t2[:], ALU.subtract))
    dv(lambda: V.tensor_tensor(ff[:, 1, :], t1[:], den[:], ALU.mult))
    dv(lambda: V.tensor_scalar(nfi[:], ff[:, 1, :], -1.0, None, ALU.mult))

    def cmul(o_r, o_i, a_r, a_i, b_r, b_i):
        dv(lambda: V.tensor_tensor(t1[:], a_r, b_r, ALU.mult))
        dv(lambda: V.tensor_tensor(t2[:], a_i, b_i, ALU.mult))
        dv(lambda: V.tensor_tensor(den[:], a_r, b_i, ALU.mult))
        dv(lambda: V.tensor_tensor(nr[:], a_i, b_r, ALU.mult))
        dv(lambda: V.tensor_tensor(o_r, t1[:], t2[:], ALU.subtract))
        dv(lambda: V.tensor_tensor(o_i, den[:], nr[:], ALU.add))

    for k in range(2, 9):
        cmul(apw[:, k, 0, :], apw[:, k, 1, :], apw[:, k - 1, 0, :], apw[:, k - 1, 1, :], ar, ai)
    dv(lambda: V.tensor_scalar(napw[:], apw[:], -1.0, None, ALU.mult))
    dv(lambda: V.tensor_copy(lev[:, 0, :, :], apw[:, 8, :, :]))
    for l in range(1, 10):
        cmul(lev[:, l, 0, :], lev[:, l, 1, :], lev[:, l - 1, 0, :], lev[:, l - 1, 1, :], lev[:, l - 1, 0, :],
             lev[:, l - 1, 1, :])
    dv(lambda: V.tensor_scalar(nlev[:], lev[:, :, 1, :], -1.0, None, ALU.mult))

    uTp = [sb(st, f"uTp{i}", [32, UCOLS], BF16) for i in range(2)]
    B_uTp = [Buf("uTp0"), Buf("uTp1")]
    bbar = sb(st, "bbar", [128, 2, 32], F32)
    B_bbar = Buf("bbar")
    Eb = sb(st, "Eb", [128, 8, 2, 32], BF16)
    B_Eb = Buf("Eb")
    Fb = [sb(st, f"Fb{i}", [128, 8, 2, 32], BF16) for i in range(2)]
    B_Fb = [Buf("Fb0"), Buf("Fb1")]
    Cb = sb(st, "Cb", [128, 2, 32], BF16)
    B_Cb = Buf("Cb")
    Bw = [sb(st, f"Bw{i}", [32, 8, 2, 128], BF16) for i in range(2)]
    B_Bw = [Buf("Bw0"), Buf("Bw1")]
    Kt = [sb(st, f"Kt{i}", [32, 8, 32], BF16) for i in range(2)]
    B_Kt = [Buf("Kt0"), Buf("Kt1")]
    K0 = sb(st, "K0", [32, 32], BF16)
    K0f = sb(st, "K0f", [32, 32], F32)
    B_K0 = Buf("K0")
    tmpE = [sb(st, f"tmpE{i}", [128, 32], F32) for i in range(4)]
    B_tmpE = [Buf(f"tmpE{i}") for i in range(4)]
    XA = [sb(st, f"XA{i}", [128, 2, NBA], F32) for i in range(2)]
    XBt = [sb(st, f"XB{i}", [128, 2, NBB], F32) for i in range(2)]
    B_XA = [Buf("XA0"), Buf("XA1")]
    B_XB = [Buf("XB0"), Buf("XB1")]
    SA = sb(st, "SA", [128, 2, NBA], BF16)
    SB_ = sb(st, "SB", [128, 2, NBB], BF16)
    B_SA, B_SB = Buf("SA"), Buf("SB")
    ys = [sb(st, f"ys{i}", [32, 512], F32) for i in range(3)]
    B_ys = [Buf(f"ys{i}") for i in range(3)]
    yo = [sb(st, f"yo{i}", [32, 512], BF16) for i in range(2)]
    B_yo = [Buf("yo0"), Buf("yo1")]
    yfs = [sb(st, f"yf{i}", [32, 512], F32) for i in range(2)]
    B_yfs = [Buf("yf0"), Buf("yf1")]
    tec = [0]
    psb6 = ps[6][:].bitcast(BF16)

    def two_term(out_ap, a_ap, sa, b_ap, sbb, reads, out_buf, part=True):
        i = tec[0] % 4
        tec[0] += 1
        kb.op(dve, lambda: V.tensor_scalar(tmpE[i][:], b_ap, sbb, None, ALU.mult), reads=reads + [B_s5],
              writes=[B_tmpE[i]])
        kb.op(dve, lambda: V.scalar_tensor_tensor(out_ap, a_ap, sa, tmpE[i][:], ALU.mult, ALU.add),
              reads=reads + [B_s5, B_tmpE[i]], parts=[out_buf])

    def hs_scan(X, BX, nblk, dp, forward):
        cur, s, l = 0, 1, 0
        while s < nblk:
            Pr, Pi, nPi = lev[:, l, 0, dp:dp + 1], lev[:, l, 1, dp:dp + 1], nlev[:, l, dp:dp + 1]
            o, n = X[cur], X[1 - cur]
            bo, bn = BX[cur], BX[1 - cur]
            if forward:
                d0, d1, s0, s1, k0, k1 = s, nblk, 0, nblk - s, 0, s
            else:
                d0, d1, s0, s1, k0, k1 = 0, nblk - s, s, nblk, nblk - s, nblk
            kb.op(dve, lambda: V.scalar_tensor_tensor(n[:, 0, d0:d1], o[:, 0, s0:s1], Pr, o[:, 0, d0:d1], ALU.mult,
                                                      ALU.add), reads=[bo, B_s5], writes=[bn])
            yield None
            kb.op(dve, lambda: V.scalar_tensor_tensor(n[:, 1, d0:d1], o[:, 0, s0:s1], Pi, o[:, 1, d0:d1], ALU.mult,
                                                      ALU.add), reads=[bo, B_s5], parts=[bn])
            yield None
            kb.op(dve, lambda: V.scalar_tensor_tensor(n[:, 0, d0:d1], o[:, 1, s0:s1], nPi, n[:, 0, d0:d1], ALU.mult,
                                                      ALU.add), reads=[bo, B_s5, bn], parts=[bn])
            yield None
            kb.op(dve, lambda: V.scalar_tensor_tensor(n[:, 1, d0:d1], o[:, 1, s0:s1], Pr, n[:, 1, d0:d1], ALU.mult,
                                                      ALU.add), reads=[bo, B_s5, bn], parts=[bn])
            kb.op(act, lambda: S.copy(n[:, :, k0:k1], o[:, :, k0:k1]), reads=[bo], parts=[bn])
            yield None
            cur, s, l = 1 - cur, s * 2, l + 1
        yield ("done", cur)

    YCH = [(0, 64), (64, 64), (128, 64), (192, 64), (256, 1)]
    ychk = [0]
    for pair in range(0 if "p3" in SKIP else 32):
        ui = pair % 2
        kb.dma(sp, uTp[ui][:], uT[pair * 32:(pair + 1) * 32, :], B_uTp[ui], B_uT)
        for dr in range(2):
            dp = dr * 32 + pair
            Br, Bi = Ball[:, dp, 0, :], Ball[:, dp, 1, :]
            Cr, Ci = Call[:, dp, 0, :], Call[:, dp, 1, :]
            fr, fi, nfi_ = ff[:, 0, dp:dp + 1], ff[:, 1, dp:dp + 1], nfi[:, dp:dp + 1]
            kb.op(dve, lambda: V.tensor_scalar(bbar[:, 0, :], Br, fr, None, ALU.mult), reads=[B_BC, B_s5],
                  writes=[B_bbar])
            kb.op(dve, lambda: V.scalar_tensor_tensor(bbar[:, 0, :], Bi, nfi_, bbar[:, 0, :], ALU.mult, ALU.add),
                  reads=[B_BC, B_s5, B_bbar], parts=[B_bbar])
            kb.op(dve, lambda: V.tensor_scalar(bbar[:, 1, :], Br, fi, None, ALU.mult), reads=[B_BC, B_s5],
                  parts=[B_bbar])
            kb.op(dve, lambda: V.scalar_tensor_tensor(bbar[:, 1, :], Bi, fr, bbar[:, 1, :], ALU.mult, ALU.add),
                  reads=[B_BC, B_s5, B_bbar], parts=[B_bbar])
            for k in range(8):
                akr, aki, naki = apw[:, k, 0, dp:dp + 1], apw[:, k, 1, dp:dp + 1], napw[:, k, 1, dp:dp + 1]
                two_term(Eb[:, k, 0, :], bbar[:, 0, :], akr, bbar[:, 1, :], naki, [B_bbar], B_Eb)
                two_term(Eb[:, k, 1, :], bbar[:, 0, :], aki, bbar[:, 1, :], akr, [B_bbar], B_Eb)
            for k in range(1, 9):
                akr, aki = apw[:, k, 0, dp:dp + 1], apw[:, k, 1, dp:dp + 1]
                nakr, naki = napw[:, k, 0, dp:dp + 1], napw[:, k, 1, dp:dp + 1]
                two_term(Fb[dr][:, k - 1, 0, :], Cr, akr, Ci, naki, [B_BC], B_Fb[dr])
                two_term(Fb[dr][:, k - 1, 1, :], Cr, naki, Ci, nakr, [B_BC], B_Fb[dr])
            kb.op(dve, lambda: V.tensor_copy(Cb[:, 0, :], Cr), reads=[B_BC], parts=[B_Cb])
            kb.op(dve, lambda: V.tensor_scalar(Cb[:, 1, :], Ci, -1.0, None, ALU.mult), reads=[B_BC], parts=[B_Cb])
            for ri in range(2):
                for sg_ in range(8):
                    k = (7 - sg_) if dr == 0 else sg_
                    kb.op(pe, lambda: T.transpose(psb6[0:32, sg_ * 128:(sg_ + 1) * 128], Eb[:, k, ri, :], identb[:, :]),
                          reads=[B_Eb, B_c], writes=[PB[6]] if sg_ == 0 else (), parts=() if sg_ == 0 else [PB[6]],
                          signal=(sg_ == 7))
                kb.op(act, lambda: S.copy(Bw[dr][:, :, ri, :], psb6[0:32, 0:1024].rearrange("p (s c) -> p s c", c=128)),
                      reads=[PB[6]], parts=[B_Bw[dr]])
            for tau in range(8):
                mm(ps[7][0:32, tau * 32:(tau + 1) * 32], Eb[:, tau, 0, :], Cb[:, 0, :], True, False,
                   reads=[B_Eb, B_Cb], writes=[PB[7]] if tau == 0 else (), parts=() if tau == 0 else [PB[7]])
                mm(ps[7][0:32, tau * 32:(tau + 1) * 32], Eb[:, tau, 1, :], Cb[:, 1, :], False, True,
                   reads=[B_Eb, B_Cb], parts=[PB[7]], signal=(tau == 7))
            kb.op(dve, lambda: V.tensor_copy(Kt[dr][:], ps[7][0:32, 0:256].rearrange("p (t c) -> p t c", c=32)),
                  reads=[PB[7]], writes=[B_Kt[dr]])
            if dr == 0:
                kb.op(dve, lambda: V.tensor_copy(K0f[:], ps[7][0:32, 0:32]), reads=[PB[7]], writes=[B_K0])
            else:
                kb.op(dve, lambda: V.tensor_tensor(K0f[:], K0f[:], ps[7][0:32, 0:32], ALU.add), reads=[PB[7], B_K0],
                      parts=[B_K0])
                kb.op(dve, lambda: V.scalar_tensor_tensor(K0f[:], ident[0:32, 0:32], dcol[:, pair:pair + 1], K0f[:],
                                                          ALU.mult, ALU.add), reads=[B_K0, B_c, B_s5], parts=[B_K0])
                kb.op(dve, lambda: V.tensor_copy(K0[:], K0f[:]), reads=[B_K0], parts=[B_K0])
        for ri in range(2):
            for sg_ in range(8):
                mm(ps[ri][:, 0:NBA], Bw[0][:, sg_, ri, :], uTp[ui][:, sg_:8 * NBA:8], sg_ == 0, sg_ == 7,
                   reads=[B_Bw[0], B_uTp[ui]], writes=[PB[ri]] if sg_ == 0 else (), parts=() if sg_ == 0 else [PB[ri]],
                   signal=(sg_ == 7))
            kb.op(act if ri == 0 else dve,
                  (lambda: S.copy(XA[0][:, ri, :], ps[ri][:, 0:NBA])) if ri == 0 else
                  (lambda: V.tensor_copy(XA[0][:, ri, :], ps[ri][:, 0:NBA])),
                  reads=[PB[ri]], writes=[B_XA[0]] if ri == 0 else (), parts=() if ri == 0 else [B_XA[0]])
        for ri in range(2):
            for c in range(2):
                pbk = 2 + 2 * ri + c
                base = CTX + 8 * 272 * c
                for sg_ in range(8):
                    mm(ps[pbk][:, 0:272], Bw[1][:, sg_, ri, :], uTp[ui][:, base + sg_:base + 8 * 272:8], sg_ == 0,
                       sg_ == 7, reads=[B_Bw[1], B_uTp[ui]], writes=[PB[pbk]] if sg_ == 0 else (),
                       parts=() if sg_ == 0 else [PB[pbk]], signal=(sg_ == 7))
                first = (ri == 0 and c == 0)
                kb.op(dve, lambda: V.tensor_copy(XBt[0][:, ri, 272 * c:272 * (c + 1)], ps[pbk][:, 0:272]),
                      reads=[PB[pbk]], writes=[B_XB[0]] if first else (), parts=() if first else [B_XB[0]])
        gens = [hs_scan(XA, B_XA, NBA, pair, True), hs_scan(XBt, B_XB, NBB, 32 + pair, False)]
        res_ = [None, None]
        while any(r_ is None for r_ in res_):
            for gi_ in range(2):
                if res_[gi_] is None:
                    v_ = next(gens[gi_])
                    if v_ is not None:
                        res_[gi_] = v_[1]
        ca, cb = res_
        kb.op(dve, lambda: V.tensor_copy(SA[:], XA[ca][:]), reads=[B_XA[ca]], writes=[B_SA])
        kb.op(dve, lambda: V.tensor_copy(SB_[:], XBt[cb][:]), reads=[B_XB[cb]], writes=[B_SB])
        for (jb0, nb) in YCH:
            yb = 6 + (ychk[0] % 2)
            ychk[0] += 1
            for sg_ in range(8):
                o_ap = ps[yb][0:32, sg_:8 * nb:8]
                kA, kB = sg_, 7 - sg_
                mm(o_ap, Fb[0][:, kA, 0, :], SA[:, 0, 31 + jb0:31 + jb0 + nb], True, False,
                   reads=[B_Fb[0], B_SA], writes=[PB[yb]] if sg_ == 0 else (), parts=() if sg_ == 0 else [PB[yb]])
                mm(o_ap, Fb[0][:, kA, 1, :], SA[:, 1, 31 + jb0:31 + jb0 + nb], False, False,
                   reads=[B_Fb[0], B_SA], parts=[PB[yb]])
                mm(o_ap, Fb[1][:, kB, 0, :], SB_[:, 0, jb0 + 1:jb0 + 1 + nb], False, False,
                   reads=[B_Fb[1], B_SB], parts=[PB[yb]])
                mm(o_ap, Fb[1][:, kB, 1, :], SB_[:, 1, jb0 + 1:jb0 + 1 + nb], False, False,
                   reads=[B_Fb[1], B_SB], parts=[PB[yb]])
                for sp_ in range(8):
                    if sp_ < sg_:
                        l_ap = Kt[0][:, sg_ - sp_, :]
                    elif sp_ > sg_:
                        l_ap = Kt[1][:, sp_ - sg_, :]
                    else:
                        l_ap = K0[:, :]
                    c0 = CTX + 8 * jb0 + sp_
                    mm(o_ap, l_ap, uTp[ui][:, c0:CTX + 8 * (jb0 + nb):8], False, sp_ == 7,
                       reads=[B_Kt[0], B_Kt[1], B_K0, B_uTp[ui]], parts=[PB[yb]], signal=(sp_ == 7 and sg_ == 7))
            n8 = 8 * nb
            yi = ychk[0] % 3
            yf = yfs[ychk[0] % 2]
            B_yf = B_yfs[ychk[0] % 2]
            kb.op(dve, lambda: V.tensor_copy(yf[:, 0:n8], ps[yb][0:32, 0:n8]), reads=[PB[yb]], writes=[B_yf])
            kb.op(act, lambda: S.activation(ys[yi][:, 0:n8], yf[:, 0:n8], AF.Square), reads=[B_yf],
                  writes=[B_ys[yi]])
            kb.op(dve, lambda: V.tensor_scalar(ys[yi][:, 0:n8], ys[yi][:, 0:n8], 0.044715, 1.0, ALU.mult, ALU.add),
                  reads=[B_ys[yi]], parts=[B_ys[yi]])
            kb.op(dve, lambda: V.tensor_tensor(ys[yi][:, 0:n8], ys[yi][:, 0:n8], yf[:, 0:n8], ALU.mult),
                  reads=[B_ys[yi], B_yf], parts=[B_ys[yi]])
            kb.op(act, lambda: S.activation(ys[yi][:, 0:n8], ys[yi][:, 0:n8], AF.Sigmoid, scale=1.5957691216),
                  reads=[B_ys[yi]], parts=[B_ys[yi]])
            oi = ychk[0] % 2
            kb.op(dve, lambda: V.tensor_tensor(yo[oi][:, 0:n8], ys[yi][:, 0:n8], yf[:, 0:n8], ALU.mult),
                  reads=[B_ys[yi], B_yf], writes=[B_yo[oi]])
            kb.dma(sp, yactT[pair * 32:(pair + 1) * 32, 8 * jb0:8 * jb0 + n8], yo[oi][:, 0:n8], B_yact, B_yo[oi])
    kb.barrier()
    st.close()
    if STOP_AFTER == "p3a":
        return finish(nc, es, kb, out, B_out)
    st = contextlib.ExitStack()
    wg = sb(st, "wg", [128, 8, 8 * 128], BF16)
    B_wg = Buf("wg")
    for m in range(8):
        kb.dma(pool, wg[:, m, :], I.wglu[m], B_wg, B_in, part=(m > 0))
    ya = [sb(st, f"ya{i}", [128, 8, NT], BF16) for i in range(2)]
    B_ya = [Buf("ya0"), Buf("ya1")]
    sgt = [sb(st, f"sgt{i}", [128, NT], F32) for i in range(2)]
    B_sgt = [Buf("sgt0"), Buf("sgt1")]
    go = [sb(st, f"go{i}", [128, NT], BF16) for i in range(2)]
    B_go = [Buf("go0"), Buf("go1")]
    gctr = 0
    for ti, (t0, w) in enumerate([] if "p3" in SKIP else OWN_TILES):
        yi = ti % 2
        for k in range(8):
            kb.dma(sp, ya[yi][:, k, 0:w], yactT[k * 128:(k + 1) * 128, t0:t0 + w], B_ya[yi], B_yact, part=(k > 0))
        for m in range(8):
            pb = m % 2
            for k in range(8):
                mm(ps[pb][:, 0:w], wg[:, m, k * 128:(k + 1) * 128], ya[yi][:, k, 0:w], k == 0, k == 7,
                   reads=[B_wg, B_ya[yi]], writes=[PB[pb]] if k == 0 else (), parts=() if k == 0 else [PB[pb]],
                   signal=(k == 7))
            gi = gctr % 2
            gctr += 1
            kb.op(act, lambda: S.activation(sgt[gi][:, 0:w], ps[pb][:, 0:w], AF.Sigmoid), reads=[PB[pb]],
                  writes=[B_sgt[gi]])
            kb.op(dve, lambda: V.tensor_tensor(go[gi][:, 0:w], sgt[gi][:, 0:w], ya[yi][:, m, 0:w], ALU.mult),
                  reads=[B_sgt[gi], B_ya[yi]], writes=[B_go[gi]])
            kb.dma(sp, s5outT[m * 128:(m + 1) * 128, t0:t0 + w], go[gi][:, 0:w], B_s5o, B_go[gi])
    kb.barrier()
    st.close()
    if STOP_AFTER == "p3":
        return finish(nc, es, kb, out, B_out)

    st = contextlib.ExitStack()
    qcn = sb(st, "qcn", [128, 8, NOWN], BF16)
    B_qcnS = Buf("qcnS")
    for j in range(0 if "p4" in SKIP else 8):
        kb.dma(sp, qcn[:, j, :], qcnTs[j * 128:(j + 1) * 128, :], B_qcnS, B_qcn, part=(j > 0))
    rq = sb(st, "rq", [64, 2, NOWN], F32)
    B_rq = Buf("rq")
    kb.dma(sp, rq[:], I.ropeq, B_rq, B_in)
    kth = [sb(st, f"kth{i}", [128, NKEY], BF16) for i in range(2)]
    vh = [sb(st, f"vh{i}", [128, 34, 128], BF16) for i in range(2)]
    wq = [sb(st, f"wq{i}", [128, 8 * 256], BF16) for i in range(2)]
    B_kth = [Buf("kth0"), Buf("kth1")]
    B_vh = [Buf("vh0"), Buf("vh1")]
    B_wq = [Buf("wq0"), Buf("wq1")]
    qn = [sb(st, f"qn{i}", [128, NT], BF16) for i in range(2)]
    qr = [sb(st, f"qr{i}", [128, NT], BF16) for i in range(2)]
    B_qn = [Buf("qn0"), Buf("qn1")]
    B_qr = [Buf("qr0"), Buf("qr1")]
    for i_ in range(2):
        kb.op(dve, lambda: V.memset(qr[i_][64:128, :], 0.0), writes=[B_qr[i_]])
    qtmp = sb(st, "qtmp", [64, NT], F32)
    qtmp2 = sb(st, "qtmp2", [64, NT], F32)
    B_qtmp, B_qtmp2 = Buf("qtmp"), Buf("qtmp2")
    NPS = 4
    pT = [sb(st, f"pT{i}", [128, NT], BF16) for i in range(NPS)]
    B_pT = [Buf(f"pT{i}") for i in range(NPS)]
    accs = [sb(st, f"acc{i}", [128, NT], F32) for i in range(2)]
    B_accs = [Buf("acc0"), Buf("acc1")]
    rinv = sb(st, "rinv", [128, NT], F32)
    B_rinv = Buf("rinv")
    ast = [sb(st, f"ast{i}", [128, NT], BF16) for i in range(2)]
    B_ast = [Buf("ast0"), Buf("ast1")]

    def load_head(h):
        i = h % 2
        kb.dma(sp, kth[i][:], KTs[h], B_kth[i], B_KT)
        kb.dma(sp, vh[i][:], Vs[:, :, h * 128:(h + 1) * 128].rearrange("k p d -> p k d"), B_vh[i], B_V)
        kb.dma(pool, wq[i][:], I.wuq[h], B_wq[i], B_in)

    work = [(h, t0, w) for h in range(0 if "p4" in SKIP else NH) for (t0, w) in OWN_TILES]

    def emit_qproj(idx):
        h, t0, w = work[idx]
        hi, qi = h % 2, idx % 2
        for k in range(8):
            mm(ps[0][:, 0:w], wq[hi][:, k * 256:k * 256 + 128], qcn[:, k, t0:t0 + w], k == 0, k == 7,
               reads=[B_wq[hi], B_qcnS], writes=[PB[0]] if k == 0 else (), parts=() if k == 0 else [PB[0]],
               signal=(k == 7))
        for half in range(2):
            pb = 1 + half
            for k in range(8):
                mm(ps[pb][0:64, 0:w], wq[hi][:, k * 256 + 128 + half * 64:k * 256 + 192 + half * 64],
                   qcn[:, k, t0:t0 + w], k == 0, k == 7, reads=[B_wq[hi], B_qcnS],
                   writes=[PB[pb]] if k == 0 else (), parts=() if k == 0 else [PB[pb]], signal=(k == 7))
        kb.op(act, lambda: S.copy(qn[qi][:, 0:w], ps[0][:, 0:w]), reads=[PB[0]], writes=[B_qn[qi]])
        kb.op(dve, lambda: V.tensor_tensor(qtmp[:, 0:w], ps[1][0:64, 0:w], rq[:, 0, t0:t0 + w], ALU.mult),
              reads=[PB[1], B_rq], writes=[B_qtmp])
        kb.op(dve, lambda: V.tensor_tensor(qtmp2[:, 0:w], ps[2][0:64, 0:w], rq[:, 1, t0:t0 + w], ALU.mult),
              reads=[PB[2], B_rq], writes=[B_qtmp2])
        kb.op(dve, lambda: V.tensor_tensor(qr[qi][0:64, 0:w], qtmp[:, 0:w], qtmp2[:, 0:w], ALU.add),
              reads=[B_qtmp, B_qtmp2], writes=[B_qr[qi]])

    if work:
        load_head(0)
    cast_jobs = []
    for m in range(KT):
        cast_jobs.append((wout_b[m].rearrange("p (a b) -> (p a) b", b=2048),
                          I.wout[m].rearrange("p (a b) -> (p a) b", b=2048)))
    for ft in range(FT):
        cast_jobs.append((wffg_b[ft].rearrange("p (a b) -> (p a) b", b=2048),
                          I.wffg[ft].rearrange("p (a b) -> (p a) b", b=2048)))
        cast_jobs.append((wffu_b[ft].rearrange("p (a b) -> (p a) b", b=2048),
                          I.wffu[ft].rearrange("p (a b) -> (p a) b", b=2048)))
    for m in range(KT):
        cast_jobs.append((wdn_b[m].rearrange("p (a b) -> (p a) b", b=1376),
                          I.wdn[m].rearrange("p (a b) -> (p a) b", b=1376)))

    def issue_casts(n):
        for _ in range(n):
            if cast_jobs:
                o_, i_ = cast_jobs.pop(0)
                kb.dma(pool, o_, i_, B_wcast, B_in)

    if work:
        emit_qproj(0)
    pctr = 0
    ob = 3
    for idx, (h, t0, w) in enumerate(work):
        hi, qi = h % 2, idx % 2
        if t0 == 0 and h + 1 < NH:
            load_head(h + 1)
        issue_casts(2)

        def score(kt):
            sbk = 4 + (kt % 3)
            mm(ps[sbk][:, 0:w], kth[hi][:, kt * 128:(kt + 1) * 128], qn[qi][:, 0:w], True, False,
               reads=[B_kth[hi], B_qn[qi]], writes=[PB[sbk]])
            mm(ps[sbk][:, 0:w], krT[:, kt * 128:(kt + 1) * 128], qr[qi][:, 0:w], False, True,
               reads=[B_krT, B_qr[qi]], parts=[PB[sbk]], signal=True)

        score(0)
        score(1)
        for kt in range(34):
            sbk = 4 + (kt % 3)
            pi = pctr % NPS
            pctr += 1
            kb.op(act, lambda: S.activation(pT[pi][:, 0:w], ps[sbk][:, 0:w], AF.Exp, scale=MLA_SCALE),
                  reads=[PB[sbk]], writes=[B_pT[pi]])
            acc, B_acc = accs[kt % 2], B_accs[kt % 2]
            if kt < 2:
                kb.op(dve, lambda: V.tensor_copy(acc[:, 0:w], pT[pi][:, 0:w]), reads=[B_pT[pi]], writes=[B_acc])
            else:
                kb.op(dve, lambda: V.tensor_tensor(acc[:, 0:w], acc[:, 0:w], pT[pi][:, 0:w], ALU.add),
                      reads=[B_pT[pi], B_acc], parts=[B_acc])
            if kt + 2 < 34:
                score(kt + 2)
            if kt == 12 and idx + 1 < len(work):
                emit_qproj(idx + 1)
            mm(ps[ob][:, 0:w], vh[hi][:, kt, :], pT[pi][:, 0:w], kt == 0, kt == 33,
               reads=[B_vh[hi], B_pT[pi]], writes=[PB[ob]] if kt == 0 else (), parts=() if kt == 0 else [PB[ob]],
               signal=(kt == 33))
        mm(ps[7][:, 0:w], onesf[:], accs[0][:, 0:w], True, False, reads=[B_c, B_accs[0]], writes=[PB[7]])
        mm(ps[7][:, 0:w], onesf[:], accs[1][:, 0:w], False, True, reads=[B_c, B_accs[1]], parts=[PB[7]],
           signal=True)
        kb.op(dve, lambda: V.reciprocal(rinv[:, 0:w], ps[7][:, 0:w]), reads=[PB[7]], writes=[B_rinv])
        ai = idx % 2
        kb.op(dve, lambda: V.tensor_tensor(ast[ai][:, 0:w], ps[ob][:, 0:w], rinv[:, 0:w], ALU.mult),
              reads=[PB[ob], B_rinv], writes=[B_ast[ai]])
        kb.dma(sp, attT[h * 128:(h + 1) * 128, t0:t0 + w], ast[ai][:, 0:w], B_att, B_ast[ai])
    issue_casts(10000)
    kb.barrier()
    st.close()
    kvst.close()
    if STOP_AFTER == "p4":
        return finish(nc, es, kb, out, B_out)

    st = contextlib.ExitStack()
    mixin = sb(st, "mixin", [128, KT, NT], BF16)
    B_mixin = Buf("mixin")
    mixT = sb(st, "mixT", [128, KT, NT], F32)
    B_mixT = Buf("mixT")
    xT = sb(st, "xT", [128, KT, NT], F32)
    B_xT = Buf("xT")
    xch = [sb(st, f"xch{i}", [128, D], F32) for i in range(2)]
    XB = [Buf(f"xch{i}") for i in range(2)]
    NWS = 3
    wslot = [sb(st, f"wslot{i}", [128, KT * 128], BF16) for i in range(NWS)]
    WB = [Buf(f"wslot{i}") for i in range(NWS)]
    sqb = [sb(st, f"sqb{i}", [128, NT], BF16) for i in range(2)]
    SQB = [Buf("sqb0"), Buf("sqb1")]
    rstd = sb(st, "rstd", [128, NT], F32)
    B_rstd = Buf("rstd")
    tmpf = [sb(st, f"tmpf{i}", [128, NT], F32) for i in range(2)]
    B_tmpf = [Buf("tmpf0"), Buf("tmpf1")]
    hst = [sb(st, f"hst{i}", [128, NT], BF16) for i in range(2)]
    B_hst = [Buf("hst0"), Buf("hst1")]
    wctr[0] = 0

    def load_w5(src_ap):
        i = wctr[0] % NWS
        wctr[0] += 1
        kb.dma(pool, wslot[i][:], src_ap, WB[i], B_wcast)
        return wslot[i], WB[i]

    def rstd_from(psb, w, nfeat):
        kb.op(act, lambda: S.activation(rstd[:, 0:w], ps[psb][:, 0:w], AF.Sqrt, bias=EPS, scale=1.0 / nfeat),
              reads=[PB[psb]], writes=[B_rstd])
        kb.op(dve, lambda: V.reciprocal(rstd[:, 0:w], rstd[:, 0:w]), reads=[B_rstd], parts=[B_rstd])

    xctr[0] = 0
    sq5 = [0]
    for (t0, w) in ([] if "p5a" in SKIP else OWN_TILES[:P5_LIMIT]):
        for k in range(KT):
            src = s5outT[k * 128:(k + 1) * 128, t0:t0 + w] if k < 8 else attT[(k - 8) * 128:(k - 7) * 128, t0:t0 + w]
            kb.dma(sp, mixin[:, k, 0:w], src, B_mixin, B_s5o if k < 8 else B_att, part=(k > 0))
        c0 = 0
        while c0 < w:
            cw = min(128, w - c0)
            xi = xctr[0] % 2
            xctr[0] += 1
            kb.dma(sp, xch[xi][0:cw, :], I.xl[t0 + c0:t0 + c0 + cw, :], XB[xi], B_in)
            for k4 in range(8):
                pb = k4 % 2
                for j in range(4):
                    k = k4 * 4 + j
                    kb.op(pe, lambda: T.transpose(ps[pb][:, j * 128:j * 128 + cw],
                                                  xch[xi][0:cw, k * 128:(k + 1) * 128], ident[0:cw, 0:cw]),
                          reads=[XB[xi], B_c], writes=[PB[pb]] if j == 0 else (), parts=() if j == 0 else [PB[pb]],
                          signal=(j == 3))
                for j in range(4):
                    k = k4 * 4 + j
                    if pb == 0:
                        kb.op(dve, lambda: V.tensor_copy(xT[:, k, c0:c0 + cw], ps[pb][:, j * 128:j * 128 + cw]),
                              reads=[PB[pb]], parts=[B_xT])
                    else:
                        kb.op(act, lambda: S.copy(xT[:, k, c0:c0 + cw], ps[pb][:, j * 128:j * 128 + cw]),
                              reads=[PB[pb]], parts=[B_xT])
            c0 += cw
        for m in range(KT):
            wt, wb = load_w5(wout_b[m])
            pb = 2 + (m % 2)
            for k in range(KT):
                mm(ps[pb][:, 0:w], wt[:, k * 128:(k + 1) * 128], mixin[:, k, 0:w], k == 0, k == KT - 1,
                   reads=[wb, B_mixin], writes=[PB[pb]] if k == 0 else (), parts=() if k == 0 else [PB[pb]],
                   signal=(k == KT - 1))
            kb.op(dve, lambda: V.tensor_copy(mixT[:, m, 0:w], ps[pb][:, 0:w]), reads=[PB[pb]], parts=[B_mixT])
            si = sq5[0] % 2
            sq5[0] += 1
            kb.op(act, lambda: S.activation(sqb[si][:, 0:w], mixT[:, m, 0:w], AF.Square), reads=[B_mixT],
                  writes=[SQB[si]])
            mm(ps[4][:, 0:w], onesb[:], sqb[si][:, 0:w], m == 0, m == KT - 1, reads=[SQB[si], B_c],
               writes=[PB[4]] if m == 0 else (), parts=() if m == 0 else [PB[4]], signal=True)
        rstd_from(4, w, float(D))
        for m in range(KT):
            ti = m % 2
            kb.op(dve, lambda: V.tensor_tensor(tmpf[ti][:, 0:w], mixT[:, m, 0:w], rstd[:, 0:w], ALU.mult),
                  reads=[B_mixT, B_rstd], writes=[B_tmpf[ti]])
            kb.op(dve, lambda: V.scalar_tensor_tensor(xT[:, m, 0:w], tmpf[ti][:, 0:w], vecs[:, 4, m:m + 1],
                                                      xT[:, m, 0:w], ALU.mult, ALU.add),
                  reads=[B_tmpf[ti], B_vecs, B_xT], parts=[B_xT])
            kb.dma(sp, xmidT[m * 128:(m + 1) * 128, t0:t0 + w], xT[:, m, 0:w], B_xmid, B_xT)
            si = sq5[0] % 2
            sq5[0] += 1
            kb.op(act, lambda: S.activation(sqb[si][:, 0:w], xT[:, m, 0:w], AF.Square), reads=[B_xT],
                  writes=[SQB[si]])
            mm(ps[5][:, 0:w], onesb[:], sqb[si][:, 0:w], m == 0, m == KT - 1, reads=[SQB[si], B_c],
               writes=[PB[5]] if m == 0 else (), parts=() if m == 0 else [PB[5]], signal=True)
        rstd_from(5, w, float(D))
        for m in range(KT):
            ti = m % 2
            kb.op(dve, lambda: V.tensor_tensor(tmpf[ti][:, 0:w], xT[:, m, 0:w], rstd[:, 0:w], ALU.mult),
                  reads=[B_xT, B_rstd], writes=[B_tmpf[ti]])
            kb.op(dve, lambda: V.tensor_scalar(hst[ti][:, 0:w], tmpf[ti][:, 0:w], vecs[:, 5, m:m + 1],
                                               vecs[:, 6, m:m + 1], ALU.mult, ALU.add),
                  reads=[B_tmpf[ti], B_vecs], writes=[B_hst[ti]])
            kb.dma(sp, hxT[m * 128:(m + 1) * 128, t0:t0 + w], hst[ti][:, 0:w], B_hx, B_hst[ti])
    kb.barrier()
    st.close()
    if STOP_AFTER == "p5a":
        return finish(nc, es, kb, out, B_out)

    st = contextlib.ExitStack()
    FT_TILES = [(0, 410), (410, 410), (820, 410), (1230, 410), (1640, 408)]
    hx = sb(st, "hx", [128, KT, NT + 2], BF16)
    B_hxs = Buf("hxs")
    actT = sb(st, "actT", [128, FT, NT], BF16)
    B_act = Buf("actT")
    NWS = 8
    wpool = sb(st, "wpool", [128, 33024], BF16)
    wslot = [wpool[:, i * 4096:(i + 1) * 4096] for i in range(NWS)]
    WB = [Buf(f"wslot{i}") for i in range(NWS)]
    dslot = [wpool[:, j * 11008:(j + 1) * 11008] for j in range(3)]
    DB = [Buf(f"dslot{j}") for j in range(3)]

    def fence(q, bufs):
        for b_ in bufs:
            q.wait_all(b_.r)
            q.wait_all(b_.w)
    cws = sb(st, "cws", [128, FT, 4], F32)
    B_cw = Buf("cw")
    kb.dma(sp, cws[:], I.convw, B_cw, B_in)
    cv = [sb(st, f"cv{i}", [128, NT], F32) for i in range(2)]
    B_cv = [Buf("cv0"), Buf("cv1")]
    sg = [sb(st, f"sg{i}", [128, NT], F32) for i in range(2)]
    B_sg = [Buf("sg0"), Buf("sg1")]
    sqb = [sb(st, f"sqb{i}", [128, NT], BF16) for i in range(2)]
    SQB = [Buf("sqb0"), Buf("sqb1")]
    rstds = [sb(st, f"rstd{i}", [128, NT], F32) for i in range(2)]
    B_rstds = [Buf("rstd0"), Buf("rstd1")]
    fst = [sb(st, f"fst{i}", [128, NT], F32) for i in range(2)]
    B_fst = [Buf("fst0"), Buf("fst1")]
    xm2s = [sb(st, f"xm2_{i}", [128, 2, NT], F32) for i in range(2)]
    B_xm2s = [Buf("xm2_0"), Buf("xm2_1")]
    f2s = [sb(st, f"f2_{i}", [128, 2, NT], F32) for i in range(2)]
    B_f2s = [Buf("f2_0"), Buf("f2_1")]
    ost = [sb(st, f"ost{i}", [128, 512], F32) for i in range(2)]
    B_ost = [Buf("ost0"), Buf("ost1")]
    wctr[0] = 0
    dctr = 0
    sqc = 0
    octr = 0
    pending_out = []
    def out_a(ti5, t0, w, mg, rsel, bi):
        rstd = rstds[rsel]
        B_rstd = B_rstds[rsel]
        f2, xm2, B_f2, B_xm2 = f2s[bi], xm2s[bi], B_f2s[bi], B_xm2s[bi]
        kb.dma(sp, f2[:, :, 0:w], fTs[ti5, mg * 256:(mg + 1) * 256, 0:w].rearrange("(a p) t -> p a t", p=128),
               B_f2, B_f)
        kb.dma(sp, xm2[:, :, 0:w], xmidT[mg * 256:(mg + 1) * 256, t0:t0 + w].rearrange("(a p) t -> p a t", p=128),
               B_xm2, B_xmid)
        for a in range(2):
            m = mg * 2 + a
            kb.op(dve, lambda: V.tensor_tensor(f2[:, a, 0:w], f2[:, a, 0:w], rstd[:, 0:w], ALU.mult),
                  reads=[B_f2, B_rstd], parts=[B_f2])
            kb.op(dve, lambda: V.scalar_tensor_tensor(f2[:, a, 0:w], f2[:, a, 0:w], vecs[:, 7, m:m + 1],
                                                      xm2[:, a, 0:w], ALU.mult, ALU.add),
                  reads=[B_f2, B_xm2, B_vecs], parts=[B_f2])

    def out_b(ti5, t0, w, mg, rsel, bi):
        nonlocal octr
        f2, B_f2 = f2s[bi], B_f2s[bi]
        c0 = 0
        while c0 < w:
            cw = min(128, w - c0)
            for a in range(2):
                kb.op(pe, lambda: T.transpose(ps[7][0:cw, a * 128:(a + 1) * 128], f2[:, a, c0:c0 + cw],
                                              ident[:, :]),
                      reads=[B_f2, B_c], writes=[PB[7]] if a == 0 else (), parts=() if a == 0 else [PB[7]],
                      signal=(a == 1))
            oi = octr % 2
            octr += 1
            kb.op(act, lambda: S.copy(ost[oi][0:cw, 0:256], ps[7][0:cw, 0:256]), reads=[PB[7]],
                  writes=[B_ost[oi]])
            kb.dma(sp, out[t0 + c0:t0 + c0 + cw, mg * 256:(mg + 1) * 256], ost[oi][0:cw, 0:256], B_out, B_ost[oi])
            c0 += cw

    in_flight = []

    def out_pump():
        nxt = pending_out.pop(0) if pending_out else None
        if nxt is not None:
            out_a(*nxt)
        if in_flight:
            out_b(*in_flight.pop(0))
        if nxt is not None:
            in_flight.append(nxt)

    for ti5, (t0, w) in enumerate(FT_TILES):
        rsel = ti5 % 2
        rstd = rstds[rsel]
        B_rstd = B_rstds[rsel]
        if t0 == 0:
            kb.op(dve, lambda: V.memset(hx[:, :, 0:1], 0.0), writes=[B_hxs])
            for k in range(KT):
                kb.dma(sp, hx[:, k, 1:w + 2], hxT[k * 128:(k + 1) * 128, 0:w + 1], B_hxs, B_hx, part=True)
        else:
            for k in range(KT):
                kb.dma(sp, hx[:, k, 0:w + 2], hxT[k * 128:(k + 1) * 128, t0 - 1:t0 + w + 1], B_hxs, B_hx,
                       part=(k > 0))
        for ft in range(FT):
            if ft % 5 == 2 and (pending_out or in_flight):
                out_pump()
            if ft == 0:
                fence(pool, DB)
            i = wctr[0] % NWS
            wctr[0] += 2
            kb.dma(pool, wslot[i], wffg_b[ft], WB[i], B_wcast)
            kb.dma(pool, wslot[i + 1], wffu_b[ft], WB[i + 1], B_wcast)
            gb, ub = (0, 1) if ft % 2 == 0 else (2, 3)
            for k in range(KT):
                mm(ps[gb][:, 0:w + 2], wslot[i][:, k * 128:(k + 1) * 128], hx[:, k, 0:w + 2], k == 0, k == KT - 1,
                   reads=[WB[i], B_hxs], writes=[PB[gb]] if k == 0 else (), parts=() if k == 0 else [PB[gb]],
                   signal=(k == KT - 1))
            for k in range(KT):
                mm(ps[ub][:, 0:w], wslot[i + 1][:, k * 128:(k + 1) * 128], hx[:, k, 1:w + 1], k == 0, k == KT - 1,
                   reads=[WB[i + 1], B_hxs], writes=[PB[ub]] if k == 0 else (), parts=() if k == 0 else [PB[ub]],
                   signal=(k == KT - 1))
            ci = ft % 2
            kb.op(dve, lambda: V.tensor_scalar(cv[ci][:, 0:w], ps[gb][:, 1:w + 1], cws[:, ft, 1:2], cws[:, ft, 3:4],
                                               ALU.mult, ALU.add), reads=[PB[gb], B_cw], writes=[B_cv[ci]])
            kb.op(dve, lambda: V.scalar_tensor_tensor(cv[ci][:, 0:w], ps[gb][:, 0:w], cws[:, ft, 0:1],
                                                      cv[ci][:, 0:w], ALU.mult, ALU.add),
                  reads=[PB[gb], B_cw, B_cv[ci]], parts=[B_cv[ci]])
            kb.op(dve, lambda: V.scalar_tensor_tensor(cv[ci][:, 0:w], ps[gb][:, 2:w + 2], cws[:, ft, 2:3],
                                                      cv[ci][:, 0:w], ALU.mult, ALU.add),
                  reads=[PB[gb], B_cw, B_cv[ci]], parts=[B_cv[ci]])
            kb.op(act, lambda: S.activation(sg[ci][:, 0:w], cv[ci][:, 0:w], AF.Silu), reads=[B_cv[ci]],
                  writes=[B_sg[ci]])
            kb.op(dve, lambda: V.tensor_tensor(actT[:, ft, 0:w], sg[ci][:, 0:w], ps[ub][:, 0:w], ALU.mult),
                  reads=[B_sg[ci], PB[ub]], parts=[B_act])
        for m in range(KT):
            di = dctr % 3
            dctr += 1
            if m == 0:
                fence(pool, WB)
            kb.dma(pool, dslot[di], wdn_b[m], DB[di], B_wcast)
            pb = 4 + (m % 2)
            for k in range(FT):
                mm(ps[pb][:, 0:w], dslot[di][:, k * 128:(k + 1) * 128], actT[:, k, 0:w], k == 0, k == FT - 1,
                   reads=[DB[di], B_act], writes=[PB[pb]] if k == 0 else (), parts=() if k == 0 else [PB[pb]],
                   signal=(k == FT - 1))
            fi = m % 2
            kb.op(dve, lambda: V.tensor_copy(fst[fi][:, 0:w], ps[pb][:, 0:w]), reads=[PB[pb]], writes=[B_fst[fi]])
            kb.dma(sp, fTs[ti5, m * 128:(m + 1) * 128, 0:w], fst[fi][:, 0:w], B_f, B_fst[fi])
            si = sqc % 2
            sqc += 1
            kb.op(act, lambda: S.activation(sqb[si][:, 0:w], fst[fi][:, 0:w], AF.Square), reads=[B_fst[fi]],
                  writes=[SQB[si]])
            mm(ps[6][:, 0:w], onesb[:], sqb[si][:, 0:w], m == 0, m == KT - 1, reads=[SQB[si], B_c],
               writes=[PB[6]] if m == 0 else (), parts=() if m == 0 else [PB[6]], signal=True)
        kb.op(act, lambda: S.activation(rstd[:, 0:w], ps[6][:, 0:w], AF.Sqrt, bias=EPS, scale=1.0 / D),
              reads=[PB[6]], writes=[B_rstd])
        kb.op(dve, lambda: V.reciprocal(rstd[:, 0:w], rstd[:, 0:w]), reads=[B_rstd], parts=[B_rstd])
        pending_out.extend([(ti5, t0, w, mg, rsel, mg % 2) for mg in range(16)])
    while pending_out or in_flight:
        out_pump()
    kb.barrier()
    st.close()
    return finish(nc, es, kb, out, B_out)


_EXTRA = []


def finish(nc, es, kb, out, B_out):
    kb.barrier()
    for s_ in _EXTRA:
        s_.close()
    es.close()
    return nc


def _cols(v):
    return np.ascontiguousarray(v.reshape(-1, 128).T)


def _wtiles(w):
    K, M = w.shape
    return np.ascontiguousarray(w.reshape(K // 128, 128, M // 128, 128).transpose(2, 1, 0, 3)).reshape(
        M // 128, 128, (K // 128) * 128)


def _rope_tables(tpos, is_x):
    inv = (10000.0 ** (-np.arange(16, dtype=np.float32) / 16)).astype(np.float32)
    t = tpos.astype(np.float32)
    row = np.floor(t / 64).astype(np.float32)
    col = (t - row * 64).astype(np.float32)
    ang = np.concatenate([row[:, None] * inv, col[:, None] * inv], axis=-1).astype(np.float32)
    cos, sin = np.cos(ang).astype(np.float32), np.sin(ang).astype(np.float32)
    cos = np.where(is_x[:, None], cos, 1.0).astype(np.float32)
    sin = np.where(is_x[:, None], sin, 0.0).astype(np.float32)
    cc = np.concatenate([cos.T, cos.T], axis=0)
    ss = np.concatenate([-sin.T, sin.T], axis=0)
    return np.ascontiguousarray(np.stack([cc, ss], axis=1)).astype(np.float32)


def _prep_shared(inp):
    sh = {}
    sh["wada"] = _wtiles(inp["w_ada"][0])
    sh["bada"] = _cols(inp["b_ada"][0])
    sh["gvec"] = np.ascontiguousarray(np.stack([_cols(inp[k][0]) for k in
                                                ("g_pre_mix", "g_post_mix", "g_pre_ffn", "g_post_ffn")], axis=1))
    w_in = inp["w_in"][0]
    kr = w_in[:, 2560:2624]
    ev, od = kr[:, 0::2], kr[:, 1::2]
    sh["win"] = _wtiles(np.concatenate([w_in[:, :2560], ev, od, od, ev], axis=1))
    sh["gq"] = _cols(inp["mla_g_q"][0])
    sh["gkv"] = _cols(inp["mla_g_kv"][0])
    wq = inp["mla_w_uq"][0].reshape(1024, NH, 192)
    nope, rp = wq[:, :, :128], wq[:, :, 128:]
    ev, od = rp[:, :, 0::2], rp[:, :, 1::2]
    wqp = np.concatenate([nope, ev, od, od, ev], axis=2)
    sh["wuq"] = np.ascontiguousarray(wqp.reshape(8, 128, NH, 256).transpose(2, 1, 0, 3)).reshape(NH, 128, 8 * 256)
    wkv = inp["mla_w_ukv"][0].reshape(512, NH, 256)
    wk = wkv[:, :, :128].reshape(4, 128, NH * 128)
    wv = wkv[:, :, 128:].reshape(4, 128, NH * 128)
    sh["wuk"] = np.ascontiguousarray(wk.transpose(1, 0, 2)).reshape(128, 4 * 3072)
    sh["wuv"] = np.ascontiguousarray(wv.transpose(1, 0, 2)).reshape(128, 4 * 3072)
    sh["wout"] = _wtiles(inp["w_out"][0])
    sh["wglu"] = _wtiles(inp["s5_w_glu"][0])
    fw = inp["ffn_w_in"][0]
    sh["wffg"] = _wtiles(fw[:, :DFF])
    sh["wffu"] = _wtiles(fw[:, DFF:])
    sh["wdn"] = _wtiles(inp["ffn_w_down"][0])
    sh["ident"] = np.eye(128, dtype=np.float32)
    return sh


def _prep_core(inp, sh, b, half):
    m = dict(sh)
    x = inp["x"][b]
    ctx = inp["ctx"][b]
    if half == 1:
        x = x[::-1]
        ctx = ctx[::-1]
    m["xl"] = np.ascontiguousarray(x)
    m["ctxl"] = np.ascontiguousarray(ctx)
    m["ccol"] = np.ascontiguousarray(np.stack([_cols(inp["c"][b]), _cols(inp["c_ctx"])], axis=-1))
    cw = inp["ffn_conv_w"][0]
    if half == 1:
        cw = cw[::-1]
    m["convw"] = np.ascontiguousarray(np.stack([_cols(cw[0]), _cols(cw[1]), _cols(cw[2]),
                                                _cols(inp["ffn_conv_b"][0])], axis=-1))
    loc = np.arange(SEQ)
    tpos = loc if half == 0 else (SEQ - 1 - loc)
    kp = np.concatenate([tpos, np.zeros(CTX, dtype=tpos.dtype)])
    isx = np.concatenate([np.ones(SEQ, bool), np.zeros(CTX, bool)])
    m["ropek"] = _rope_tables(kp, isx)
    m["ropeq"] = _rope_tables(tpos[:NOWN], np.ones(NOWN, bool))
    dsel = [0, 1] if half == 0 else [1, 0]

    def gp(a):
        a = a[dsel].reshape(2, 32, 2, 64)
        return a.transpose(2, 3, 0, 1).reshape(128, 64)

    m["s5lam"] = np.ascontiguousarray(np.stack([gp(inp["s5_lambda_re"][0]), gp(inp["s5_lambda_im"][0])], axis=1))
    ls = np.broadcast_to(inp["s5_log_step"][0][:, :, None], (2, 64, 64))
    m["s5ls"] = np.ascontiguousarray(gp(ls))

    def gb(ar, ai):
        a = np.stack([ar, ai], axis=-2)[dsel]
        a = a.reshape(2, 32, 2, 64, 2, 16)
        return np.ascontiguousarray(a.transpose(2, 3, 0, 1, 4, 5)).reshape(128, 64, 2, 16)

    m["s5b"] = gb(inp["s5_b_re"][0], inp["s5_b_im"][0])
    m["s5c"] = gb(inp["s5_c_re"][0].transpose(0, 1, 3, 2), inp["s5_c_im"][0].transpose(0, 1, 3, 2))
    m["s5d"] = np.ascontiguousarray(inp["s5_d"][0].reshape(32, 2, 16).transpose(1, 2, 0)).reshape(32, 32)
    return {k: np.ascontiguousarray(v, dtype=np.float32) for k, v in m.items()}


def kernel(**inputs):
    inp = {k: np.asarray(v) for k, v in inputs.items()}
    sh = _prep_shared(inp)
    in_maps = [_prep_core(inp, sh, c // 2, c % 2) for c in range(8)]
    nc = _build()
    in_maps = [{k: v for k, v in m.items() if k in nc._declared_inputs} for m in in_maps]
    res = run_bass_kernel_spmd(nc, in_maps, core_ids=list(range(8)))
    outp = np.empty((4, SEQ, D), dtype=np.float32)
    for c in range(8):
        o = res.results[c]["out"]
        b, half = c // 2, c % 2
        if half == 0:
            outp[b, :2048] = o
        else:
            outp[b, 2048:] = o[::-1]
    return outp
```
